# Optimizing a Trainium2 kernel written in Bass

```python
import math
import jax, jax.numpy as jnp
from jax import lax
import numpy as np

D_MODEL = 1024
BATCH = 4
SEQ = 8192
DEPTH = 2

D_MIX = D_MODEL
D_LRU = 3 * D_MODEL // 8
D_RET = 3 * D_MODEL // 8
D_SSM = D_MIX - D_LRU - D_RET
LRU_HEADS = 6
LRU_BLOCK = D_LRU // LRU_HEADS
CONV_WIDTH = 4
LRU_C = 8.0
RET_HEADS = 6
RET_HEAD_DIM = D_RET // RET_HEADS
RET_CHUNK = 128
ROPE_BASE = 10000.0
SSM_GROUP = 16
SSM_GROUPS = D_SSM // SSM_GROUP
SSM_STATE = 64
D_FF = ((8 * D_MODEL // 3 + 255) // 256) * 256
N_IN = 2 * D_LRU + 4 * D_RET + D_SSM
SPLITS = (D_LRU, 2 * D_LRU, 2 * D_LRU + D_RET, 2 * D_LRU + 2 * D_RET,
          2 * D_LRU + 3 * D_RET, 2 * D_LRU + 4 * D_RET)
N_MOD = 9
DEEPNORM_ALPHA = (2.0 * DEPTH) ** 0.25
DEEPNORM_BETA = (8.0 * DEPTH) ** -0.25
MACARON_HALF = 0.5
LN_EPS = 1e-5

kernel_name = "hymba_style_lru_retention_s5_macaron_deepnorm"

F32 = jnp.float32


def layer_norm(x, g, b):
    xf = x.astype(F32)
    mu = jnp.mean(xf, -1, keepdims=True)
    var = jnp.mean(jnp.square(xf - mu), -1, keepdims=True)
    return ((xf - mu) * lax.rsqrt(var + LN_EPS) * g.astype(F32) + b.astype(F32)).astype(x.dtype)


def modulate(x, shift, scale):
    return x * (1.0 + scale[:, None, :]) + shift[:, None, :]


def swiglu(h, w1, w3, w2):
    return (jax.nn.silu(h @ w1) * (h @ w3)) @ w2


def causal_conv(u, w, b):
    S = u.shape[1]
    up = jnp.pad(u, ((0, 0), (CONV_WIDTH - 1, 0), (0, 0)))
    out = b
    for k in range(CONV_WIDTH):
        out = out + up[:, k:k + S, :] * w[k]
    return out


def _linear_combine(e1, e2):
    a1, b1 = e1
    a2, b2 = e2
    return a1 * a2, a2 * b1 + b2


def rglru(u, w_a, b_a, w_x, b_x, lam):
    Bsz, S, _ = u.shape
    uh = u.reshape(Bsz, S, LRU_HEADS, LRU_BLOCK)
    r = jax.nn.sigmoid(jnp.einsum('bshi,hij->bshj', uh, w_a).reshape(Bsz, S, D_LRU) + b_a).astype(F32)
    i = jax.nn.sigmoid(jnp.einsum('bshi,hij->bshj', uh, w_x).reshape(Bsz, S, D_LRU) + b_x).astype(F32)
    log_a = -LRU_C * r * jax.nn.softplus(-lam.astype(F32))
    a = jnp.exp(log_a)
    bterm = jnp.sqrt(-jnp.expm1(2.0 * log_a)) * (i * u.astype(F32))
    _, h = lax.associative_scan(_linear_combine, (a, bterm), axis=1)
    return h.astype(u.dtype)


def rotary(t, pos):
    half = RET_HEAD_DIM // 2
    inv = ROPE_BASE ** (-jnp.arange(half, dtype=F32) / half)
    ang = pos.astype(F32)[..., None] * inv
    cos = jnp.cos(ang)[:, :, None, :]
    sin = jnp.sin(ang)[:, :, None, :]
    tf = t.astype(F32)
    t1, t2 = tf[..., :half], tf[..., half:]
    return jnp.concatenate([t1 * cos - t2 * sin, t2 * cos + t1 * sin], -1)


def retention(q, k, v, g, pos, gn_g, gn_b):
    Bsz, S, _ = q.shape
    H, Dh, C = RET_HEADS, RET_HEAD_DIM, RET_CHUNK
    NC = S // C
    qr = rotary(q.reshape(Bsz, S, H, Dh), pos)
    kr = rotary(k.reshape(Bsz, S, H, Dh), pos) * (Dh ** -0.5)
    vr = v.astype(F32).reshape(Bsz, S, H, Dh)

    def chunks(t):
        return t.reshape(Bsz, NC, C, H, Dh).transpose(0, 3, 1, 2, 4)

    qc, kc, vc = chunks(qr), chunks(kr), chunks(vr)
    log_gamma = jnp.log1p(-jnp.exp2(-5.0 - jnp.arange(H, dtype=F32)))
    idx = jnp.arange(C, dtype=F32)
    diff = idx[:, None] - idx[None, :]
    causal = diff >= 0
    decay_mask = jnp.where(causal, jnp.exp(log_gamma[:, None, None] * jnp.where(causal, diff, 0.0)), 0.0)
    scores = jnp.einsum('bhnqd,bhnkd->bhnqk', qc, kc) * decay_mask[:, None]
    o_inner = jnp.einsum('bhnqk,bhnkd->bhnqd', scores, vc)
    k_decay = jnp.exp(log_gamma[:, None] * (C - 1 - idx))
    kv = jnp.einsum('bhnkd,bhnke->bhnde', kc * k_decay[:, None, :, None], vc)
    chunk_decay = jnp.exp(log_gamma * C)[None, :, None, None]

    def step(state, kv_n):
        return chunk_decay * state + kv_n, state

    init = jnp.zeros((Bsz, H, Dh, Dh), F32)
    _, prev = lax.scan(step, init, kv.transpose(2, 0, 1, 3, 4))
    prev = prev.transpose(1, 2, 0, 3, 4)
    q_decay = jnp.exp(log_gamma[:, None] * (idx + 1.0))
    o_cross = jnp.einsum('bhnqd,bhnde->bhnqe', qc * q_decay[:, None, :, None], prev)
    o = o_inner + o_cross
    mu = jnp.mean(o, -1, keepdims=True)
    var = jnp.mean(jnp.square(o - mu), -1, keepdims=True)
    o = (o - mu) * lax.rsqrt(var + LN_EPS)
    o = o.transpose(0, 2, 3, 1, 4).reshape(Bsz, S, D_RET) * gn_g.astype(F32) + gn_b.astype(F32)
    return (jax.nn.silu(g.astype(F32)) * o).astype(q.dtype)


def _complex_combine(e1, e2):
    ar1, ai1, br1, bi1 = e1
    ar2, ai2, br2, bi2 = e2
    return (ar2 * ar1 - ai2 * ai1, ar2 * ai1 + ai2 * ar1,
            ar2 * br1 - ai2 * bi1 + br2, ar2 * bi1 + ai2 * br1 + bi2)


def s5(u, lam_re, lam_im, log_step, b_re, b_im, c_re, c_im, d_skip, w_glu, b_glu):
    Bsz, S, _ = u.shape
    uf = u.astype(F32).reshape(Bsz, S, SSM_GROUPS, SSM_GROUP)
    lr, li = lam_re.astype(F32), lam_im.astype(F32)
    dt = jnp.exp(log_step.astype(F32))[:, None]
    mag = jnp.exp(lr * dt)
    zr, zi = mag * jnp.cos(li * dt), mag * jnp.sin(li * dt)
    den = lr * lr + li * li
    er = ((zr - 1.0) * lr + zi * li) / den
    ei = (zi * lr - (zr - 1.0) * li) / den
    br, bi = b_re.astype(F32), b_im.astype(F32)
    bbar_re = er[..., None] * br - ei[..., None] * bi
    bbar_im = er[..., None] * bi + ei[..., None] * br
    bu_re = jnp.einsum('bsgh,gph->bsgp', uf, bbar_re)
    bu_im = jnp.einsum('bsgh,gph->bsgp', uf, bbar_im)
    a_re = jnp.broadcast_to(zr, (1, S, SSM_GROUPS, SSM_STATE))
    a_im = jnp.broadcast_to(zi, (1, S, SSM_GROUPS, SSM_STATE))
    _, _, xr, xi = lax.associative_scan(_complex_combine, (a_re, a_im, bu_re, bu_im), axis=1)
    y = (jnp.einsum('bsgp,ghp->bsgh', xr, c_re.astype(F32))
         - jnp.einsum('bsgp,ghp->bsgh', xi, c_im.astype(F32)))
    y = y.reshape(Bsz, S, D_SSM) + d_skip.astype(F32) * u.astype(F32)
    y = jax.nn.gelu(y)
    y = y * jax.nn.sigmoid(y @ w_glu.astype(F32) + b_glu.astype(F32))
    return y.astype(u.dtype)


def token_mixer(h, pos, w_in, conv_w, conv_b, lru_wa, lru_ba, lru_wx, lru_bx, lru_lam,
                ret_gn_g, ret_gn_b, ssm_lam_re, ssm_lam_im, ssm_log_step, ssm_b_re, ssm_b_im,
                ssm_c_re, ssm_c_im, ssm_d, ssm_w_glu, ssm_b_glu, w_out):
    z = h @ w_in
    u_lru, g_lru, q, k, v, g_ret, u_ssm = jnp.split(z, SPLITS, axis=-1)
    y_lru = rglru(causal_conv(u_lru, conv_w, conv_b), lru_wa, lru_ba, lru_wx, lru_bx, lru_lam) * jax.nn.gelu(g_lru)
    y_ret = retention(q, k, v, g_ret, pos, ret_gn_g, ret_gn_b)
    y_ssm = s5(u_ssm, ssm_lam_re, ssm_lam_im, ssm_log_step, ssm_b_re, ssm_b_im,
               ssm_c_re, ssm_c_im, ssm_d, ssm_w_glu, ssm_b_glu)
    return jnp.concatenate([y_lru, y_ret, y_ssm], axis=-1) @ w_out


def _normal(k, shape, scale):
    return jax.random.normal(k, shape, F32) * scale


def setup_inputs(seed: int = 0) -> dict:
    key = jax.random.key(seed)
    ks = iter(jax.random.split(key, 48))
    L, D, F = DEPTH, D_MODEL, D_FF
    G, P, Hs = SSM_GROUPS, SSM_STATE, SSM_GROUP
    x = _normal(next(ks), (BATCH, SEQ, D), 1.0)
    c = _normal(next(ks), (BATCH, D), 1.0)
    positions = jnp.broadcast_to(jnp.arange(SEQ, dtype=jnp.int32)[None, :], (BATCH, SEQ))
    u_lam = jax.random.uniform(next(ks), (L, D_LRU), F32, 0.9, 0.999)
    a0 = u_lam ** (1.0 / LRU_C)
    lru_lam = jnp.log(a0) - jnp.log1p(-a0)
    return {
        "x": x, "c": c, "positions": positions,
        "ada_w": _normal(next(ks), (L, D, N_MOD * D), 0.5 * D ** -0.5),
        "ada_b": _normal(next(ks), (L, N_MOD * D), 0.01),
        "ln_g": 1.0 + _normal(next(ks), (L, 3, D), 0.02),
        "ln_b": _normal(next(ks), (L, 3, D), 0.01),
        "ffn1_w1": _normal(next(ks), (L, D, F), D ** -0.5),
        "ffn1_w3": _normal(next(ks), (L, D, F), D ** -0.5),
        "ffn1_w2": _normal(next(ks), (L, F, D), DEEPNORM_BETA * F ** -0.5),
        "mix_w_in": _normal(next(ks), (L, D, N_IN), D ** -0.5),
        "conv_w": _normal(next(ks), (L, CONV_WIDTH, D_LRU), CONV_WIDTH ** -0.5),
        "conv_b": _normal(next(ks), (L, D_LRU), 0.01),
        "lru_wa": _normal(next(ks), (L, LRU_HEADS, LRU_BLOCK, LRU_BLOCK), LRU_BLOCK ** -0.5),
        "lru_ba": _normal(next(ks), (L, D_LRU), 0.01),
        "lru_wx": _normal(next(ks), (L, LRU_HEADS, LRU_BLOCK, LRU_BLOCK), LRU_BLOCK ** -0.5),
        "lru_bx": _normal(next(ks), (L, D_LRU), 0.01),
        "lru_lam": lru_lam,
        "ret_gn_g": 1.0 + _normal(next(ks), (L, D_RET), 0.02),
        "ret_gn_b": _normal(next(ks), (L, D_RET), 0.01),
        "ssm_lam_re": -0.5 + _normal(next(ks), (L, G, P), 0.005),
        "ssm_lam_im": math.pi * jnp.broadcast_to(jnp.arange(P, dtype=F32), (L, G, P)) + _normal(next(ks), (L, G, P), 0.005),
        "ssm_log_step": jax.random.uniform(next(ks), (L, G), F32, math.log(0.001), math.log(0.1)),
        "ssm_b_re": _normal(next(ks), (L, G, P, Hs), (2.0 * Hs) ** -0.5),
        "ssm_b_im": _normal(next(ks), (L, G, P, Hs), (2.0 * Hs) ** -0.5),
        "ssm_c_re": _normal(next(ks), (L, G, Hs, P), (2.0 * P) ** -0.5),
        "ssm_c_im": _normal(next(ks), (L, G, Hs, P), (2.0 * P) ** -0.5),
        "ssm_d": _normal(next(ks), (L, D_SSM), 1.0),
        "ssm_w_glu": _normal(next(ks), (L, D_SSM, D_SSM), D_SSM ** -0.5),
        "ssm_b_glu": _normal(next(ks), (L, D_SSM), 0.01),
        "mix_w_out": _normal(next(ks), (L, D_MIX, D), DEEPNORM_BETA * D_MIX ** -0.5),
        "ffn2_w1": _normal(next(ks), (L, D, F), D ** -0.5),
        "ffn2_w3": _normal(next(ks), (L, D, F), D ** -0.5),
        "ffn2_w2": _normal(next(ks), (L, F, D), DEEPNORM_BETA * F ** -0.5),
    }


def reference(x, c, positions, ada_w, ada_b, ln_g, ln_b, ffn1_w1, ffn1_w3, ffn1_w2,
              mix_w_in, conv_w, conv_b, lru_wa, lru_ba, lru_wx, lru_bx, lru_lam,
              ret_gn_g, ret_gn_b, ssm_lam_re, ssm_lam_im, ssm_log_step, ssm_b_re, ssm_b_im,
              ssm_c_re, ssm_c_im, ssm_d, ssm_w_glu, ssm_b_glu, mix_w_out,
              ffn2_w1, ffn2_w3, ffn2_w2):
    cond = jax.nn.silu(c)
    for l in range(DEPTH):
        mod = cond @ ada_w[l] + ada_b[l]
        sh1, sc1, gt1, sh2, sc2, gt2, sh3, sc3, gt3 = jnp.split(mod, N_MOD, axis=-1)
        f1 = swiglu(modulate(x, sh1, sc1), ffn1_w1[l], ffn1_w3[l], ffn1_w2[l])
        x = layer_norm(DEEPNORM_ALPHA * x + MACARON_HALF * gt1[:, None, :] * f1, ln_g[l, 0], ln_b[l, 0])
        m = token_mixer(modulate(x, sh2, sc2), positions, mix_w_in[l], conv_w[l], conv_b[l],
                        lru_wa[l], lru_ba[l], lru_wx[l], lru_bx[l], lru_lam[l],
                        ret_gn_g[l], ret_gn_b[l], ssm_lam_re[l], ssm_lam_im[l], ssm_log_step[l],
                        ssm_b_re[l], ssm_b_im[l], ssm_c_re[l], ssm_c_im[l], ssm_d[l],
                        ssm_w_glu[l], ssm_b_glu[l], mix_w_out[l])
        x = layer_norm(DEEPNORM_ALPHA * x + gt2[:, None, :] * m, ln_g[l, 1], ln_b[l, 1])
        f2 = swiglu(modulate(x, sh3, sc3), ffn2_w1[l], ffn2_w3[l], ffn2_w2[l])
        x = layer_norm(DEEPNORM_ALPHA * x + MACARON_HALF * gt3[:, None, :] * f2, ln_g[l, 2], ln_b[l, 2])
    return x
```

```python
from contextlib import ExitStack
import numpy as np
import concourse.bass as bass
import concourse.mybir as mybir
from concourse.bass_utils import run_bass_kernel_spmd

F32 = mybir.dt.float32
BF16 = mybir.dt.bfloat16
I32 = mybir.dt.int32
AF = mybir.ActivationFunctionType
ALU = mybir.AluOpType
AX = mybir.AxisListType

D = 1024
DFF = 2816
NF = DFF // 128
DEPTH = 2
NIN = 2560
ALPHA = (2.0 * DEPTH) ** 0.25
EPS_LN = 1e-5 / (ALPHA * ALPHA)
NTOK = 4096
MAGIC = 12582912.0
TWO_PI = 6.283185307179586
C1 = 6.28125
C2 = TWO_PI - C1


class Sched:
    ENGS = ("pe", "dve", "act", "pool", "sp")

    def __init__(self, nc):
        self.nc = nc
        self.ops = []
        self.last_w = {}
        self.readers = {}
        self.n_dma_sems = 24
        self.bar_start = 0

    def op(self, eng, fn, reads=(), writes=(), dma=False):
        deps = set()
        for k in reads:
            w = self.last_w.get(k)
            if w is not None:
                deps.add(w)
        for k in writes:
            w = self.last_w.get(k)
            if w is not None:
                deps.add(w)
            for r in self.readers.get(k, ()):
                deps.add(r)
        idx = len(self.ops)
        if not dma and eng == "pe":
            deps = {d for d in deps if self.ops[d]["dma"] or self.ops[d]["eng"] != "pe"}
        self.ops.append(dict(eng=eng, fn=fn, deps=deps, dma=dma, sig=False))
        for k in writes:
            self.last_w[k] = idx
            self.readers[k] = []
        for k in reads:
            if k not in writes:
                self.readers.setdefault(k, []).append(idx)
        return idx

    def dma(self, fn, reads=(), writes=(), q="sp", inc=16):
        i = self.op(q, fn, reads, writes, dma=True)
        self.ops[i]["inc"] = inc
        return i

    def barrier(self):
        last = {}
        for i, o in enumerate(self.ops):
            if i < self.bar_start:
                continue
            key = ("dma", i) if o["dma"] else o["eng"]
            last[key] = i
        deps = set(last.values())
        for e in self.ENGS:
            self.ops.append(dict(eng=e, fn=None, deps=set(deps), dma=False, sig=False))
        self.bar_start = len(self.ops)
        self.last_w = {}
        self.readers = {}

    def emit(self, es, final_wait_ops=()):
        nc = self.nc
        ops = self.ops
        for o in ops:
            for d in o["deps"]:
                ops[d]["sig"] = True
        for i in final_wait_ops:
            ops[i]["sig"] = True
        esem = {e: es.enter_context(nc.semaphore("s_" + e)) for e in self.ENGS}
        dsem = [es.enter_context(nc.semaphore("d%d" % i)) for i in range(self.n_dma_sems)]
        csem = es.enter_context(nc.semaphore("ccsem"))
        ccnt = 0
        cnt = {e: 0 for e in self.ENGS}
        dcnt = [0] * self.n_dma_sems
        dlast = [None] * self.n_dma_sems
        nd = 0
        for i, o in enumerate(ops):
            if o["dma"] and o.get("inc", 16) == 1:
                ccnt += 1
                o["ev"] = (csem, ccnt, ("c", 0))
            elif o["dma"]:
                k = nd % self.n_dma_sems
                nd += 1
                dcnt[k] += o.get("inc", 16)
                o["ev"] = (dsem[k], dcnt[k], ("d", k))
                if dlast[k] is not None:
                    o["deps"] = set(o["deps"]) | {dlast[k]}
                dlast[k] = i
            elif o["sig"]:
                cnt[o["eng"]] += 1
                o["ev"] = (esem[o["eng"]], cnt[o["eng"]], ("e", o["eng"]))
        streams = {e: [] for e in self.ENGS}
        for i, o in enumerate(ops):
            streams[o["eng"]].append(i)
        final = list(final_wait_ops)

        def run(e, eng):
            known = {}
            for i in streams[e]:
                o = ops[i]
                need = {}
                for d in o["deps"]:
                    sem, val, key = ops[d]["ev"]
                    if need.get(key, (None, 0))[1] < val:
                        need[key] = (sem, val)
                for key, (sem, val) in need.items():
                    if known.get(key, 0) < val:
                        eng.wait_ge(sem, val)
                        known[key] = val
                if o["fn"] is None:
                    continue
                ins = o["fn"](eng)
                if o["dma"]:
                    ins.then_inc(o["ev"][0], o.get("inc", 16))
                elif o["sig"]:
                    ins.then_inc(o["ev"][0], 1)
            if e == "sp":
                for i in final:
                    sem, val, key = ops[i]["ev"]
                    eng.wait_ge(sem, val)

        block = es.enter_context(nc.Block())

        @block.tensor
        def _(eng):
            run("pe", eng)

        @block.vector
        def _(eng):
            run("dve", eng)

        @block.scalar
        def _(eng):
            run("act", eng)

        @block.gpsimd
        def _(eng):
            run("pool", eng)

        @block.sync
        def _(eng):
            run("sp", eng)


class SbufAlloc:
    def __init__(self, nc, limit=229344):
        self.nc = nc
        self.off = 16512
        self.limit = limit
        self.n = 0

    def mark(self):
        return self.off

    def reset(self, m):
        self.off = m

    def alloc(self, shape, dtype, name=None):
        nbytes = int(np.prod(shape[1:])) * (2 if dtype == BF16 else 4)
        nbytes = (nbytes + 63) // 64 * 64
        assert self.off + nbytes <= self.limit, ("SBUF overflow", name, self.off, nbytes)
        self.n += 1
        t = self.nc.alloc_sbuf_tensor_at("%s_%d" % (name or "t", self.n), list(shape), dtype, offset=self.off)
        self.off += nbytes
        return t


class Ctx:
    pass


def build_program(ntok=NTOK, depth=DEPTH, phases=("ffn1", "mix", "ffn2"), debug_out=False):
    nc = bass.Bass("TRN2", target_bir_lowering=False)
    g = Ctx()
    g.nc = nc
    g.ntok = ntok
    s = Sched(nc)
    g.s = s
    sb = SbufAlloc(nc)
    g.sb = sb

    def din(name, shape, dt=F32):
        return nc.dram_tensor(name, list(shape), dt, kind="ExternalInput").ap()

    g.xT = din("xT", [D, ntok])
    g.outT = nc.dram_tensor("outT", [D, ntok], F32, kind="ExternalOutput").ap()
    g.cvec = din("cvec", [128, 8])
    g.pos = din("pos", [1, ntok], I32)
    g.ada_w = din("ada_w", [DEPTH, D, 9 * D])
    g.ada_b = din("ada_b", [128, DEPTH * 72])
    g.ln_g = din("ln_g", [128, DEPTH * 3 * 8])
    g.ln_b = din("ln_b", [128, DEPTH * 3 * 8])
    g.w1 = [din("ffn1_w1", [DEPTH, D, DFF]), din("ffn2_w1", [DEPTH, D, DFF])]
    g.w3 = [din("ffn1_w3", [DEPTH, D, DFF]), din("ffn2_w3", [DEPTH, D, DFF])]
    g.w2 = [din("ffn1_w2", [DEPTH, DFF, D]), din("ffn2_w2", [DEPTH, DFF, D])]
    g.scr = [nc.dram_tensor("scr%d" % i, [D, ntok], F32).ap() for i in range(2)]
    g.mix_w_in = din("mix_w_in", [DEPTH, D, NIN])
    g.w_sw = din("w_sw", [DEPTH, D, 768])
    g.mix_w_out = din("mix_w_out", [DEPTH, D, D])
    g.w_glu = din("ssm_w_glu", [DEPTH, 256, 256])
    g.pp = din("pp", [DEPTH, 128, 52])
    g.lruw = din("lruw", [DEPTH, 2, 6, 64, 64])
    g.gn = din("gn", [DEPTH, 2, 384])
    g.srow = din("srow", [DEPTH, 256, 5, 64])
    g.cst = din("cst", [DEPTH, 128, 8, 2, 16])
    g.c_small = din("c_small", [128, 8])
    g.c_iota = din("c_iota", [128, 257])
    g.c_maskT = din("c_maskT", [128, 6, 128])
    g.c_qdec = din("c_qdec", [128, 3, 128])
    g.c_kdt = din("c_kdt", [128, 384])
    g.c_gmask = din("c_gmask", [128, 8])
    g.c_ident = din("c_ident", [128, 128])
    g.onehot = din("onehot", [128, 8])
    g.selprev = din("selprev", [128, 8])
    g.cc_in = [nc.dram_tensor("cc_in%d" % i, [128, 8 * NST], F32) for i in range(DEPTH)]
    g.cc_out = [nc.dram_tensor("cc_out%d" % i, [128, 8 * NST], F32) for i in range(DEPTH)]

    g.ones = sb.alloc([128, 128], BF16, "ones")
    g.mod = sb.alloc([128, DEPTH * 72], F32, "mod")
    g.sc1p = sb.alloc([128, DEPTH * 72], F32, "sc1p")
    g.lng = sb.alloc([128, DEPTH * 24], F32, "lng")
    g.lnb = sb.alloc([128, DEPTH * 24], F32, "lnb")
    g.cond = sb.alloc([128, 8], F32, "cond")
    g.adab = sb.alloc([128, DEPTH * 72], F32, "adab")
    g.psum = None
    base_mark = sb.mark()

    s.op("pool", lambda e: e.memset(g.ones[:], 1.0 / 1024.0), writes=["ones"])
    s.dma(lambda e: e.dma_start(out=g.cond[:], in_=g.cvec), writes=["cond"])
    s.dma(lambda e: e.dma_start(out=g.adab[:], in_=g.ada_b), writes=["adab"])
    s.dma(lambda e: e.dma_start(out=g.lng[:], in_=g.ln_g), writes=["lng"])
    s.dma(lambda e: e.dma_start(out=g.lnb[:], in_=g.ln_b), writes=["lnb"])
    s.op("act", lambda e: e.activation(out=g.cond[:], in_=g.cond[:], func=AF.Silu), reads=["cond"], writes=["cond"])

    es = ExitStack()
    g.ps = [es.enter_context(nc.psum_tensor("ps%d" % i, [128, 512], F32)) for i in range(8)]

    emit_mod(g, depth)
    s.barrier()

    src = g.xT
    nsub = 0
    for l in range(depth):
        for ph in phases:
            last = (l == depth - 1) and (ph == phases[-1])
            dst = g.outT if last else g.scr[nsub % 2]
            sb.reset(base_mark)
            if ph == "ffn1":
                emit_ffn(g, l, 0, src, dst)
            elif ph == "ffn2":
                emit_ffn(g, l, 1, src, dst)
            else:
                emit_mixer(g, l, src, dst)
            s.barrier()
            src = dst
            nsub += 1
    final = [i for i, o in enumerate(s.ops) if o["dma"] and o.get("is_out")]
    s.emit(es, final_wait_ops=final)
    es.close()
    return nc


def emit_mod(g, depth):
    s, sb, nc = g.s, g.sb, g.nc
    m = sb.mark()
    stg = [sb.alloc([128, 8, 512], F32, "adastg") for _ in range(3)]
    ps = g.ps[0]
    n = 0
    for l in range(depth):
        for piece in range(18):
            b = n % 3
            n += 1
            st = stg[b]
            s.dma(lambda e, st=st, l=l, piece=piece: e.dma_start(
                out=st[:], in_=g.ada_w[l, :, piece * 512:(piece + 1) * 512].rearrange("(k p) n -> p k n", p=128)),
                writes=["adastg%d" % b])
            for c in range(4):
                col = l * 72 + piece * 4 + c
                for k in range(8):
                    s.op("pe", lambda e, st=st, c=c, col=col, k=k: e.matmul(
                        ps[:, col:col + 1], lhsT=st[:, k, c * 128:(c + 1) * 128], rhs=g.cond[:, k:k + 1],
                        start=(k == 0), stop=(k == 7)),
                        reads=["adastg%d" % b, "cond"], writes=["modps"])
    W = depth * 72
    s.op("dve", lambda e: e.tensor_tensor(out=g.mod[:, 0:W], in0=ps[:, 0:W], in1=g.adab[:, 0:W], op=ALU.add),
         reads=["modps", "adab"], writes=["mod"])
    for l in range(depth):
        for n_ in range(9):
            c0 = l * 72 + n_ * 8
            if n_ in (1, 4, 7):
                s.op("dve", lambda e, c0=c0: e.tensor_scalar_add(out=g.sc1p[:, c0:c0 + 8], in0=g.mod[:, c0:c0 + 8], scalar1=1.0),
                     reads=["mod"], writes=["sc1p"])
            elif n_ in (2, 5, 8):
                coef = (0.5 if n_ in (2, 8) else 1.0) / ALPHA
                s.op("dve", lambda e, c0=c0, coef=coef: e.tensor_scalar_mul(out=g.sc1p[:, c0:c0 + 8], in0=g.mod[:, c0:c0 + 8], scalar1=coef),
                     reads=["mod"], writes=["sc1p"])
            else:
                s.op("dve", lambda e, c0=c0: e.tensor_copy(out=g.sc1p[:, c0:c0 + 8], in_=g.mod[:, c0:c0 + 8]),
                     reads=["mod"], writes=["sc1p"])
    sb.reset(m)


def load_cast(g, dram_ap, dst_ap, stage_tiles, stage_keys, n, dst_key, shape3=None):
    s = g.s
    b = n % len(stage_tiles)
    st = stage_tiles[b]
    view = st[:, 0:int(np.prod(dst_ap.shape[1:]))]
    if len(dst_ap.shape) == 3:
        view = view.rearrange("p (a b) -> p a b", a=dst_ap.shape[1])
    s.dma(lambda e: e.dma_start(out=view, in_=dram_ap), writes=[stage_keys[b]])
    eng = ("act", "pool", "dve")[n % 3]
    if eng == "act":
        s.op("act", lambda e: e.copy(out=dst_ap, in_=view), reads=[stage_keys[b]], writes=[dst_key])
    else:
        s.op(eng, lambda e: e.tensor_copy(out=dst_ap, in_=view), reads=[stage_keys[b]], writes=[dst_key])


def emit_resid_ln(g, l, isub, xin, xkey, acc_fn, T, tmp):
    s = g.s
    gcol = l * 72 + (isub * 3 + 2) * 8
    lcol = l * 24 + isub * 8
    ybf, ysq, mean_sb, m2, var_sb = tmp["ybf"], tmp["ysq"], tmp["mean"], tmp["m2"], tmp["var"]
    for j in range(8):
        acc, akey = acc_fn(j)
        s.op("dve", lambda e, j=j, acc=acc: e.scalar_tensor_tensor(
            out=xin[:, j, :], in0=acc, scalar=g.sc1p[:, gcol + j:gcol + j + 1], in1=xin[:, j, :],
            op0=ALU.mult, op1=ALU.add), reads=[akey, xkey + "%d" % j, "sc1p"], writes=[xkey + "%d" % j])
        s.op("pool", lambda e, j=j: e.tensor_copy(out=ybf[:, j, :], in_=xin[:, j, :]),
             reads=[xkey + "%d" % j], writes=["ybf%d" % j])
        s.op("act", lambda e, j=j: e.activation(out=ysq[:, j, :], in_=xin[:, j, :], func=AF.Square),
             reads=[xkey + "%d" % j], writes=["ysq%d" % j])
    import os
    DBG = int(os.environ.get("KDBG", "9"))
    if DBG <= 4:
        return
    pm, pe2 = g.ps[6], g.ps[7]
    for j in range(8):
        s.op("pe", lambda e, j=j: e.matmul(pm[:, 0:T], lhsT=g.ones[:], rhs=ybf[:, j, :], start=(j == 0), stop=(j == 7)),
             reads=["ybf%d" % j, "ones"], writes=["ps6"])
    for j in range(8):
        s.op("pe", lambda e, j=j: e.matmul(pe2[:, 0:T], lhsT=g.ones[:], rhs=ysq[:, j, :], start=(j == 0), stop=(j == 7)),
             reads=["ysq%d" % j, "ones"], writes=["ps7"])
    if DBG <= 5:
        return
    s.op("act", lambda e: e.copy(out=mean_sb[:], in_=pm[:, 0:T]), reads=["ps6"], writes=["mean"])
    s.op("dve", lambda e: e.tensor_tensor(out=m2[:], in0=mean_sb[:], in1=mean_sb[:], op=ALU.mult), reads=["mean", "var"], writes=["m2", "var"])
    s.op("dve", lambda e: e.tensor_tensor(out=var_sb[:], in0=pe2[:, 0:T], in1=m2[:], op=ALU.subtract), reads=["ps7", "m2", "var"], writes=["var", "m2"])
    s.op("dve", lambda e: e.tensor_scalar(out=var_sb[:], in0=var_sb[:], scalar1=0.0, scalar2=EPS_LN, op0=ALU.max, op1=ALU.add),
         reads=["var"], writes=["var"])
    s.op("act", lambda e: e.activation(out=var_sb[:], in_=var_sb[:], func=AF.Sqrt), reads=["var"], writes=["var"])
    s.op("dve", lambda e: e.reciprocal(out=var_sb[:], in_=var_sb[:]), reads=["var"], writes=["var"])
    if DBG <= 6:
        return
    for j in range(8):
        k = xkey + "%d" % j
        s.op("dve", lambda e, j=j: e.tensor_tensor(out=xin[:, j, :], in0=xin[:, j, :], in1=mean_sb[:], op=ALU.subtract),
             reads=[k, "mean"], writes=[k])
        s.op("dve", lambda e, j=j: e.tensor_tensor(out=xin[:, j, :], in0=xin[:, j, :], in1=var_sb[:], op=ALU.mult),
             reads=[k, "var"], writes=[k])
        s.op("dve", lambda e, j=j: e.tensor_scalar(out=xin[:, j, :], in0=xin[:, j, :],
                                                    scalar1=g.lng[:, lcol + j:lcol + j + 1], scalar2=g.lnb[:, lcol + j:lcol + j + 1],
                                                    op0=ALU.mult, op1=ALU.add),
             reads=[k, "lng", "lnb"], writes=[k])


def dram_tile(ap, t0, T):
    return ap[:, t0:t0 + T].rearrange("(k p) t -> p k t", p=128)


def emit_ffn(g, l, which, src, dst):
    s, sb, nc = g.s, g.sb, g.nc
    T = 512
    isub = 0 if which == 0 else 2
    w1b = sb.alloc([128, 8, DFF], BF16, "w1b")
    w3b = sb.alloc([128, 8, DFF], BF16, "w3b")
    w2b = sb.alloc([128, NF, D], BF16, "w2b")
    xin = [sb.alloc([128, 8, T], F32, "xin") for _ in range(1)]
    h = sb.alloc([128, 8, T], BF16, "h")
    sil = [sb.alloc([128, T], BF16, "sil") for _ in range(2)]
    tmp = dict(mean=sb.alloc([128, T], F32, "mean"), m2=sb.alloc([128, T], F32, "m2"), var=sb.alloc([128, T], F32, "var"))
    mk = sb.mark()
    stg = [sb.alloc([128, DFF], F32, "stg") for _ in range(2)]
    skeys = ["stg0", "stg1"]
    n = 0
    for k in range(8):
        load_cast(g, g.w1[which][l, k * 128:(k + 1) * 128, :], w1b[:, k, :], stg, skeys, n, "w1b"); n += 1
        load_cast(g, g.w3[which][l, k * 128:(k + 1) * 128, :], w3b[:, k, :], stg, skeys, n, "w3b"); n += 1
    for c in range(0, NF, 2):
        load_cast(g, g.w2[which][l, c * 128:(c + 2) * 128, :].rearrange("(c p) n -> p c n", p=128),
                  w2b[:, c:c + 2, :], stg, skeys, n, "w2b"); n += 1
    s.barrier()
    import os
    DBG = int(os.environ.get("KDBG", "9"))
    if DBG <= 1:
        return
    sb.reset(mk)
    gT = sb.alloc([128, NF, T], BF16, "gT")
    tmp["ybf"] = sb.alloc([128, 8, T], BF16, "ybf")
    tmp["ysq"] = sb.alloc([128, 8, T], BF16, "ysq")
    shc = l * 72 + (isub * 3 + 0) * 8
    scc = l * 72 + (isub * 3 + 1) * 8
    ntile = g.ntok // T
    for it in range(ntile):
        t0 = it * T
        xb = 0
        xt = xin[xb]
        xkey = "xin%d_" % xb
        s.dma(lambda e, xt=xt, t0=t0: e.dma_start(out=xt[:], in_=dram_tile(src, t0, T)),
              writes=[xkey + "%d" % j for j in range(8)])
        for k in range(8):
            s.op("dve", lambda e, k=k, xt=xt: e.tensor_scalar(
                out=h[:, k, :], in0=xt[:, k, :], scalar1=g.sc1p[:, scc + k:scc + k + 1], scalar2=g.sc1p[:, shc + k:shc + k + 1],
                op0=ALU.mult, op1=ALU.add), reads=[xkey + "%d" % k, "sc1p"], writes=["h%d" % k])
        if DBG <= 2:
            continue
        for f in range(NF):
            pb = f % 2
            p1, p3 = g.ps[2 * pb], g.ps[2 * pb + 1]
            for k in range(8):
                s.op("pe", lambda e, k=k, f=f, p1=p1: e.matmul(p1[:, 0:T], lhsT=w1b[:, k, f * 128:(f + 1) * 128], rhs=h[:, k, :],
                                                              start=(k == 0), stop=(k == 7)),
                     reads=["h%d" % k, "w1b"], writes=["ps%d" % (2 * pb)])
            for k in range(8):
                s.op("pe", lambda e, k=k, f=f, p3=p3: e.matmul(p3[:, 0:T], lhsT=w3b[:, k, f * 128:(f + 1) * 128], rhs=h[:, k, :],
                                                              start=(k == 0), stop=(k == 7)),
                     reads=["h%d" % k, "w3b"], writes=["ps%d" % (2 * pb + 1)])
            sl = sil[pb]
            s.op("act", lambda e, p1=p1, sl=sl: e.activation(out=sl[:], in_=p1[:, 0:T], func=AF.Silu),
                 reads=["ps%d" % (2 * pb)], writes=["sil%d" % pb])
            s.op("dve", lambda e, f=f, p3=p3, sl=sl: e.tensor_tensor(out=gT[:, f, :], in0=p3[:, 0:T], in1=sl[:], op=ALU.mult),
                 reads=["ps%d" % (2 * pb + 1), "sil%d" % pb], writes=["gT%d" % f])

        def acc_fn(j):
            pa = g.ps[4 + j % 2]
            key = "ps%d" % (4 + j % 2)
            for f in range(NF):
                s.op("pe", lambda e, f=f, j=j, pa=pa: e.matmul(pa[:, 0:T], lhsT=w2b[:, f, j * 128:(j + 1) * 128], rhs=gT[:, f, :],
                                                              start=(f == 0), stop=(f == NF - 1)),
                     reads=["gT%d" % f, "w2b"], writes=[key])
            return pa[:, 0:T], key

        if DBG <= 3:
            continue
        emit_resid_ln(g, l, isub, xt, xkey, acc_fn, T, tmp)
        i = s.dma(lambda e, xt=xt, t0=t0: e.dma_start(out=dram_tile(dst, t0, T), in_=xt[:]),
                  reads=[xkey + "%d" % j for j in range(8)])
        s.ops[i]["is_out"] = dst is g.outT


NST = 3 + 9 + 192 + 16


def bc_inner(ap2, n):
    return ap2.unsqueeze(2).to_broadcast([ap2.shape[0], ap2.shape[1], n])


def emit_sincos(g, ang, sin_out, cos_out, tmpa, tmpb, keys, sin_scale=None, eng="dve"):
    s = g.s
    ka, ks, kc, kt1, kt2 = keys
    for (shift, outp, okey, scale) in ((0.0, sin_out, ks, sin_scale), (0.25, cos_out, kc, None)):
        s.op(eng, lambda e, shift=shift: e.tensor_scalar(out=tmpa, in0=ang, scalar1=1.0 / TWO_PI, scalar2=shift,
                                                         op0=ALU.mult, op1=ALU.add), reads=[ka], writes=[kt1])
        s.op(eng, lambda e: e.tensor_scalar(out=tmpa, in0=tmpa, scalar1=MAGIC, scalar2=MAGIC, op0=ALU.add, op1=ALU.subtract),
             reads=[kt1], writes=[kt1])
        s.op(eng, lambda e: e.scalar_tensor_tensor(out=tmpb, in0=tmpa, scalar=-C1, in1=ang, op0=ALU.mult, op1=ALU.add),
             reads=[kt1, ka], writes=[kt2])
        s.op(eng, lambda e: e.scalar_tensor_tensor(out=tmpb, in0=tmpa, scalar=-C2, in1=tmpb, op0=ALU.mult, op1=ALU.add),
             reads=[kt1, kt2], writes=[kt2])
        s.op(eng, lambda e, shift=shift: e.tensor_scalar(out=tmpb, in0=tmpb, scalar1=shift * TWO_PI, scalar2=-3.1415925,
                                                         op0=ALU.add, op1=ALU.max), reads=[kt2], writes=[kt2])
        s.op(eng, lambda e: e.tensor_scalar_min(out=tmpb, in0=tmpb, scalar1=3.1415925), reads=[kt2], writes=[kt2])
        if scale is None:
            s.op("act", lambda e, outp=outp: e.activation(out=outp, in_=tmpb, func=AF.Sin), reads=[kt2], writes=[okey])
        else:
            s.op("act", lambda e, outp=outp, scale=scale: e.activation(out=outp, in_=tmpb, func=AF.Sin, scale=scale),
                 reads=[kt2], writes=[okey])


def emit_mixer(g, l, src, dst):
    s, sb, nc = g.s, g.sb, g.nc
    T = 256
    NCH = T // 128
    ntile = g.ntok // T
    A = sb.alloc
    winb = A([128, 8, NIN], BF16, "winb")
    wswb = A([128, 8, 768], BF16, "wswb")
    woutb = A([128, 8, D], BF16, "woutb")
    pp = A([128, 52], F32, "pp")
    cs = A([128, 8], F32, "cs")
    iota = A([128, 257], F32, "iota")
    maskT = A([128, 6, 128], F32, "maskT")
    qdec = A([128, 3, 128], F32, "qdec")
    kdt = A([128, 384], F32, "kdt")
    gng = A([128, 384], F32, "gng")
    gnb = A([128, 384], F32, "gnb")
    identb = A([128, 128], BF16, "identb")
    gmask = A([128, 8], F32, "gmask")
    CT = A([128, 8, 257], F32, "CT")
    ST = A([128, 8, 257], F32, "ST")
    TBre = A([128, 8, 128], BF16, "TBre")
    TBim = A([128, 8, 128], BF16, "TBim")
    TCre = A([128, 8, 128], BF16, "TCre")
    TCim = A([128, 8, 128], BF16, "TCim")
    wglub = A([128, 2, 256], BF16, "wglub")
    wabd = A([128, 3, 128], BF16, "wabd")
    wxbd = A([128, 3, 128], BF16, "wxbd")
    sp_ = A([128, 64], F32, "sp")
    cneg = A([128, 6], F32, "cneg")
    stt = A([128, 3, 64], F32, "stt")
    stbf = A([128, 3, 64], BF16, "stbf")
    lstate = A([128, 3], F32, "lstate")
    uext = A([128, 3, T + 3], F32, "uext")
    carr = A([128, 16], F32, "carr")
    stpack = A([128, NST], F32, "stpack")
    oneh = A([128, 8], F32, "oneh")
    selp = A([128, 8], F32, "selp")
    mk = sb.mark()
    stg = [A([128, NIN], F32, "stg") for _ in range(2)]
    skeys = ["stg0", "stg1"]
    n = 0
    for k in range(8):
        load_cast(g, g.mix_w_in[l, k * 128:(k + 1) * 128, :], winb[:, k, :], stg, skeys, n, "winb"); n += 1
        load_cast(g, g.w_sw[l, k * 128:(k + 1) * 128, :], wswb[:, k, :], stg, skeys, n, "wswb"); n += 1
        load_cast(g, g.mix_w_out[l, k * 128:(k + 1) * 128, :], woutb[:, k, :], stg, skeys, n, "woutb"); n += 1
    load_cast(g, g.w_glu[l].rearrange("(c p) n -> p c n", p=128), wglub[:, :, :], stg, skeys, n, "wglub"); n += 1
    s.dma(lambda e: e.dma_start(out=pp[:], in_=g.pp[l]), writes=["pp"])
    s.dma(lambda e: e.dma_start(out=cs[:], in_=g.c_small), writes=["cs"])
    s.dma(lambda e: e.dma_start(out=iota[:], in_=g.c_iota), writes=["iota"])
    s.dma(lambda e: e.dma_start(out=maskT[:], in_=g.c_maskT), writes=["maskT"])
    s.dma(lambda e: e.dma_start(out=qdec[:], in_=g.c_qdec), writes=["qdec"])
    s.dma(lambda e: e.dma_start(out=kdt[:], in_=g.c_kdt), writes=["kdt"])
    s.dma(lambda e: e.dma_start(out=gmask[:], in_=g.c_gmask), writes=["gmask"])
    s.dma(lambda e: e.dma_start(out=gng[:], in_=g.gn[l, 0:1, :].partition_broadcast(128)), writes=["gng"])
    s.dma(lambda e: e.dma_start(out=gnb[:], in_=g.gn[l, 1:2, :].partition_broadcast(128)), writes=["gnb"])
    s.dma(lambda e: e.dma_start(out=oneh[:], in_=g.onehot), writes=["oneh"])
    s.dma(lambda e: e.dma_start(out=selp[:], in_=g.selprev), writes=["selp"])
    s.op("pool", lambda e: e.memset(stpack[:], 0.0), writes=["stpack"])
    idf = A([128, 128], F32, "idf")
    s.dma(lambda e: e.dma_start(out=idf[:], in_=g.c_ident), writes=["idf"])
    s.op("dve", lambda e: e.tensor_copy(out=identb[:], in_=idf[:]), reads=["idf"], writes=["identb"])
    bdf = A([128, 2, 3, 128], F32, "bdf")
    s.op("pool", lambda e: e.memset(bdf[:], 0.0), writes=["bdf"])
    for ax in range(2):
        for hd in range(6):
            po = 64 * (hd % 2)
            s.dma(lambda e, ax=ax, hd=hd, po=po: e.dma_start(out=bdf[po:po + 64, ax, hd // 2, po:po + 64], in_=g.lruw[l, ax, hd]),
                  reads=["bdf"], writes=["bdf"])
    s.op("dve", lambda e: e.tensor_copy(out=wabd[:], in_=bdf[:, 0, :, :]), reads=["bdf"], writes=["wabd"])
    s.op("dve", lambda e: e.tensor_copy(out=wxbd[:], in_=bdf[:, 1, :, :]), reads=["bdf"], writes=["wxbd"])
    s.op("act", lambda e: e.activation(out=cneg[:, 0:3], in_=pp[:, 21:24], func=AF.Exp, scale=-1.0), reads=["pp"], writes=["cneg"])
    s.op("dve", lambda e: e.tensor_scalar_add(out=cneg[:, 0:3], in0=cneg[:, 0:3], scalar1=1.0), reads=["cneg"], writes=["cneg"])
    s.op("act", lambda e: e.activation(out=cneg[:, 0:3], in_=cneg[:, 0:3], func=AF.Ln), reads=["cneg"], writes=["cneg"])
    s.op("dve", lambda e: e.tensor_scalar_mul(out=cneg[:, 3:6], in0=cneg[:, 0:3], scalar1=-16.0), reads=["cneg"], writes=["cneg2"])
    s.op("dve", lambda e: e.tensor_scalar_mul(out=cneg[:, 0:3], in0=cneg[:, 0:3], scalar1=-8.0), reads=["cneg", "cneg2"], writes=["cneg"])
    def unpack_state():
        s.op("dve", lambda e: e.tensor_copy(out=lstate[:], in_=stpack[:, 0:3]), reads=["stpack"], writes=["lstate"])
        s.op("dve", lambda e: e.tensor_copy(out=uext[:, :, 0:3], in_=stpack[:, 3:12].rearrange("p (a b) -> p a b", a=3)),
             reads=["stpack"], writes=["uext0", "uext1", "uext2"])
        s.op("dve", lambda e: e.tensor_copy(out=stt[:], in_=stpack[:, 12:204].rearrange("p (a b) -> p a b", a=3)),
             reads=["stpack"], writes=["stt"])
        s.op("dve", lambda e: e.tensor_copy(out=stbf[:], in_=stt[:]), reads=["stt"], writes=["stbf"])
        s.op("dve", lambda e: e.tensor_copy(out=carr[:], in_=stpack[:, 204:220]), reads=["stpack"], writes=["carr"])

    def pack_state():
        s.op("dve", lambda e: e.tensor_copy(out=stpack[:, 0:3], in_=lstate[:]), reads=["lstate"], writes=["stpack"])
        s.op("dve", lambda e: e.tensor_copy(out=stpack[:, 3:12].rearrange("p (a b) -> p a b", a=3), in_=uext[:, :, 0:3]),
             reads=["uext0", "uext1", "uext2", "stpack"], writes=["stpack"])
        s.op("dve", lambda e: e.tensor_copy(out=stpack[:, 12:204].rearrange("p (a b) -> p a b", a=3), in_=stt[:]), reads=["stt", "stpack"], writes=["stpack"])
        s.op("dve", lambda e: e.tensor_copy(out=stpack[:, 204:220], in_=carr[:]), reads=["carr", "stpack"], writes=["stpack"])

    unpack_state()

    def unpack_state_zero():
        s.op("pool", lambda e: e.memset(stpack[:], 0.0), reads=["stpack"], writes=["stpack"])
        unpack_state()

    def ssm_params(lr, li, ls, W, pfx, tmp):
        t = lambda i: tmp[:, i, :]
        dt_, mag, ang, sn, cn, ta, tb, zr1, er, ei, den, tt = [t(i) for i in range(12)]
        K = lambda nm: pfx + nm
        s.op("act", lambda e: e.activation(out=dt_, in_=ls, func=AF.Exp), reads=[K("in")], writes=[K("dt")])
        s.op("dve", lambda e: e.tensor_tensor(out=mag, in0=lr, in1=dt_, op=ALU.mult), reads=[K("in"), K("dt")], writes=[K("mag")])
        s.op("act", lambda e: e.activation(out=mag, in_=mag, func=AF.Exp), reads=[K("mag")], writes=[K("mag")])
        s.op("dve", lambda e: e.tensor_tensor(out=ang, in0=li, in1=dt_, op=ALU.mult), reads=[K("in"), K("dt")], writes=[K("ang")])
        emit_sincos(g, ang, sn, cn, ta, tb, (K("ang"), K("sn"), K("cn"), K("ta"), K("tb")))
        s.op("dve", lambda e: e.tensor_tensor(out=zr1, in0=mag, in1=cn, op=ALU.mult), reads=[K("mag"), K("cn")], writes=[K("zr1")])
        s.op("dve", lambda e: e.tensor_scalar_add(out=zr1, in0=zr1, scalar1=-1.0), reads=[K("zr1")], writes=[K("zr1")])
        s.op("dve", lambda e: e.tensor_tensor(out=tt, in0=mag, in1=sn, op=ALU.mult), reads=[K("mag"), K("sn")], writes=[K("zi")])
        s.op("dve", lambda e: e.tensor_tensor(out=den, in0=lr, in1=lr, op=ALU.mult), reads=[K("in")], writes=[K("den")])
        s.op("dve", lambda e: e.tensor_tensor(out=ta, in0=li, in1=li, op=ALU.mult), reads=[K("in"), K("ta")], writes=[K("ta")])
        s.op("dve", lambda e: e.tensor_tensor(out=den, in0=den, in1=ta, op=ALU.add), reads=[K("den"), K("ta")], writes=[K("den")])
        s.op("dve", lambda e: e.reciprocal(out=den, in_=den), reads=[K("den")], writes=[K("den")])
        s.op("dve", lambda e: e.tensor_tensor(out=er, in0=zr1, in1=lr, op=ALU.mult), reads=[K("zr1"), K("in")], writes=[K("er")])
        s.op("dve", lambda e: e.tensor_tensor(out=ta, in0=tt, in1=li, op=ALU.mult), reads=[K("zi"), K("in"), K("ta")], writes=[K("ta")])
        s.op("dve", lambda e: e.tensor_tensor(out=er, in0=er, in1=ta, op=ALU.add), reads=[K("er"), K("ta")], writes=[K("er")])
        s.op("dve", lambda e: e.tensor_tensor(out=er, in0=er, in1=den, op=ALU.mult), reads=[K("er"), K("den")], writes=[K("er")])
        s.op("dve", lambda e: e.tensor_tensor(out=ei, in0=tt, in1=lr, op=ALU.mult), reads=[K("zi"), K("in")], writes=[K("ei")])
        s.op("dve", lambda e: e.tensor_tensor(out=tb, in0=zr1, in1=li, op=ALU.mult), reads=[K("zr1"), K("in"), K("tb")], writes=[K("tb")])
        s.op("dve", lambda e: e.tensor_tensor(out=ei, in0=ei, in1=tb, op=ALU.subtract), reads=[K("ei"), K("tb")], writes=[K("ei")])
        s.op("dve", lambda e: e.tensor_tensor(out=ei, in0=ei, in1=den, op=ALU.mult), reads=[K("ei"), K("den")], writes=[K("ei")])
        return dict(mag=mag, ang=ang, er=er, ei=ei)

    sptmp = A([128, 12, 8], F32, "sptmp")
    s.op("dve", lambda e: e.tensor_copy(out=sp_[:, 0:24], in_=pp[:, 28:52]), reads=["pp"], writes=["S_in"])
    P = ssm_params(sp_[:, 0:8], sp_[:, 8:16], sp_[:, 16:24], 8, "S_", sptmp)
    rho = sp_[:, 48:56]
    s.op("dve", lambda e: e.tensor_copy(out=rho, in_=P["mag"]), reads=["S_mag"], writes=["rho"])
    th = sp_[:, 24:32]
    s.op("dve", lambda e: e.tensor_scalar(out=sp_[:, 32:40], in0=P["ang"], scalar1=1.0 / TWO_PI, scalar2=MAGIC, op0=ALU.mult, op1=ALU.add),
         reads=["S_ang"], writes=["S_k"])
    s.op("dve", lambda e: e.tensor_scalar_add(out=sp_[:, 32:40], in0=sp_[:, 32:40], scalar1=-MAGIC), reads=["S_k"], writes=["S_k"])
    s.op("dve", lambda e: e.scalar_tensor_tensor(out=th, in0=sp_[:, 32:40], scalar=-C1, in1=P["ang"], op0=ALU.mult, op1=ALU.add),
         reads=["S_k", "S_ang"], writes=["S_th"])
    s.op("dve", lambda e: e.scalar_tensor_tensor(out=th, in0=sp_[:, 32:40], scalar=-C2, in1=th, op0=ALU.mult, op1=ALU.add),
         reads=["S_k", "S_th"], writes=["S_th"])
    angt = A([128, 257], F32, "angt")
    tta = A([128, 257], F32, "tta")
    ttb = A([128, 257], F32, "ttb")
    for p in range(8):
        s.op("dve", lambda e, p=p: e.tensor_scalar_mul(out=angt[:], in0=iota[:], scalar1=th[:, p:p + 1]),
             reads=["iota", "S_th"], writes=["angt"])
        emit_sincos(g, angt[:], ST[:, p, :], CT[:, p, :], tta[:], ttb[:], ("angt", "ST%d" % p, "CT%d" % p, "tta", "ttb"))
    nST = sp_[:, 40:48]
    s.op("dve", lambda e: e.tensor_scalar_mul(out=nST, in0=ST[:, :, 256], scalar1=-1.0), reads=["ST%d" % p for p in range(8)], writes=["nST"])
    srow = A([128, 2, 5, 64], F32, "srow")
    s.dma(lambda e: e.dma_start(out=srow[:], in_=g.srow[l].rearrange("(c p) a b -> p c a b", p=128)), writes=["R_in"])
    rtmp = A([128, 12, 128], F32, "rtmp")
    rin = A([128, 3, 128], F32, "rin")
    for a_ in range(3):
        s.op("dve", lambda e, a_=a_: e.tensor_copy(out=rin[:, a_, :].rearrange("p (c b) -> p c b", c=2), in_=srow[:, :, a_, :]),
             reads=["R_in"], writes=["R_in2"])
    s.op("dve", lambda e: e.tensor_copy(out=rin[:, 0, 0:1], in_=rin[:, 0, 0:1]), reads=["R_in2"], writes=["R_in"])
    R = ssm_params(rin[:, 0, :], rin[:, 1, :], rin[:, 2, :], 128, "R_", rtmp)
    bbr = A([128, 2, 64], F32, "bbr")
    bbi = A([128, 2, 64], F32, "bbi")
    bt1 = A([128, 2, 64], F32, "bt1")
    er2 = R["er"].rearrange("p (c b) -> p c b", c=2)
    ei2 = R["ei"].rearrange("p (c b) -> p c b", c=2)
    s.op("dve", lambda e: e.tensor_tensor(out=bbr[:], in0=er2, in1=srow[:, :, 3, :], op=ALU.mult), reads=["R_er", "R_in"], writes=["bbr"])
    s.op("dve", lambda e: e.tensor_tensor(out=bt1[:], in0=ei2, in1=srow[:, :, 4, :], op=ALU.mult), reads=["R_ei", "R_in"], writes=["bt1"])
    s.op("dve", lambda e: e.tensor_tensor(out=bbr[:], in0=bbr[:], in1=bt1[:], op=ALU.subtract), reads=["bbr", "bt1"], writes=["bbr"])
    s.op("dve", lambda e: e.tensor_tensor(out=bbi[:], in0=er2, in1=srow[:, :, 4, :], op=ALU.mult), reads=["R_er", "R_in"], writes=["bbi"])
    s.op("dve", lambda e: e.tensor_tensor(out=bt1[:], in0=ei2, in1=srow[:, :, 3, :], op=ALU.mult), reads=["R_ei", "R_in", "bbr"], writes=["bt1"])
    s.op("dve", lambda e: e.tensor_tensor(out=bbi[:], in0=bbi[:], in1=bt1[:], op=ALU.add), reads=["bbi", "bt1"], writes=["bbi"])
    for p in range(8):
        ct, q = p // 4, p % 4
        for gl in range(2):
            mcol = 2 * q + gl
            s.op("dve", lambda e, p=p, ct=ct, gl=gl, mcol=mcol: e.tensor_scalar_mul(
                out=TBre[:, p, 64 * gl:64 * gl + 64], in0=bbr[:, ct, :], scalar1=gmask[:, mcol:mcol + 1]),
                reads=["bbr", "gmask"], writes=["TBre"])
            s.op("dve", lambda e, p=p, ct=ct, gl=gl, mcol=mcol: e.tensor_scalar_mul(
                out=TBim[:, p, 64 * gl:64 * gl + 64], in0=bbi[:, ct, :], scalar1=gmask[:, mcol:mcol + 1]),
                reads=["bbi", "gmask"], writes=["TBim"])
    cstt = A([128, 8, 2, 16], F32, "cstt")
    s.dma(lambda e: e.dma_start(out=cstt[:], in_=g.cst[l]), writes=["cstt"])
    s.op("pool", lambda e: e.memset(TCre[:], 0.0), writes=["TCre"])
    s.op("pool", lambda e: e.memset(TCim[:], 0.0), writes=["TCim"])
    for p in range(8):
        q = p % 4
        for gl in range(2):
            r0 = 64 * gl
            c0 = 32 * q + 16 * gl
            s.op("dve", lambda e, p=p, r0=r0, c0=c0: e.tensor_copy(out=TCre[r0:r0 + 64, p, c0:c0 + 16], in_=cstt[r0:r0 + 64, p, 0, :]),
                 reads=["cstt", "TCre"], writes=["TCre"])
            s.op("dve", lambda e, p=p, r0=r0, c0=c0: e.tensor_scalar_mul(out=TCim[r0:r0 + 64, p, c0:c0 + 16], in0=cstt[r0:r0 + 64, p, 1, :], scalar1=-1.0),
                 reads=["cstt", "TCim"], writes=["TCim"])
    s.barrier()
    sb.reset(mk)

    xin = A([128, 8, T], F32, "xin")
    h = A([128, 8, T], BF16, "h")
    ymix = A([128, 8, T], BF16, "ymix")
    posi = A([128, T], I32, "posi")
    posf = A([128, T], F32, "posf")
    rsn = A([128, T], F32, "rsn")
    rcs = A([128, T], F32, "rcs")
    rta = A([128, T], F32, "rta")
    rtb = A([128, T], F32, "rtb")
    Lc = A([128, 3, T], F32, "Lc")
    Lcb = A([128, 3, T], BF16, "Lcb")
    Lr = A([128, 3, T], F32, "Lr")
    Li = A([128, 3, T], F32, "Li")
    La = A([128, 3, T], F32, "La")
    La2 = A([128, 3, T], F32, "La2")
    Lg = La2
    qr = A([128, 3, T], BF16, "qr")
    kr = A([128, 3, T], BF16, "kr")
    qd = A([128, 3, T], BF16, "qd")
    vtm = A([128, NCH, 384], BF16, "vtm")
    gsl = A([128, NCH, 384], F32, "gsl")
    ktm = A([128, 384], BF16, "ktm")
    sm = A([128, 6, 128], BF16, "sm")
    osb = A([128, 384], F32, "osb")
    osq = A([128, 384], F32, "osq")
    gst = A([128, 4, 6], F32, "gst")
    yret = A([128, 384], BF16, "yret")
    uf = A([128, 2, T], F32, "uf")
    ubf = A([128, 2, T], BF16, "ubf")
    t1 = A([128, T], F32, "t1")
    t2 = A([128, T], F32, "t2")
    Vr = A([128, T], F32, "Vr")
    Vi = A([128, T], F32, "Vi")
    Wr = A([128, T], F32, "Wr")
    Wi = A([128, T], F32, "Wi")
    TS = [(t1, t2, Vr, Vi, Wr, Wi), tuple(A([128, T], F32, "ts2_%d" % i) for i in range(6))]
    Xr = A([128, 4, T], BF16, "Xr")
    Xi = A([128, 4, T], BF16, "Xi")
    ygf = A([128, 2, T], F32, "ygf")
    ygb = A([128, 2, T], BF16, "ygb")
    sgl = rta
    ctmp = A([128, 4], F32, "ctmp")
    tmp = dict(mean=A([128, T], F32, "mean"), m2=A([128, T], F32, "m2"),
               ybf=A([128, 8, T], BF16, "ybf"), ysq=A([128, 8, T], BF16, "ysq"))
    tmp["var"] = tmp["m2"]
    shc = l * 72 + 3 * 8
    scc = l * 72 + 4 * 8
    pbn = [0]

    def pbank():
        b = pbn[0] % 5
        pbn[0] += 1
        return g.ps[b], "ps%d" % b

    def proj(wt, c0, ncol=128):
        ps, key = pbank()
        for k in range(8):
            s.op("pe", lambda e, k=k: e.matmul(ps[:, 0:T], lhsT=wt[:, k, c0:c0 + 128], rhs=h[:, k, :], start=(k == 0), stop=(k == 7)),
                 reads=["h%d" % k, "wts"], writes=[key])
        return ps[:, 0:T], key

    def run_tile(it, full):
        t0 = it * T
        s.dma(lambda e, t0=t0: e.dma_start(out=xin[:], in_=dram_tile(src, t0, T)), writes=["xm%d" % j for j in range(8)])
        s.dma(lambda e, t0=t0: e.dma_start(out=posi[:], in_=g.pos[0:1, t0:t0 + T].partition_broadcast(128)), writes=["posi"])
        for k in range(8):
            s.op("dve", lambda e, k=k: e.tensor_scalar(
                out=h[:, k, :], in0=xin[:, k, :], scalar1=g.sc1p[:, scc + k:scc + k + 1], scalar2=g.sc1p[:, shc + k:shc + k + 1],
                op0=ALU.mult, op1=ALU.add), reads=["xm%d" % k, "sc1p"], writes=["h%d" % k])
        s.op("dve", lambda e: e.tensor_copy(out=posf[:], in_=posi[:]), reads=["posi"], writes=["posf"])
        s.op("dve", lambda e: e.tensor_scalar_mul(out=posf[:], in0=posf[:], scalar1=cs[:, 0:1]), reads=["posf", "cs"], writes=["posf"])
        emit_sincos(g, posf[:], rsn[:], rcs[:], rta[:], rtb[:], ("posf", "rsn", "rcs", "rta", "rtb"), sin_scale=cs[:, 1:2])

        for i in range(3):
            ups, ukey = proj(winb, i * 128)
            s.op("act", lambda e, i=i, ups=ups: e.copy(out=uext[:, i, 3:3 + T], in_=ups), reads=[ukey], writes=["uext%d" % i])
            s.op("dve", lambda e, i=i: e.tensor_scalar(out=Lc[:, i, :], in0=uext[:, i, 3:3 + T], scalar1=pp[:, i * 4 + 3:i * 4 + 4],
                                                       scalar2=pp[:, 12 + i:13 + i], op0=ALU.mult, op1=ALU.add),
                 reads=["uext%d" % i, "pp"], writes=["Lc%d" % i])
            for kk in range(3):
                s.op("dve", lambda e, i=i, kk=kk: e.scalar_tensor_tensor(
                    out=Lc[:, i, :], in0=uext[:, i, kk:kk + T], scalar=pp[:, i * 4 + kk:i * 4 + kk + 1], in1=Lc[:, i, :],
                    op0=ALU.mult, op1=ALU.add), reads=["uext%d" % i, "pp", "Lc%d" % i], writes=["Lc%d" % i])
            s.op("pool", lambda e, i=i: e.tensor_copy(out=uext[:, i, 0:3], in_=uext[:, i, T:T + 3]), reads=["uext%d" % i], writes=["uext%d" % i])
            s.op("pool", lambda e, i=i: e.tensor_copy(out=Lcb[:, i, :], in_=Lc[:, i, :]), reads=["Lc%d" % i], writes=["Lcb%d" % i])
        for i in range(3):
            pa, ka = pbank()
            s.op("pe", lambda e, i=i, pa=pa: e.matmul(pa[:, 0:T], lhsT=wabd[:, i, :], rhs=Lcb[:, i, :], start=True, stop=True),
                 reads=["Lcb%d" % i, "wts"], writes=[ka])
            s.op("act", lambda e, i=i, pa=pa: e.activation(out=Lr[:, i, :], in_=pa[:, 0:T], func=AF.Sigmoid, bias=pp[:, 15 + i:16 + i]),
                 reads=[ka, "pp"], writes=["Lr%d" % i])
            px, kx = pbank()
            s.op("pe", lambda e, i=i, px=px: e.matmul(px[:, 0:T], lhsT=wxbd[:, i, :], rhs=Lcb[:, i, :], start=True, stop=True),
                 reads=["Lcb%d" % i, "wts"], writes=[kx])
            s.op("act", lambda e, i=i, px=px: e.activation(out=Li[:, i, :], in_=px[:, 0:T], func=AF.Sigmoid, bias=pp[:, 18 + i:19 + i]),
                 reads=[kx, "pp"], writes=["Li%d" % i])
        for i in range(3):
            s.op("act", lambda e, i=i: e.activation(out=La[:, i, :], in_=Lr[:, i, :], func=AF.Exp, scale=cneg[:, i:i + 1]),
                 reads=["Lr%d" % i, "cneg"], writes=["La%d" % i])
            s.op("act", lambda e, i=i: e.activation(out=La2[:, i, :], in_=Lr[:, i, :], func=AF.Exp, scale=cneg[:, 3 + i:4 + i]),
                 reads=["Lr%d" % i, "cneg2"], writes=["La2%d" % i])
        for i in range(3):
            s.op("dve", lambda e, i=i: e.tensor_scalar(out=La2[:, i, :], in0=La2[:, i, :], scalar1=-1.0, scalar2=1.0, op0=ALU.mult, op1=ALU.add),
                 reads=["La2%d" % i], writes=["La2%d" % i])
            s.op("dve", lambda e, i=i: e.tensor_scalar_max(out=La2[:, i, :], in0=La2[:, i, :], scalar1=0.0), reads=["La2%d" % i], writes=["La2%d" % i])
            s.op("act", lambda e, i=i: e.activation(out=La2[:, i, :], in_=La2[:, i, :], func=AF.Sqrt), reads=["La2%d" % i], writes=["La2%d" % i])
        for i in range(3):
            s.op("dve", lambda e, i=i: e.tensor_tensor(out=Li[:, i, :], in0=Li[:, i, :], in1=Lc[:, i, :], op=ALU.mult),
                 reads=["Li%d" % i, "Lc%d" % i], writes=["Li%d" % i])
            s.op("dve", lambda e, i=i: e.tensor_tensor(out=Li[:, i, :], in0=Li[:, i, :], in1=La2[:, i, :], op=ALU.mult),
                 reads=["Li%d" % i, "La2%d" % i], writes=["Li%d" % i])
            s.op("dve", lambda e, i=i: e.tensor_tensor_scan(out=Lr[:, i, :], data0=La[:, i, :], data1=Li[:, i, :], initial=lstate[:, i:i + 1],
                                                            op0=ALU.mult, op1=ALU.add),
                 reads=["La%d" % i, "Li%d" % i, "lstate", "Lr%d" % i], writes=["Lr%d" % i])
            s.op("dve", lambda e, i=i: e.tensor_copy(out=lstate[:, i:i + 1], in_=Lr[:, i, T - 1:T]), reads=["Lr%d" % i, "lstate"], writes=["lstate"])
        for i in range(3 if full else 0):
            gps, gkey = proj(winb, 384 + i * 128)
            s.op("act", lambda e, i=i, gps=gps: e.activation(out=Lg[:, i, :], in_=gps, func=AF.Gelu_apprx_tanh), reads=[gkey, "La2%d" % i], writes=["La2%d" % i])
            s.op("dve", lambda e, i=i: e.tensor_tensor(out=ymix[:, i, :], in0=Lr[:, i, :], in1=Lg[:, i, :], op=ALU.mult),
                 reads=["Lr%d" % i, "La2%d" % i], writes=["ym%d" % i])

        for (dstt, c0, nm) in (((qr, 768, "qr"), (kr, 1152, "kr")) if full else ((kr, 1152, "kr"),)):
            for i in range(3):
                par = i % 2
                a1, a2 = TS[par][0], TS[par][1]
                kk1, kk2 = "t1_%d" % par, "t2_%d" % par
                p1, k1 = proj(winb, c0 + i * 128)
                s.op("dve", lambda e, p1=p1, a1=a1: e.tensor_tensor(out=a1[:], in0=p1, in1=rcs[:], op=ALU.mult), reads=[k1, "rcs"], writes=[kk1])
                p2, k2 = proj(wswb, (0 if nm == "qr" else 384) + i * 128)
                s.op("dve", lambda e, p2=p2, a2=a2: e.tensor_tensor(out=a2[:], in0=p2, in1=rsn[:], op=ALU.mult), reads=[k2, "rsn"], writes=[kk2])
                s.op("pool", lambda e, i=i, dstt=dstt, a1=a1, a2=a2: e.tensor_tensor(out=dstt[:, i, :], in0=a1[:], in1=a2[:], op=ALU.add),
                     reads=[kk1, kk2], writes=["%s%d" % (nm, i)])
        for i in range(3 if full else 0):
            for c in range(NCH):
                s.op("pool", lambda e, i=i, c=c: e.tensor_tensor(out=qd[:, i, c * 128:(c + 1) * 128], in0=qr[:, i, c * 128:(c + 1) * 128],
                                                                 in1=qdec[:, i, :], op=ALU.mult),
                     reads=["qr%d" % i, "qdec"], writes=["qd%d" % i])
        for c in range(NCH):
            for (c0, nm) in (((1536, "v"), (1920, "g")) if full else ((1536, "v"),)):
                ps, key = pbank()
                for k in range(8):
                    s.op("pe", lambda e, k=k, c=c, c0=c0, ps=ps: e.matmul(ps[:, 0:384], lhsT=h[:, k, c * 128:(c + 1) * 128],
                                                                         rhs=winb[:, k, c0:c0 + 384], start=(k == 0), stop=(k == 7)),
                         reads=["h%d" % k, "wts"], writes=[key])
                if nm == "v":
                    s.op("act", lambda e, c=c, ps=ps: e.copy(out=vtm[:, c, :], in_=ps[:, 0:384]), reads=[key], writes=["vtm%d" % c])
                else:
                    s.op("act", lambda e, c=c, ps=ps: e.activation(out=gsl[:, c, :], in_=ps[:, 0:384], func=AF.Silu), reads=[key], writes=["gsl%d" % c])
        for c in range(NCH):
            cs_ = slice(c * 128, (c + 1) * 128)
            pk, kk_ = pbank()
            for i in range(3):
                s.op("pe", lambda e, i=i, pk=pk, cs_=cs_: e.matmul(pk[:, i * 128:(i + 1) * 128], lhsT=kr[:, i, cs_], rhs=identb[:], start=True, stop=True),
                     reads=["kr%d" % i, "identb"], writes=[kk_])
            s.op("dve", lambda e, pk=pk: e.tensor_tensor(out=ktm[:], in0=pk[:, 0:384], in1=kdt[:], op=ALU.mult), reads=[kk_, "kdt"], writes=["ktm"])
            po_, ko_ = g.ps[5], "ps5"
            for hd in range(6 if full else 0):
                i, po = hd // 2, 64 * (hd % 2)
                psc, ksc = pbank()
                s.op("pe", lambda e, i=i, po=po, psc=psc, cs_=cs_: e.matmul(psc[:, 0:128], lhsT=kr[po:po + 64, i, cs_], rhs=qr[po:po + 64, i, cs_],
                                                                           start=True, stop=True),
                     reads=["kr%d" % i, "qr%d" % i], writes=[ksc])
                s.op("dve", lambda e, hd=hd, psc=psc: e.tensor_tensor(out=sm[:, hd, :], in0=psc[:, 0:128], in1=maskT[:, hd, :], op=ALU.mult),
                     reads=[ksc, "maskT"], writes=["sm%d" % hd])
                s.op("pe", lambda e, hd=hd, c=c, po_=po_: e.matmul(po_[:, hd * 64:(hd + 1) * 64], lhsT=sm[:, hd, :], rhs=vtm[:, c, hd * 64:(hd + 1) * 64],
                                                                  start=True, stop=False),
                     reads=["sm%d" % hd, "vtm%d" % c], writes=[ko_])
                s.op("pe", lambda e, hd=hd, i=i, po=po, po_=po_, cs_=cs_: e.matmul(po_[:, hd * 64:(hd + 1) * 64], lhsT=qd[po:po + 64, i, cs_],
                                                                                  rhs=stbf[po:po + 64, i, :], start=False, stop=True),
                     reads=["qd%d" % i, "stbf"], writes=[ko_])
            for i in range(3):
                pkv, kkv = pbank()
                s.op("pe", lambda e, i=i, c=c, pkv=pkv: e.matmul(pkv[:, 0:128], lhsT=ktm[:, i * 128:(i + 1) * 128], rhs=vtm[:, c, i * 128:(i + 1) * 128],
                                                                start=True, stop=True),
                     reads=["ktm", "vtm%d" % c], writes=[kkv])
                for hh in range(2):
                    hd = 2 * i + hh
                    po = 64 * hh
                    cdv = float(np.exp(128.0 * np.log1p(-np.exp2(-5.0 - hd))))
                    s.op("dve", lambda e, i=i, po=po, pkv=pkv, cdv=cdv: e.scalar_tensor_tensor(
                        out=stt[po:po + 64, i, :], in0=stt[po:po + 64, i, :], scalar=cdv, in1=pkv[po:po + 64, po:po + 64],
                        op0=ALU.mult, op1=ALU.add), reads=[kkv, "stt", "stbf"], writes=["stt"])
            s.op("pool", lambda e: e.tensor_copy(out=stbf[:], in_=stt[:]), reads=["stt"], writes=["stbf"])
            if not full:
                continue
            s.op("act", lambda e, po_=po_: e.copy(out=osb[:], in_=po_[:, 0:384]), reads=[ko_], writes=["osb"])
            s.op("act", lambda e, po_=po_: e.activation(out=osq[:], in_=po_[:, 0:384], func=AF.Square), reads=[ko_], writes=["osq"])
            s.op("dve", lambda e: e.tensor_reduce(out=gst[:, 0, :], in_=osb[:].rearrange("p (a b) -> p a b", a=6), axis=AX.X, op=ALU.add),
                 reads=["osb"], writes=["gst0"])
            s.op("dve", lambda e: e.tensor_reduce(out=gst[:, 1, :], in_=osq[:].rearrange("p (a b) -> p a b", a=6), axis=AX.X, op=ALU.add),
                 reads=["osq"], writes=["gst1"])
            s.op("dve", lambda e: e.tensor_scalar_mul(out=gst[:, 0, :], in0=gst[:, 0, :], scalar1=1.0 / 64), reads=["gst0"], writes=["gst0"])
            s.op("dve", lambda e: e.tensor_tensor(out=gst[:, 2, :], in0=gst[:, 0, :], in1=gst[:, 0, :], op=ALU.mult), reads=["gst0"], writes=["gst2"])
            s.op("dve", lambda e: e.scalar_tensor_tensor(out=gst[:, 1, :], in0=gst[:, 1, :], scalar=1.0 / 64, in1=gst[:, 2, :], op0=ALU.mult, op1=ALU.subtract),
                 reads=["gst1", "gst2"], writes=["gst1"])
            s.op("dve", lambda e: e.tensor_scalar(out=gst[:, 1, :], in0=gst[:, 1, :], scalar1=0.0, scalar2=1e-5, op0=ALU.max, op1=ALU.add),
                 reads=["gst1"], writes=["gst1"])
            s.op("act", lambda e: e.activation(out=gst[:, 1, :], in_=gst[:, 1, :], func=AF.Sqrt), reads=["gst1"], writes=["gst1"])
            s.op("dve", lambda e: e.reciprocal(out=gst[:, 1, :], in_=gst[:, 1, :]), reads=["gst1"], writes=["gst1"])
            o3 = osb[:].rearrange("p (a b) -> p a b", a=6)
            s.op("dve", lambda e: e.tensor_tensor(out=o3, in0=o3, in1=bc_inner(gst[:, 0, :], 64), op=ALU.subtract), reads=["osb", "gst0"], writes=["osb"])
            s.op("dve", lambda e: e.tensor_tensor(out=o3, in0=o3, in1=bc_inner(gst[:, 1, :], 64), op=ALU.mult), reads=["osb", "gst1"], writes=["osb"])
            s.op("pool", lambda e: e.tensor_tensor(out=osb[:], in0=osb[:], in1=gng[:], op=ALU.mult), reads=["osb", "gng"], writes=["osb"])
            s.op("pool", lambda e: e.tensor_tensor(out=osb[:], in0=osb[:], in1=gnb[:], op=ALU.add), reads=["osb", "gnb"], writes=["osb"])
            s.op("dve", lambda e, c=c: e.tensor_tensor(out=yret[:], in0=osb[:], in1=gsl[:, c, :], op=ALU.mult), reads=["osb", "gsl%d" % c], writes=["yret"])
            for i in range(3):
                pt, kt = pbank()
                s.op("pe", lambda e, i=i, pt=pt: e.matmul(pt[:, 0:128], lhsT=yret[:, i * 128:(i + 1) * 128], rhs=identb[:], start=True, stop=True),
                     reads=["yret", "identb"], writes=[kt])
                s.op("act", lambda e, i=i, pt=pt, cs_=cs_: e.copy(out=ymix[:, 3 + i, cs_], in_=pt[:, 0:128]), reads=[kt], writes=["ym%d" % (3 + i)])

        for ct in range(2):
            ups, ukey = proj(winb, 2304 + ct * 128)
            s.op("act", lambda e, ct=ct, ups=ups: e.copy(out=uf[:, ct, :], in_=ups), reads=[ukey], writes=["uf%d" % ct])
            s.op("pool", lambda e, ct=ct: e.tensor_copy(out=ubf[:, ct, :], in_=uf[:, ct, :]), reads=["uf%d" % ct], writes=["ubf%d" % ct])
        def stageA(p):
            ct = p // 4
            par = p % 2
            a1, a2, aVr, aVi, aWr, aWi = TS[par]
            k1, k2, kVr, kVi, kWr, kWi = ["%s_%d" % (nm, par) for nm in ("t1", "t2", "Vr", "Vi", "Wr", "Wi")]
            pr, kr_ = pbank()
            s.op("pe", lambda e, p=p, ct=ct, pr=pr: e.matmul(pr[:, 0:T], lhsT=TBre[:, p, :], rhs=ubf[:, ct, :], start=True, stop=True),
                 reads=["ubf%d" % ct, "wts"], writes=[kr_])
            pi_, ki_ = pbank()
            s.op("pe", lambda e, p=p, ct=ct, pi_=pi_: e.matmul(pi_[:, 0:T], lhsT=TBim[:, p, :], rhs=ubf[:, ct, :], start=True, stop=True),
                 reads=["ubf%d" % ct, "wts"], writes=[ki_])
            Cp, Sp = CT[:, p, 0:T], ST[:, p, 0:T]
            s.op("dve", lambda e, pr=pr, Cp=Cp, a1=a1: e.tensor_tensor(out=a1[:], in0=pr[:, 0:T], in1=Cp, op=ALU.mult), reads=[kr_], writes=[k1])
            s.op("dve", lambda e, pi_=pi_, Sp=Sp, a2=a2: e.tensor_tensor(out=a2[:], in0=pi_[:, 0:T], in1=Sp, op=ALU.mult), reads=[ki_], writes=[k2])
            s.op("pool", lambda e, a1=a1, a2=a2, aVr=aVr: e.tensor_tensor(out=aVr[:], in0=a1[:], in1=a2[:], op=ALU.add), reads=[k1, k2], writes=[kVr])
            s.op("dve", lambda e, pi_=pi_, Cp=Cp, a1=a1: e.tensor_tensor(out=a1[:], in0=pi_[:, 0:T], in1=Cp, op=ALU.mult), reads=[ki_, k1], writes=[k1])
            s.op("dve", lambda e, pr=pr, Sp=Sp, a2=a2: e.tensor_tensor(out=a2[:], in0=pr[:, 0:T], in1=Sp, op=ALU.mult), reads=[kr_, k2], writes=[k2])
            s.op("pool", lambda e, a1=a1, a2=a2, aVi=aVi: e.tensor_tensor(out=aVi[:], in0=a1[:], in1=a2[:], op=ALU.subtract), reads=[k1, k2], writes=[kVi])
            rb = rho[:, p:p + 1].to_broadcast([128, T])
            s.op("dve", lambda e, p=p, rb=rb, aWr=aWr, aVr=aVr: e.tensor_tensor_scan(out=aWr[:], data0=rb, data1=aVr[:], initial=carr[:, p:p + 1], op0=ALU.mult, op1=ALU.add),
                 reads=[kVr, "carr", kWr], writes=[kWr])
            s.op("dve", lambda e, p=p, rb=rb, aWi=aWi, aVi=aVi: e.tensor_tensor_scan(out=aWi[:], data0=rb, data1=aVi[:], initial=carr[:, 8 + p:9 + p], op0=ALU.mult, op1=ALU.add),
                 reads=[kVi, "carr", kWi], writes=[kWi])
            c0_, c1_ = 2 * par, 2 * par + 1
            s.op("pool", lambda e, p=p, aWi=aWi, c0_=c0_: e.tensor_scalar_mul(out=ctmp[:, c0_:c0_ + 1], in0=aWi[:, T - 1:T], scalar1=nST[:, p:p + 1]),
                 reads=[kWi, "nST"], writes=["ctmp%d" % c0_])
            s.op("dve", lambda e, p=p, aWr=aWr, c0_=c0_: e.scalar_tensor_tensor(out=carr[:, p:p + 1], in0=aWr[:, T - 1:T], scalar=CT[:, p, T:T + 1], in1=ctmp[:, c0_:c0_ + 1],
                                                            op0=ALU.mult, op1=ALU.add), reads=[kWr, "ctmp%d" % c0_, "carr"], writes=["carr"])
            s.op("pool", lambda e, p=p, aWr=aWr, c1_=c1_: e.tensor_scalar_mul(out=ctmp[:, c1_:c1_ + 1], in0=aWr[:, T - 1:T], scalar1=ST[:, p, T:T + 1]),
                 reads=[kWr], writes=["ctmp%d" % c1_])
            s.op("dve", lambda e, p=p, aWi=aWi, c1_=c1_: e.scalar_tensor_tensor(out=carr[:, 8 + p:9 + p], in0=aWi[:, T - 1:T], scalar=CT[:, p, T:T + 1], in1=ctmp[:, c1_:c1_ + 1],
                                                            op0=ALU.mult, op1=ALU.add), reads=[kWi, "ctmp%d" % c1_, "carr"], writes=["carr"])
        def stageB(p):
            ct = p // 4
            par = p % 2
            a1, a2, aVr, aVi, aWr, aWi = TS[par]
            k1, k2, kVr, kVi, kWr, kWi = ["%s_%d" % (nm, par) for nm in ("t1", "t2", "Vr", "Vi", "Wr", "Wi")]
            Cp, Sp = CT[:, p, 0:T], ST[:, p, 0:T]
            s.op("dve", lambda e, Cp=Cp, a1=a1, aWr=aWr: e.tensor_tensor(out=a1[:], in0=aWr[:], in1=Cp, op=ALU.mult), reads=[kWr, k1], writes=[k1])
            s.op("pool", lambda e, Sp=Sp, a2=a2, aWi=aWi: e.tensor_tensor(out=a2[:], in0=aWi[:], in1=Sp, op=ALU.mult), reads=[kWi, k2], writes=[k2])
            s.op("pool", lambda e, p=p, a1=a1, a2=a2: e.tensor_tensor(out=Xr[:, p % 4, :], in0=a1[:], in1=a2[:], op=ALU.subtract), reads=[k1, k2], writes=["Xr%d" % (p % 4)])
            s.op("dve", lambda e, Cp=Cp, a1=a1, aWi=aWi: e.tensor_tensor(out=a1[:], in0=aWi[:], in1=Cp, op=ALU.mult), reads=[kWi, k1], writes=[k1])
            s.op("pool", lambda e, Sp=Sp, a2=a2, aWr=aWr: e.tensor_tensor(out=a2[:], in0=aWr[:], in1=Sp, op=ALU.mult), reads=[kWr, k2], writes=[k2])
            s.op("pool", lambda e, p=p, a1=a1, a2=a2: e.tensor_tensor(out=Xi[:, p % 4, :], in0=a1[:], in1=a2[:], op=ALU.add), reads=[k1, k2], writes=["Xi%d" % (p % 4)])
        def ymm(ct):
            py, ky = pbank()
            for q in range(4):
                p = ct * 4 + q
                s.op("pe", lambda e, p=p, q=q, py=py: e.matmul(py[:, 0:T], lhsT=TCre[:, p, :], rhs=Xr[:, q, :], start=(q == 0), stop=False),
                     reads=["Xr%d" % q, "wts"], writes=[ky])
                s.op("pe", lambda e, p=p, q=q, py=py: e.matmul(py[:, 0:T], lhsT=TCim[:, p, :], rhs=Xi[:, q, :], start=False, stop=(q == 3)),
                     reads=["Xi%d" % q, "wts"], writes=[ky])
            s.op("dve", lambda e, ct=ct, py=py: e.scalar_tensor_tensor(out=ygf[:, ct, :], in0=uf[:, ct, :], scalar=pp[:, 24 + ct:25 + ct], in1=py[:, 0:T],
                                                                      op0=ALU.mult, op1=ALU.add), reads=[ky, "uf%d" % ct, "pp"], writes=["ygf%d" % ct])
            s.op("act", lambda e, ct=ct: e.activation(out=ygf[:, ct, :], in_=ygf[:, ct, :], func=AF.Gelu_apprx_tanh), reads=["ygf%d" % ct], writes=["ygf%d" % ct])
            s.op("pool", lambda e, ct=ct: e.tensor_copy(out=ygb[:, ct, :], in_=ygf[:, ct, :]), reads=["ygf%d" % ct], writes=["ygb%d" % ct])

        stageA(0)
        for p in range(8):
            if p + 1 < 8:
                stageA(p + 1)
            if full:
                stageB(p)
                if p % 4 == 3:
                    ymm(p // 4)
        if not full:
            return
        for co in range(2):
            pg, kg = pbank()
            for ck in range(2):
                s.op("pe", lambda e, co=co, ck=ck, pg=pg: e.matmul(pg[:, 0:T], lhsT=wglub[:, ck, co * 128:(co + 1) * 128], rhs=ygb[:, ck, :],
                                                                  start=(ck == 0), stop=(ck == 1)), reads=["ygb%d" % ck, "wts"], writes=[kg])
            s.op("act", lambda e, co=co, pg=pg: e.activation(out=sgl[:], in_=pg[:, 0:T], func=AF.Sigmoid, bias=pp[:, 26 + co:27 + co]),
                 reads=[kg, "pp", "rta"], writes=["sgl", "rta"])
            s.op("dve", lambda e, co=co: e.tensor_tensor(out=ymix[:, 6 + co, :], in0=ygf[:, co, :], in1=sgl[:], op=ALU.mult),
                 reads=["ygf%d" % co, "sgl"], writes=["ym%d" % (6 + co)])

        def acc_fn(j):
            pa, key = pbank()
            for k in range(8):
                s.op("pe", lambda e, k=k, j=j, pa=pa: e.matmul(pa[:, 0:T], lhsT=woutb[:, k, j * 128:(j + 1) * 128], rhs=ymix[:, k, :],
                                                              start=(k == 0), stop=(k == 7)), reads=["ym%d" % k, "wts"], writes=[key])
            return pa[:, 0:T], key

        emit_resid_ln(g, l, 1, xin, "xm", acc_fn, T, tmp)
        i_ = s.dma(lambda e, t0=t0: e.dma_start(out=dram_tile(dst, t0, T), in_=xin[:]), reads=["xm%d" % j for j in range(8)])
        s.ops[i_]["is_out"] = dst is g.outT


    import os
    MODE = os.environ.get("KMODE", "full")
    if MODE == "B":
        for it in range(ntile):
            run_tile(it, True)
        return
    for it in range(ntile):
        run_tile(it, False)
    pack_state()
    if MODE == "AB":
        unpack_state_zero()
        for it in range(ntile):
            run_tile(it, True)
        return
    contrib = xin[:].rearrange("p a b -> p (a b)")[:, 0:8 * NST].rearrange("p (a b) -> p a b", a=8)
    CK = ["xm%d" % j for j in range(8)]
    for r in range(8):
        s.op("dve", lambda e, r=r: e.tensor_scalar_mul(out=contrib[:, r, :], in0=stpack[:], scalar1=oneh[:, r:r + 1]),
             reads=["stpack", "oneh"], writes=["contrib"] + CK)
    s.dma(lambda e: e.dma_start(out=g.cc_in[l][:, :], in_=contrib.rearrange("p a b -> p (a b)")), reads=["contrib"], writes=["cc_in"], q="pool")
    s.dma(lambda e: e.collective_compute("AllReduce", ALU.add, replica_groups=[list(range(8))],
                                         ins=[g.cc_in[l].ap().opt()], outs=[g.cc_out[l].ap().opt()]),
          reads=["cc_in"], writes=["cc_out"], q="pool", inc=1)
    s.dma(lambda e: e.dma_start(out=contrib.rearrange("p a b -> p (a b)"), in_=g.cc_out[l][:, :]), reads=["cc_out"], writes=["contrib"] + CK, q="pool")
    s.op("dve", lambda e: e.tensor_scalar_mul(out=stpack[:], in0=contrib[:, 0, :], scalar1=selp[:, 0:1]), reads=["contrib", "selp"], writes=["stpack"])
    for r in range(1, 8):
        s.op("dve", lambda e, r=r: e.scalar_tensor_tensor(out=stpack[:], in0=contrib[:, r, :], scalar=selp[:, r:r + 1], in1=stpack[:],
                                                          op0=ALU.mult, op1=ALU.add), reads=["contrib", "selp", "stpack"], writes=["stpack"])
    s.op("dve", lambda e: e.tensor_copy(out=ctmp[:, 0:1], in_=contrib[:, 0, 0:1]), reads=["contrib"] + CK, writes=["ctmp0"])
    unpack_state()
    for it in range(ntile):
        run_tile(it, True)


_NC_CACHE = {}


def _prep_common(inp):
    cm = {}
    ada_b = np.asarray(inp["ada_b"], np.float32)
    cm["ada_b"] = np.ascontiguousarray(ada_b.reshape(DEPTH, 72, 128).transpose(2, 0, 1).reshape(128, DEPTH * 72))
    for nm in ("ln_g", "ln_b"):
        a = np.asarray(inp[nm], np.float32)
        cm[nm] = np.ascontiguousarray(a.reshape(DEPTH, 3, 8, 128).transpose(3, 0, 1, 2).reshape(128, DEPTH * 24))
    for nm in ("ada_w", "ffn1_w1", "ffn1_w3", "ffn1_w2", "ffn2_w1", "ffn2_w3", "ffn2_w2", "mix_w_in", "mix_w_out", "ssm_w_glu"):
        cm[nm] = np.ascontiguousarray(np.asarray(inp[nm], np.float32))
    f = lambda nm: np.asarray(inp[nm], np.float32)
    L = DEPTH
    w_in = cm["mix_w_in"]
    idx = []
    for base in (768, 1152):
        for hd in range(6):
            for j in range(64):
                idx.append(base + hd * 64 + (j + 32) % 64)
    cm["w_sw"] = np.ascontiguousarray(w_in[:, :, idx])
    pp = np.zeros((L, 128, 52), np.float32)
    pp[:, :, 0:12] = f("conv_w").reshape(L, 4, 3, 128).transpose(0, 3, 2, 1).reshape(L, 128, 12)
    for c0, nm in ((12, "conv_b"), (15, "lru_ba"), (18, "lru_bx"), (21, "lru_lam")):
        pp[:, :, c0:c0 + 3] = f(nm).reshape(L, 3, 128).transpose(0, 2, 1)
    for c0, nm in ((24, "ssm_d"), (26, "ssm_b_glu")):
        pp[:, :, c0:c0 + 2] = f(nm).reshape(L, 2, 128).transpose(0, 2, 1)
    for c0, nm in ((28, "ssm_lam_re"), (36, "ssm_lam_im")):
        pp[:, :, c0:c0 + 8] = f(nm).reshape(L, 8, 2, 64).transpose(0, 2, 3, 1).reshape(L, 128, 8)
    ls = np.repeat(f("ssm_log_step")[:, :, None], 64, axis=2)
    pp[:, :, 44:52] = ls.reshape(L, 8, 2, 64).transpose(0, 2, 3, 1).reshape(L, 128, 8)
    cm["pp"] = pp
    cm["lruw"] = np.ascontiguousarray(np.stack([f("lru_wa"), f("lru_wx")], axis=1))
    cm["gn"] = np.ascontiguousarray(np.stack([f("ret_gn_g"), f("ret_gn_b")], axis=1))
    srow = np.zeros((L, 16, 16, 5, 64), np.float32)
    srow[:, :, :, 0, :] = f("ssm_lam_re")[:, :, None, :]
    srow[:, :, :, 1, :] = f("ssm_lam_im")[:, :, None, :]
    srow[:, :, :, 2, :] = f("ssm_log_step")[:, :, None, None]
    srow[:, :, :, 3, :] = f("ssm_b_re").transpose(0, 1, 3, 2)
    srow[:, :, :, 4, :] = f("ssm_b_im").transpose(0, 1, 3, 2)
    cm["srow"] = srow.reshape(L, 256, 5, 64)
    cst = np.zeros((L, 2, 64, 8, 2, 16), np.float32)
    for ri, nm in ((0, "ssm_c_re"), (1, "ssm_c_im")):
        a = f(nm).reshape(L, 8, 2, 16, 64)
        cst[:, :, :, :, ri, :] = a.transpose(0, 2, 4, 1, 3)
    cm["cst"] = cst.reshape(L, 128, 8, 2, 16)
    p = np.arange(128)
    csm = np.zeros((128, 8), np.float32)
    csm[:, 0] = (10000.0 ** (-(p % 32).astype(np.float32) / 32.0)).astype(np.float32)
    csm[:, 1] = np.where((p % 64) < 32, -1.0, 1.0)
    cm["c_small"] = csm
    cm["c_iota"] = np.ascontiguousarray(np.broadcast_to(np.arange(257, dtype=np.float32)[None, :], (128, 257)))
    lg = np.log1p(-np.exp2(-5.0 - np.arange(6, dtype=np.float64)))
    kk = np.arange(128)[:, None]
    qq = np.arange(128)[None, :]
    mT = np.zeros((128, 6, 128), np.float32)
    for hd in range(6):
        mT[:, hd, :] = np.where(qq >= kk, np.exp(lg[hd] * np.maximum(qq - kk, 0)), 0.0) * 0.125
    cm["c_maskT"] = mT
    qd = np.zeros((128, 3, 128), np.float32)
    for i in range(3):
        for h2 in range(2):
            qd[h2 * 64:(h2 + 1) * 64, i, :] = np.exp(lg[2 * i + h2] * (np.arange(128) + 1.0))[None, :]
    cm["c_qdec"] = qd
    kd = np.zeros((128, 6, 64), np.float32)
    for hd in range(6):
        kd[:, hd, :] = (np.exp(lg[hd] * (127.0 - np.arange(128))) * 0.125)[:, None]
    cm["c_kdt"] = kd.reshape(128, 384)
    cm["c_gmask"] = (p[:, None] // 16 == np.arange(8)[None, :]).astype(np.float32)
    cm["c_ident"] = np.eye(128, dtype=np.float32)
    return cm


def kernel(**inp):
    x = np.asarray(inp["x"], np.float32)
    c = np.asarray(inp["c"], np.float32)
    pos = np.asarray(inp["positions"], np.int32)
    B, S, _ = x.shape
    if "full" not in _NC_CACHE:
        _NC_CACHE["full"] = build_program()
    nc = _NC_CACHE["full"]
    cm = _prep_common(inp)
    in_maps = []
    for core in range(8):
        b, hf = core // 2, core % 2
        m = dict(cm)
        m["xT"] = np.ascontiguousarray(x[b, hf * NTOK:(hf + 1) * NTOK, :].T)
        m["cvec"] = np.ascontiguousarray(c[b].reshape(8, 128).T)
        m["pos"] = np.ascontiguousarray(pos[b, hf * NTOK:(hf + 1) * NTOK][None, :])
        oh = np.zeros((128, 8), np.float32)
        oh[:, core] = 1.0
        sp = np.zeros((128, 8), np.float32)
        if hf == 1:
            sp[:, core - 1] = 1.0
        m["onehot"] = oh
        m["selprev"] = sp
        in_maps.append(m)
    res = run_bass_kernel_spmd(nc, in_maps, core_ids=list(range(8)))
    out = np.empty((B, S, D), np.float32)
    for core in range(8):
        b, hf = core // 2, core % 2
        out[b, hf * NTOK:(hf + 1) * NTOK, :] = res.results[core]["outT"].T
    return out
```

```python
from contextlib import ExitStack
import numpy as np
import concourse.bass as bass
import concourse.mybir as mybir
from concourse.bass_utils import run_bass_kernel_spmd

F32 = mybir.dt.float32
BF16 = mybir.dt.bfloat16
I32 = mybir.dt.int32
AF = mybir.ActivationFunctionType
ALU = mybir.AluOpType
AX = mybir.AxisListType

D = 1024
DFF = 2816
NF = DFF // 128
DEPTH = 2
NIN = 2560
ALPHA = (2.0 * DEPTH) ** 0.25
EPS_LN = 1e-5 / (ALPHA * ALPHA)
NTOK = 4096
MAGIC = 12582912.0
TWO_PI = 6.283185307179586
C1 = 6.28125
C2 = TWO_PI - C1


class Sched:
    ENGS = ("pe", "dve", "act", "pool", "sp")

    def __init__(self, nc):
        self.nc = nc
        self.ops = []
        self.last_w = {}
        self.readers = {}
        self.n_dma_sems = 24
        self.bar_start = 0

    def op(self, eng, fn, reads=(), writes=(), dma=False):
        deps = set()
        for k in reads:
            w = self.last_w.get(k)
            if w is not None:
                deps.add(w)
        for k in writes:
            w = self.last_w.get(k)
            if w is not None:
                deps.add(w)
            for r in self.readers.get(k, ()):
                deps.add(r)
        idx = len(self.ops)
        if not dma and eng == "pe":
            deps = {d for d in deps if self.ops[d]["dma"] or self.ops[d]["eng"] != "pe"}
        self.ops.append(dict(eng=eng, fn=fn, deps=deps, dma=dma, sig=False))
        for k in writes:
            self.last_w[k] = idx
            self.readers[k] = []
        for k in reads:
            if k not in writes:
                self.readers.setdefault(k, []).append(idx)
        return idx

    def dma(self, fn, reads=(), writes=(), q="sp", inc=16):
        i = self.op(q, fn, reads, writes, dma=True)
        self.ops[i]["inc"] = inc
        return i

    def barrier(self):
        last = {}
        for i, o in enumerate(self.ops):
            if i < self.bar_start:
                continue
            key = ("dma", i) if o["dma"] else o["eng"]
            last[key] = i
        deps = set(last.values())
        for e in self.ENGS:
            self.ops.append(dict(eng=e, fn=None, deps=set(deps), dma=False, sig=False))
        self.bar_start = len(self.ops)
        self.last_w = {}
        self.readers = {}

    def emit(self, es, final_wait_ops=()):
        nc = self.nc
        ops = self.ops
        for o in ops:
            for d in o["deps"]:
                ops[d]["sig"] = True
        for i in final_wait_ops:
            ops[i]["sig"] = True
        esem = {e: es.enter_context(nc.semaphore("s_" + e)) for e in self.ENGS}
        dsem = [es.enter_context(nc.semaphore("d%d" % i)) for i in range(self.n_dma_sems)]
        csem = es.enter_context(nc.semaphore("ccsem"))
        ccnt = 0
        cnt = {e: 0 for e in self.ENGS}
        dcnt = [0] * self.n_dma_sems
        dlast = [None] * self.n_dma_sems
        nd = 0
        for i, o in enumerate(ops):
            if o["dma"] and o.get("inc", 16) == 1:
                ccnt += 1
                o["ev"] = (csem, ccnt, ("c", 0))
            elif o["dma"]:
                k = nd % self.n_dma_sems
                nd += 1
                dcnt[k] += o.get("inc", 16)
                o["ev"] = (dsem[k], dcnt[k], ("d", k))
                if dlast[k] is not None:
                    o["deps"] = set(o["deps"]) | {dlast[k]}
                dlast[k] = i
            elif o["sig"]:
                cnt[o["eng"]] += 1
                o["ev"] = (esem[o["eng"]], cnt[o["eng"]], ("e", o["eng"]))
        streams = {e: [] for e in self.ENGS}
        for i, o in enumerate(ops):
            streams[o["eng"]].append(i)
        final = list(final_wait_ops)

        def run(e, eng):
            known = {}
            for i in streams[e]:
                o = ops[i]
                need = {}
                for d in o["deps"]:
                    sem, val, key = ops[d]["ev"]
                    if need.get(key, (None, 0))[1] < val:
                        need[key] = (sem, val)
                for key, (sem, val) in need.items():
                    if known.get(key, 0) < val:
                        eng.wait_ge(sem, val)
                        known[key] = val
                if o["fn"] is None:
                    continue
                ins = o["fn"](eng)
                if o["dma"]:
                    ins.then_inc(o["ev"][0], o.get("inc", 16))
                elif o["sig"]:
                    ins.then_inc(o["ev"][0], 1)
            if e == "sp":
                for i in final:
                    sem, val, key = ops[i]["ev"]
                    eng.wait_ge(sem, val)

        block = es.enter_context(nc.Block())

        @block.tensor
        def _(eng):
            run("pe", eng)

        @block.vector
        def _(eng):
            run("dve", eng)

        @block.scalar
        def _(eng):
            run("act", eng)

        @block.gpsimd
        def _(eng):
            run("pool", eng)

        @block.sync
        def _(eng):
            run("sp", eng)


class SbufAlloc:
    def __init__(self, nc, limit=229344):
        self.nc = nc
        self.off = 16512
        self.limit = limit
        self.n = 0

    def mark(self):
        return self.off

    def reset(self, m):
        self.off = m

    def alloc(self, shape, dtype, name=None):
        nbytes = int(np.prod(shape[1:])) * (2 if dtype == BF16 else 4)
        nbytes = (nbytes + 63) // 64 * 64
        assert self.off + nbytes <= self.limit, ("SBUF overflow", name, self.off, nbytes)
        self.n += 1
        t = self.nc.alloc_sbuf_tensor_at("%s_%d" % (name or "t", self.n), list(shape), dtype, offset=self.off)
        self.off += nbytes
        return t


class Ctx:
    pass


def build_program(ntok=NTOK, depth=DEPTH, phases=("ffn1", "mix", "ffn2"), debug_out=False):
    nc = bass.Bass("TRN2", target_bir_lowering=False)
    g = Ctx()
    g.nc = nc
    g.ntok = ntok
    s = Sched(nc)
    g.s = s
    sb = SbufAlloc(nc)
    g.sb = sb

    def din(name, shape, dt=F32):
        return nc.dram_tensor(name, list(shape), dt, kind="ExternalInput").ap()

    g.xT = din("xT", [D, ntok])
    g.outT = nc.dram_tensor("outT", [D, ntok], F32, kind="ExternalOutput").ap()
    g.cvec = din("cvec", [128, 8])
    g.pos = din("pos", [1, ntok], I32)
    g.ada_w = din("ada_w", [DEPTH, D, 9 * D])
    g.ada_b = din("ada_b", [128, DEPTH * 72])
    g.ln_g = din("ln_g", [128, DEPTH * 3 * 8])
    g.ln_b = din("ln_b", [128, DEPTH * 3 * 8])
    g.w1 = [din("ffn1_w1", [DEPTH, D, DFF]), din("ffn2_w1", [DEPTH, D, DFF])]
    g.w3 = [din("ffn1_w3", [DEPTH, D, DFF]), din("ffn2_w3", [DEPTH, D, DFF])]
    g.w2 = [din("ffn1_w2", [DEPTH, DFF, D]), din("ffn2_w2", [DEPTH, DFF, D])]
    g.scr = [nc.dram_tensor("scr%d" % i, [D, ntok], F32).ap() for i in range(2)]
    g.mix_w_in = din("mix_w_in", [DEPTH, D, NIN])
    g.w_sw = din("w_sw", [DEPTH, D, 768])
    g.mix_w_out = din("mix_w_out", [DEPTH, D, D])
    g.w_glu = din("ssm_w_glu", [DEPTH, 256, 256])
    g.pp = din("pp", [DEPTH, 128, 52])
    g.lruw = din("lruw", [DEPTH, 2, 6, 64, 64])
    g.gn = din("gn", [DEPTH, 2, 384])
    g.srow = din("srow", [DEPTH, 256, 5, 64])
    g.cst = din("cst", [DEPTH, 128, 8, 2, 16])
    g.c_small = din("c_small", [128, 8])
    g.c_iota = din("c_iota", [128, 257])
    g.c_maskT = din("c_maskT", [128, 6, 128])
    g.c_qdec = din("c_qdec", [128, 3, 128])
    g.c_kdt = din("c_kdt", [128, 384])
    g.c_gmask = din("c_gmask", [128, 8])
    g.c_ident = din("c_ident", [128, 128])
    g.onehot = din("onehot", [128, 8])
    g.selprev = din("selprev", [128, 8])
    g.cc_in = [nc.dram_tensor("cc_in%d" % i, [128, 8 * NST], F32) for i in range(DEPTH)]
    g.cc_out = [nc.dram_tensor("cc_out%d" % i, [128, 8 * NST], F32) for i in range(DEPTH)]

    g.ones = sb.alloc([128, 128], BF16, "ones")
    g.mod = sb.alloc([128, DEPTH * 72], F32, "mod")
    g.sc1p = sb.alloc([128, DEPTH * 72], F32, "sc1p")
    g.lng = sb.alloc([128, DEPTH * 24], F32, "lng")
    g.lnb = sb.alloc([128, DEPTH * 24], F32, "lnb")
    g.cond = sb.alloc([128, 8], F32, "cond")
    g.adab = sb.alloc([128, DEPTH * 72], F32, "adab")
    g.psum = None
    base_mark = sb.mark()

    s.op("pool", lambda e: e.memset(g.ones[:], 1.0 / 1024.0), writes=["ones"])
    s.dma(lambda e: e.dma_start(out=g.cond[:], in_=g.cvec), writes=["cond"])
    s.dma(lambda e: e.dma_start(out=g.adab[:], in_=g.ada_b), writes=["adab"])
    s.dma(lambda e: e.dma_start(out=g.lng[:], in_=g.ln_g), writes=["lng"])
    s.dma(lambda e: e.dma_start(out=g.lnb[:], in_=g.ln_b), writes=["lnb"])
    s.op("act", lambda e: e.activation(out=g.cond[:], in_=g.cond[:], func=AF.Silu), reads=["cond"], writes=["cond"])

    es = ExitStack()
    g.ps = [es.enter_context(nc.psum_tensor("ps%d" % i, [128, 512], F32)) for i in range(8)]

    emit_mod(g, depth)
    s.barrier()

    src = g.xT
    nsub = 0
    for l in range(depth):
        for ph in phases:
            last = (l == depth - 1) and (ph == phases[-1])
            dst = g.outT if last else g.scr[nsub % 2]
            sb.reset(base_mark)
            if ph == "ffn1":
                emit_ffn(g, l, 0, src, dst)
            elif ph == "ffn2":
                emit_ffn(g, l, 1, src, dst)
            else:
                emit_mixer(g, l, src, dst)
            s.barrier()
            src = dst
            nsub += 1
    final = [i for i, o in enumerate(s.ops) if o["dma"] and o.get("is_out")]
    s.emit(es, final_wait_ops=final)
    es.close()
    return nc


def emit_mod(g, depth):
    s, sb, nc = g.s, g.sb, g.nc
    m = sb.mark()
    stg = [sb.alloc([128, 8, 512], F32, "adastg") for _ in range(3)]
    ps = g.ps[0]
    n = 0
    for l in range(depth):
        for piece in range(18):
            b = n % 3
            n += 1
            st = stg[b]
            s.dma(lambda e, st=st, l=l, piece=piece: e.dma_start(
                out=st[:], in_=g.ada_w[l, :, piece * 512:(piece + 1) * 512].rearrange("(k p) n -> p k n", p=128)),
                writes=["adastg%d" % b])
            for c in range(4):
                col = l * 72 + piece * 4 + c
                for k in range(8):
                    s.op("pe", lambda e, st=st, c=c, col=col, k=k: e.matmul(
                        ps[:, col:col + 1], lhsT=st[:, k, c * 128:(c + 1) * 128], rhs=g.cond[:, k:k + 1],
                        start=(k == 0), stop=(k == 7)),
                        reads=["adastg%d" % b, "cond"], writes=["modps"])
    W = depth * 72
    s.op("dve", lambda e: e.tensor_tensor(out=g.mod[:, 0:W], in0=ps[:, 0:W], in1=g.adab[:, 0:W], op=ALU.add),
         reads=["modps", "adab"], writes=["mod"])
    for l in range(depth):
        for n_ in range(9):
            c0 = l * 72 + n_ * 8
            if n_ in (1, 4, 7):
                s.op("dve", lambda e, c0=c0: e.tensor_scalar_add(out=g.sc1p[:, c0:c0 + 8], in0=g.mod[:, c0:c0 + 8], scalar1=1.0),
                     reads=["mod"], writes=["sc1p"])
            elif n_ in (2, 5, 8):
                coef = (0.5 if n_ in (2, 8) else 1.0) / ALPHA
                s.op("dve", lambda e, c0=c0, coef=coef: e.tensor_scalar_mul(out=g.sc1p[:, c0:c0 + 8], in0=g.mod[:, c0:c0 + 8], scalar1=coef),
                     reads=["mod"], writes=["sc1p"])
            else:
                s.op("dve", lambda e, c0=c0: e.tensor_copy(out=g.sc1p[:, c0:c0 + 8], in_=g.mod[:, c0:c0 + 8]),
                     reads=["mod"], writes=["sc1p"])
    sb.reset(m)


def load_cast(g, dram_ap, dst_ap, stage_tiles, stage_keys, n, dst_key, shape3=None):
    s = g.s
    b = n % len(stage_tiles)
    st = stage_tiles[b]
    view = st[:, 0:int(np.prod(dst_ap.shape[1:]))]
    if len(dst_ap.shape) == 3:
        view = view.rearrange("p (a b) -> p a b", a=dst_ap.shape[1])
    s.dma(lambda e: e.dma_start(out=view, in_=dram_ap), writes=[stage_keys[b]])
    eng = ("act", "pool", "dve")[n % 3]
    if eng == "act":
        s.op("act", lambda e: e.copy(out=dst_ap, in_=view), reads=[stage_keys[b]], writes=[dst_key])
    else:
        s.op(eng, lambda e: e.tensor_copy(out=dst_ap, in_=view), reads=[stage_keys[b]], writes=[dst_key])


def emit_resid_ln(g, l, isub, xin, xkey, acc_fn, T, tmp):
    s = g.s
    gcol = l * 72 + (isub * 3 + 2) * 8
    lcol = l * 24 + isub * 8
    ybf, ysq, mean_sb, m2, var_sb = tmp["ybf"], tmp["ysq"], tmp["mean"], tmp["m2"], tmp["var"]
    for j in range(8):
        acc, akey = acc_fn(j)
        s.op("dve", lambda e, j=j, acc=acc: e.scalar_tensor_tensor(
            out=xin[:, j, :], in0=acc, scalar=g.sc1p[:, gcol + j:gcol + j + 1], in1=xin[:, j, :],
            op0=ALU.mult, op1=ALU.add), reads=[akey, xkey + "%d" % j, "sc1p"], writes=[xkey + "%d" % j])
        s.op("pool", lambda e, j=j: e.tensor_copy(out=ybf[:, j, :], in_=xin[:, j, :]),
             reads=[xkey + "%d" % j], writes=["ybf%d" % j])
        s.op("act", lambda e, j=j: e.activation(out=ysq[:, j, :], in_=xin[:, j, :], func=AF.Square),
             reads=[xkey + "%d" % j], writes=["ysq%d" % j])
    import os
    DBG = int(os.environ.get("KDBG", "9"))
    if DBG <= 4:
        return
    pm, pe2 = g.ps[6], g.ps[7]
    for j in range(8):
        s.op("pe", lambda e, j=j: e.matmul(pm[:, 0:T], lhsT=g.ones[:], rhs=ybf[:, j, :], start=(j == 0), stop=(j == 7)),
             reads=["ybf%d" % j, "ones"], writes=["ps6"])
    for j in range(8):
        s.op("pe", lambda e, j=j: e.matmul(pe2[:, 0:T], lhsT=g.ones[:], rhs=ysq[:, j, :], start=(j == 0), stop=(j == 7)),
             reads=["ysq%d" % j, "ones"], writes=["ps7"])
    if DBG <= 5:
        return
    s.op("act", lambda e: e.copy(out=mean_sb[:], in_=pm[:, 0:T]), reads=["ps6"], writes=["mean"])
    s.op("dve", lambda e: e.tensor_tensor(out=m2[:], in0=mean_sb[:], in1=mean_sb[:], op=ALU.mult), reads=["mean"], writes=["m2"])
    s.op("dve", lambda e: e.tensor_tensor(out=var_sb[:], in0=pe2[:, 0:T], in1=m2[:], op=ALU.subtract), reads=["ps7", "m2"], writes=["var"])
    s.op("dve", lambda e: e.tensor_scalar(out=var_sb[:], in0=var_sb[:], scalar1=0.0, scalar2=EPS_LN, op0=ALU.max, op1=ALU.add),
         reads=["var"], writes=["var"])
    s.op("act", lambda e: e.activation(out=var_sb[:], in_=var_sb[:], func=AF.Sqrt), reads=["var"], writes=["var"])
    s.op("dve", lambda e: e.reciprocal(out=var_sb[:], in_=var_sb[:]), reads=["var"], writes=["var"])
    if DBG <= 6:
        return
    for j in range(8):
        k = xkey + "%d" % j
        s.op("dve", lambda e, j=j: e.tensor_tensor(out=xin[:, j, :], in0=xin[:, j, :], in1=mean_sb[:], op=ALU.subtract),
             reads=[k, "mean"], writes=[k])
        s.op("dve", lambda e, j=j: e.tensor_tensor(out=xin[:, j, :], in0=xin[:, j, :], in1=var_sb[:], op=ALU.mult),
             reads=[k, "var"], writes=[k])
        s.op("dve", lambda e, j=j: e.tensor_scalar(out=xin[:, j, :], in0=xin[:, j, :],
                                                    scalar1=g.lng[:, lcol + j:lcol + j + 1], scalar2=g.lnb[:, lcol + j:lcol + j + 1],
                                                    op0=ALU.mult, op1=ALU.add),
             reads=[k, "lng", "lnb"], writes=[k])


def dram_tile(ap, t0, T):
    return ap[:, t0:t0 + T].rearrange("(k p) t -> p k t", p=128)


def emit_ffn(g, l, which, src, dst):
    s, sb, nc = g.s, g.sb, g.nc
    T = 512
    isub = 0 if which == 0 else 2
    w1b = sb.alloc([128, 8, DFF], BF16, "w1b")
    w3b = sb.alloc([128, 8, DFF], BF16, "w3b")
    w2b = sb.alloc([128, NF, D], BF16, "w2b")
    xin = [sb.alloc([128, 8, T], F32, "xin") for _ in range(1)]
    h = sb.alloc([128, 8, T], BF16, "h")
    sil = [sb.alloc([128, T], BF16, "sil") for _ in range(2)]
    tmp = dict(mean=sb.alloc([128, T], F32, "mean"), m2=sb.alloc([128, T], F32, "m2"), var=sb.alloc([128, T], F32, "var"))
    mk = sb.mark()
    stg = [sb.alloc([128, DFF], F32, "stg") for _ in range(2)]
    skeys = ["stg0", "stg1"]
    n = 0
    for k in range(8):
        load_cast(g, g.w1[which][l, k * 128:(k + 1) * 128, :], w1b[:, k, :], stg, skeys, n, "w1b"); n += 1
        load_cast(g, g.w3[which][l, k * 128:(k + 1) * 128, :], w3b[:, k, :], stg, skeys, n, "w3b"); n += 1
    for c in range(0, NF, 2):
        load_cast(g, g.w2[which][l, c * 128:(c + 2) * 128, :].rearrange("(c p) n -> p c n", p=128),
                  w2b[:, c:c + 2, :], stg, skeys, n, "w2b"); n += 1
    s.barrier()
    import os
    DBG = int(os.environ.get("KDBG", "9"))
    if DBG <= 1:
        return
    sb.reset(mk)
    gT = sb.alloc([128, NF, T], BF16, "gT")
    tmp["ybf"] = sb.alloc([128, 8, T], BF16, "ybf")
    tmp["ysq"] = sb.alloc([128, 8, T], BF16, "ysq")
    shc = l * 72 + (isub * 3 + 0) * 8
    scc = l * 72 + (isub * 3 + 1) * 8
    ntile = g.ntok // T
    for it in range(ntile):
        t0 = it * T
        xb = 0
        xt = xin[xb]
        xkey = "xin%d_" % xb
        for j in range(8):
            s.dma(lambda e, xt=xt, t0=t0, j=j: e.dma_start(out=xt[:, j, :], in_=src[j * 128:(j + 1) * 128, t0:t0 + T]),
                  writes=[xkey + "%d" % j])
        for k in range(8):
            s.op("dve", lambda e, k=k, xt=xt: e.tensor_scalar(
                out=h[:, k, :], in0=xt[:, k, :], scalar1=g.sc1p[:, scc + k:scc + k + 1], scalar2=g.sc1p[:, shc + k:shc + k + 1],
                op0=ALU.mult, op1=ALU.add), reads=[xkey + "%d" % k, "sc1p"], writes=["h%d" % k])
        if DBG <= 2:
            continue
        for f in range(NF):
            pb = f % 2
            p1, p3 = g.ps[2 * pb], g.ps[2 * pb + 1]
            for k in range(8):
                s.op("pe", lambda e, k=k, f=f, p1=p1: e.matmul(p1[:, 0:T], lhsT=w1b[:, k, f * 128:(f + 1) * 128], rhs=h[:, k, :],
                                                              start=(k == 0), stop=(k == 7)),
                     reads=["h%d" % k, "w1b"], writes=["ps%d" % (2 * pb)])
            for k in range(8):
                s.op("pe", lambda e, k=k, f=f, p3=p3: e.matmul(p3[:, 0:T], lhsT=w3b[:, k, f * 128:(f + 1) * 128], rhs=h[:, k, :],
                                                              start=(k == 0), stop=(k == 7)),
                     reads=["h%d" % k, "w3b"], writes=["ps%d" % (2 * pb + 1)])
            sl = sil[pb]
            s.op("act", lambda e, p1=p1, sl=sl: e.activation(out=sl[:], in_=p1[:, 0:T], func=AF.Silu),
                 reads=["ps%d" % (2 * pb)], writes=["sil%d" % pb])
            s.op("dve", lambda e, f=f, p3=p3, sl=sl: e.tensor_tensor(out=gT[:, f, :], in0=p3[:, 0:T], in1=sl[:], op=ALU.mult),
                 reads=["ps%d" % (2 * pb + 1), "sil%d" % pb], writes=["gT%d" % f])

        def acc_fn(j):
            pa = g.ps[4 + j % 2]
            key = "ps%d" % (4 + j % 2)
            for f in range(NF):
                s.op("pe", lambda e, f=f, j=j, pa=pa: e.matmul(pa[:, 0:T], lhsT=w2b[:, f, j * 128:(j + 1) * 128], rhs=gT[:, f, :],
                                                              start=(f == 0), stop=(f == NF - 1)),
                     reads=["gT%d" % f, "w2b"], writes=[key])
            return pa[:, 0:T], key

        if DBG <= 3:
            continue
        emit_resid_ln(g, l, isub, xt, xkey, acc_fn, T, tmp)
        for j in range(8):
            i = s.dma(lambda e, xt=xt, t0=t0, j=j: e.dma_start(out=dst[j * 128:(j + 1) * 128, t0:t0 + T], in_=xt[:, j, :]),
                      reads=[xkey + "%d" % j])
            s.ops[i]["is_out"] = dst is g.outT


NST = 3 + 9 + 192 + 16


def bc_inner(ap2, n):
    return ap2.unsqueeze(2).to_broadcast([ap2.shape[0], ap2.shape[1], n])


def emit_sincos(g, ang, sin_out, cos_out, tmpa, tmpb, keys, sin_scale=None, eng="dve"):
    s = g.s
    ka, ks, kc, kt1, kt2 = keys
    for (shift, outp, okey, scale) in ((0.0, sin_out, ks, sin_scale), (0.25, cos_out, kc, None)):
        s.op(eng, lambda e, shift=shift: e.tensor_scalar(out=tmpa, in0=ang, scalar1=1.0 / TWO_PI, scalar2=shift,
                                                         op0=ALU.mult, op1=ALU.add), reads=[ka], writes=[kt1])
        s.op(eng, lambda e: e.tensor_scalar(out=tmpa, in0=tmpa, scalar1=MAGIC, scalar2=MAGIC, op0=ALU.add, op1=ALU.subtract),
             reads=[kt1], writes=[kt1])
        s.op(eng, lambda e: e.scalar_tensor_tensor(out=tmpb, in0=tmpa, scalar=-C1, in1=ang, op0=ALU.mult, op1=ALU.add),
             reads=[kt1, ka], writes=[kt2])
        s.op(eng, lambda e: e.scalar_tensor_tensor(out=tmpb, in0=tmpa, scalar=-C2, in1=tmpb, op0=ALU.mult, op1=ALU.add),
             reads=[kt1, kt2], writes=[kt2])
        s.op(eng, lambda e, shift=shift: e.tensor_scalar(out=tmpb, in0=tmpb, scalar1=shift * TWO_PI, scalar2=-3.1415925,
                                                         op0=ALU.add, op1=ALU.max), reads=[kt2], writes=[kt2])
        s.op(eng, lambda e: e.tensor_scalar_min(out=tmpb, in0=tmpb, scalar1=3.1415925), reads=[kt2], writes=[kt2])
        if scale is None:
            s.op("act", lambda e, outp=outp: e.activation(out=outp, in_=tmpb, func=AF.Sin), reads=[kt2], writes=[okey])
        else:
            s.op("act", lambda e, outp=outp, scale=scale: e.activation(out=outp, in_=tmpb, func=AF.Sin, scale=scale),
                 reads=[kt2], writes=[okey])


def emit_mixer(g, l, src, dst):
    s, sb, nc = g.s, g.sb, g.nc
    T = 256
    NCH = T // 128
    ntile = g.ntok // T
    A = sb.alloc
    winb = A([128, 8, NIN], BF16, "winb")
    wswb = A([128, 8, 768], BF16, "wswb")
    woutb = A([128, 8, D], BF16, "woutb")
    pp = A([128, 52], F32, "pp")
    cs = A([128, 8], F32, "cs")
    iota = A([128, 257], F32, "iota")
    maskT = A([128, 6, 128], F32, "maskT")
    qdec = A([128, 3, 128], F32, "qdec")
    kdt = A([128, 384], F32, "kdt")
    gng = A([128, 384], F32, "gng")
    gnb = A([128, 384], F32, "gnb")
    identb = A([128, 128], BF16, "identb")
    gmask = A([128, 8], F32, "gmask")
    CT = A([128, 8, 257], F32, "CT")
    ST = A([128, 8, 257], F32, "ST")
    TBre = A([128, 8, 128], BF16, "TBre")
    TBim = A([128, 8, 128], BF16, "TBim")
    TCre = A([128, 8, 128], BF16, "TCre")
    TCim = A([128, 8, 128], BF16, "TCim")
    wglub = A([128, 2, 256], BF16, "wglub")
    wabd = A([128, 3, 128], BF16, "wabd")
    wxbd = A([128, 3, 128], BF16, "wxbd")
    sp_ = A([128, 64], F32, "sp")
    cneg = A([128, 6], F32, "cneg")
    stt = A([128, 3, 64], F32, "stt")
    stbf = A([128, 3, 64], BF16, "stbf")
    lstate = A([128, 3], F32, "lstate")
    uext = A([128, 3, T + 3], F32, "uext")
    carr = A([128, 16], F32, "carr")
    stpack = A([128, NST], F32, "stpack")
    oneh = A([128, 8], F32, "oneh")
    selp = A([128, 8], F32, "selp")
    mk = sb.mark()
    stg = [A([128, NIN], F32, "stg") for _ in range(2)]
    skeys = ["stg0", "stg1"]
    n = 0
    for k in range(8):
        load_cast(g, g.mix_w_in[l, k * 128:(k + 1) * 128, :], winb[:, k, :], stg, skeys, n, "winb"); n += 1
        load_cast(g, g.w_sw[l, k * 128:(k + 1) * 128, :], wswb[:, k, :], stg, skeys, n, "wswb"); n += 1
        load_cast(g, g.mix_w_out[l, k * 128:(k + 1) * 128, :], woutb[:, k, :], stg, skeys, n, "woutb"); n += 1
    load_cast(g, g.w_glu[l].rearrange("(c p) n -> p c n", p=128), wglub[:, :, :], stg, skeys, n, "wglub"); n += 1
    s.dma(lambda e: e.dma_start(out=pp[:], in_=g.pp[l]), writes=["pp"])
    s.dma(lambda e: e.dma_start(out=cs[:], in_=g.c_small), writes=["cs"])
    s.dma(lambda e: e.dma_start(out=iota[:], in_=g.c_iota), writes=["iota"])
    s.dma(lambda e: e.dma_start(out=maskT[:], in_=g.c_maskT), writes=["maskT"])
    s.dma(lambda e: e.dma_start(out=qdec[:], in_=g.c_qdec), writes=["qdec"])
    s.dma(lambda e: e.dma_start(out=kdt[:], in_=g.c_kdt), writes=["kdt"])
    s.dma(lambda e: e.dma_start(out=gmask[:], in_=g.c_gmask), writes=["gmask"])
    s.dma(lambda e: e.dma_start(out=gng[:], in_=g.gn[l, 0:1, :].partition_broadcast(128)), writes=["gng"])
    s.dma(lambda e: e.dma_start(out=gnb[:], in_=g.gn[l, 1:2, :].partition_broadcast(128)), writes=["gnb"])
    s.dma(lambda e: e.dma_start(out=oneh[:], in_=g.onehot), writes=["oneh"])
    s.dma(lambda e: e.dma_start(out=selp[:], in_=g.selprev), writes=["selp"])
    s.op("pool", lambda e: e.memset(stpack[:], 0.0), writes=["stpack"])
    idf = A([128, 128], F32, "idf")
    s.dma(lambda e: e.dma_start(out=idf[:], in_=g.c_ident), writes=["idf"])
    s.op("dve", lambda e: e.tensor_copy(out=identb[:], in_=idf[:]), reads=["idf"], writes=["identb"])
    bdf = A([128, 2, 3, 128], F32, "bdf")
    s.op("pool", lambda e: e.memset(bdf[:], 0.0), writes=["bdf"])
    for ax in range(2):
        for hd in range(6):
            po = 64 * (hd % 2)
            s.dma(lambda e, ax=ax, hd=hd, po=po: e.dma_start(out=bdf[po:po + 64, ax, hd // 2, po:po + 64], in_=g.lruw[l, ax, hd]),
                  reads=["bdf"], writes=["bdf"])
    s.op("dve", lambda e: e.tensor_copy(out=wabd[:], in_=bdf[:, 0, :, :]), reads=["bdf"], writes=["wabd"])
    s.op("dve", lambda e: e.tensor_copy(out=wxbd[:], in_=bdf[:, 1, :, :]), reads=["bdf"], writes=["wxbd"])
    s.op("act", lambda e: e.activation(out=cneg[:, 0:3], in_=pp[:, 21:24], func=AF.Exp, scale=-1.0), reads=["pp"], writes=["cneg"])
    s.op("dve", lambda e: e.tensor_scalar_add(out=cneg[:, 0:3], in0=cneg[:, 0:3], scalar1=1.0), reads=["cneg"], writes=["cneg"])
    s.op("act", lambda e: e.activation(out=cneg[:, 0:3], in_=cneg[:, 0:3], func=AF.Ln), reads=["cneg"], writes=["cneg"])
    s.op("dve", lambda e: e.tensor_scalar_mul(out=cneg[:, 3:6], in0=cneg[:, 0:3], scalar1=-16.0), reads=["cneg"], writes=["cneg2"])
    s.op("dve", lambda e: e.tensor_scalar_mul(out=cneg[:, 0:3], in0=cneg[:, 0:3], scalar1=-8.0), reads=["cneg", "cneg2"], writes=["cneg"])
    def unpack_state():
        s.op("dve", lambda e: e.tensor_copy(out=lstate[:], in_=stpack[:, 0:3]), reads=["stpack"], writes=["lstate"])
        s.op("dve", lambda e: e.tensor_copy(out=uext[:, :, 0:3], in_=stpack[:, 3:12].rearrange("p (a b) -> p a b", a=3)),
             reads=["stpack"], writes=["uext0", "uext1", "uext2"])
        s.op("dve", lambda e: e.tensor_copy(out=stt[:], in_=stpack[:, 12:204].rearrange("p (a b) -> p a b", a=3)),
             reads=["stpack"], writes=["stt"])
        s.op("dve", lambda e: e.tensor_copy(out=stbf[:], in_=stt[:]), reads=["stt"], writes=["stbf"])
        s.op("dve", lambda e: e.tensor_copy(out=carr[:], in_=stpack[:, 204:220]), reads=["stpack"], writes=["carr"])

    def pack_state():
        s.op("dve", lambda e: e.tensor_copy(out=stpack[:, 0:3], in_=lstate[:]), reads=["lstate"], writes=["stpack"])
        s.op("dve", lambda e: e.tensor_copy(out=stpack[:, 3:12].rearrange("p (a b) -> p a b", a=3), in_=uext[:, :, 0:3]),
             reads=["uext0", "uext1", "uext2", "stpack"], writes=["stpack"])
        s.op("dve", lambda e: e.tensor_copy(out=stpack[:, 12:204].rearrange("p (a b) -> p a b", a=3), in_=stt[:]), reads=["stt", "stpack"], writes=["stpack"])
        s.op("dve", lambda e: e.tensor_copy(out=stpack[:, 204:220], in_=carr[:]), reads=["carr", "stpack"], writes=["stpack"])

    unpack_state()

    def unpack_state_zero():
        s.op("pool", lambda e: e.memset(stpack[:], 0.0), reads=["stpack"], writes=["stpack"])
        unpack_state()

    def ssm_params(lr, li, ls, W, pfx, tmp):
        t = lambda i: tmp[:, i, :]
        dt_, mag, ang, sn, cn, ta, tb, zr1, er, ei, den, tt = [t(i) for i in range(12)]
        K = lambda nm: pfx + nm
        s.op("act", lambda e: e.activation(out=dt_, in_=ls, func=AF.Exp), reads=[K("in")], writes=[K("dt")])
        s.op("dve", lambda e: e.tensor_tensor(out=mag, in0=lr, in1=dt_, op=ALU.mult), reads=[K("in"), K("dt")], writes=[K("mag")])
        s.op("act", lambda e: e.activation(out=mag, in_=mag, func=AF.Exp), reads=[K("mag")], writes=[K("mag")])
        s.op("dve", lambda e: e.tensor_tensor(out=ang, in0=li, in1=dt_, op=ALU.mult), reads=[K("in"), K("dt")], writes=[K("ang")])
        emit_sincos(g, ang, sn, cn, ta, tb, (K("ang"), K("sn"), K("cn"), K("ta"), K("tb")))
        s.op("dve", lambda e: e.tensor_tensor(out=zr1, in0=mag, in1=cn, op=ALU.mult), reads=[K("mag"), K("cn")], writes=[K("zr1")])
        s.op("dve", lambda e: e.tensor_scalar_add(out=zr1, in0=zr1, scalar1=-1.0), reads=[K("zr1")], writes=[K("zr1")])
        s.op("dve", lambda e: e.tensor_tensor(out=tt, in0=mag, in1=sn, op=ALU.mult), reads=[K("mag"), K("sn")], writes=[K("zi")])
        s.op("dve", lambda e: e.tensor_tensor(out=den, in0=lr, in1=lr, op=ALU.mult), reads=[K("in")], writes=[K("den")])
        s.op("dve", lambda e: e.tensor_tensor(out=ta, in0=li, in1=li, op=ALU.mult), reads=[K("in"), K("ta")], writes=[K("ta")])
        s.op("dve", lambda e: e.tensor_tensor(out=den, in0=den, in1=ta, op=ALU.add), reads=[K("den"), K("ta")], writes=[K("den")])
        s.op("dve", lambda e: e.reciprocal(out=den, in_=den), reads=[K("den")], writes=[K("den")])
        s.op("dve", lambda e: e.tensor_tensor(out=er, in0=zr1, in1=lr, op=ALU.mult), reads=[K("zr1"), K("in")], writes=[K("er")])
        s.op("dve", lambda e: e.tensor_tensor(out=ta, in0=tt, in1=li, op=ALU.mult), reads=[K("zi"), K("in"), K("ta")], writes=[K("ta")])
        s.op("dve", lambda e: e.tensor_tensor(out=er, in0=er, in1=ta, op=ALU.add), reads=[K("er"), K("ta")], writes=[K("er")])
        s.op("dve", lambda e: e.tensor_tensor(out=er, in0=er, in1=den, op=ALU.mult), reads=[K("er"), K("den")], writes=[K("er")])
        s.op("dve", lambda e: e.tensor_tensor(out=ei, in0=tt, in1=lr, op=ALU.mult), reads=[K("zi"), K("in")], writes=[K("ei")])
        s.op("dve", lambda e: e.tensor_tensor(out=tb, in0=zr1, in1=li, op=ALU.mult), reads=[K("zr1"), K("in"), K("tb")], writes=[K("tb")])
        s.op("dve", lambda e: e.tensor_tensor(out=ei, in0=ei, in1=tb, op=ALU.subtract), reads=[K("ei"), K("tb")], writes=[K("ei")])
        s.op("dve", lambda e: e.tensor_tensor(out=ei, in0=ei, in1=den, op=ALU.mult), reads=[K("ei"), K("den")], writes=[K("ei")])
        return dict(mag=mag, ang=ang, er=er, ei=ei)

    sptmp = A([128, 12, 8], F32, "sptmp")
    s.op("dve", lambda e: e.tensor_copy(out=sp_[:, 0:24], in_=pp[:, 28:52]), reads=["pp"], writes=["S_in"])
    P = ssm_params(sp_[:, 0:8], sp_[:, 8:16], sp_[:, 16:24], 8, "S_", sptmp)
    rho = sp_[:, 48:56]
    s.op("dve", lambda e: e.tensor_copy(out=rho, in_=P["mag"]), reads=["S_mag"], writes=["rho"])
    th = sp_[:, 24:32]
    s.op("dve", lambda e: e.tensor_scalar(out=sp_[:, 32:40], in0=P["ang"], scalar1=1.0 / TWO_PI, scalar2=MAGIC, op0=ALU.mult, op1=ALU.add),
         reads=["S_ang"], writes=["S_k"])
    s.op("dve", lambda e: e.tensor_scalar_add(out=sp_[:, 32:40], in0=sp_[:, 32:40], scalar1=-MAGIC), reads=["S_k"], writes=["S_k"])
    s.op("dve", lambda e: e.scalar_tensor_tensor(out=th, in0=sp_[:, 32:40], scalar=-C1, in1=P["ang"], op0=ALU.mult, op1=ALU.add),
         reads=["S_k", "S_ang"], writes=["S_th"])
    s.op("dve", lambda e: e.scalar_tensor_tensor(out=th, in0=sp_[:, 32:40], scalar=-C2, in1=th, op0=ALU.mult, op1=ALU.add),
         reads=["S_k", "S_th"], writes=["S_th"])
    angt = A([128, 257], F32, "angt")
    tta = A([128, 257], F32, "tta")
    ttb = A([128, 257], F32, "ttb")
    for p in range(8):
        s.op("dve", lambda e, p=p: e.tensor_scalar_mul(out=angt[:], in0=iota[:], scalar1=th[:, p:p + 1]),
             reads=["iota", "S_th"], writes=["angt"])
        emit_sincos(g, angt[:], ST[:, p, :], CT[:, p, :], tta[:], ttb[:], ("angt", "ST%d" % p, "CT%d" % p, "tta", "ttb"))
    nST = sp_[:, 40:48]
    s.op("dve", lambda e: e.tensor_scalar_mul(out=nST, in0=ST[:, :, 256], scalar1=-1.0), reads=["ST%d" % p for p in range(8)], writes=["nST"])
    srow = A([128, 2, 5, 64], F32, "srow")
    s.dma(lambda e: e.dma_start(out=srow[:], in_=g.srow[l].rearrange("(c p) a b -> p c a b", p=128)), writes=["R_in"])
    rtmp = A([128, 12, 128], F32, "rtmp")
    rin = A([128, 3, 128], F32, "rin")
    for a_ in range(3):
        s.op("dve", lambda e, a_=a_: e.tensor_copy(out=rin[:, a_, :].rearrange("p (c b) -> p c b", c=2), in_=srow[:, :, a_, :]),
             reads=["R_in"], writes=["R_in2"])
    s.op("dve", lambda e: e.tensor_copy(out=rin[:, 0, 0:1], in_=rin[:, 0, 0:1]), reads=["R_in2"], writes=["R_in"])
    R = ssm_params(rin[:, 0, :], rin[:, 1, :], rin[:, 2, :], 128, "R_", rtmp)
    bbr = A([128, 2, 64], F32, "bbr")
    bbi = A([128, 2, 64], F32, "bbi")
    bt1 = A([128, 2, 64], F32, "bt1")
    er2 = R["er"].rearrange("p (c b) -> p c b", c=2)
    ei2 = R["ei"].rearrange("p (c b) -> p c b", c=2)
    s.op("dve", lambda e: e.tensor_tensor(out=bbr[:], in0=er2, in1=srow[:, :, 3, :], op=ALU.mult), reads=["R_er", "R_in"], writes=["bbr"])
    s.op("dve", lambda e: e.tensor_tensor(out=bt1[:], in0=ei2, in1=srow[:, :, 4, :], op=ALU.mult), reads=["R_ei", "R_in"], writes=["bt1"])
    s.op("dve", lambda e: e.tensor_tensor(out=bbr[:], in0=bbr[:], in1=bt1[:], op=ALU.subtract), reads=["bbr", "bt1"], writes=["bbr"])
    s.op("dve", lambda e: e.tensor_tensor(out=bbi[:], in0=er2, in1=srow[:, :, 4, :], op=ALU.mult), reads=["R_er", "R_in"], writes=["bbi"])
    s.op("dve", lambda e: e.tensor_tensor(out=bt1[:], in0=ei2, in1=srow[:, :, 3, :], op=ALU.mult), reads=["R_ei", "R_in", "bbr"], writes=["bt1"])
    s.op("dve", lambda e: e.tensor_tensor(out=bbi[:], in0=bbi[:], in1=bt1[:], op=ALU.add), reads=["bbi", "bt1"], writes=["bbi"])
    for p in range(8):
        ct, q = p // 4, p % 4
        for gl in range(2):
            mcol = 2 * q + gl
            s.op("dve", lambda e, p=p, ct=ct, gl=gl, mcol=mcol: e.tensor_scalar_mul(
                out=TBre[:, p, 64 * gl:64 * gl + 64], in0=bbr[:, ct, :], scalar1=gmask[:, mcol:mcol + 1]),
                reads=["bbr", "gmask"], writes=["TBre"])
            s.op("dve", lambda e, p=p, ct=ct, gl=gl, mcol=mcol: e.tensor_scalar_mul(
                out=TBim[:, p, 64 * gl:64 * gl + 64], in0=bbi[:, ct, :], scalar1=gmask[:, mcol:mcol + 1]),
                reads=["bbi", "gmask"], writes=["TBim"])
    cstt = A([128, 8, 2, 16], F32, "cstt")
    s.dma(lambda e: e.dma_start(out=cstt[:], in_=g.cst[l]), writes=["cstt"])
    s.op("pool", lambda e: e.memset(TCre[:], 0.0), writes=["TCre"])
    s.op("pool", lambda e: e.memset(TCim[:], 0.0), writes=["TCim"])
    for p in range(8):
        q = p % 4
        for gl in range(2):
            r0 = 64 * gl
            c0 = 32 * q + 16 * gl
            s.op("dve", lambda e, p=p, r0=r0, c0=c0: e.tensor_copy(out=TCre[r0:r0 + 64, p, c0:c0 + 16], in_=cstt[r0:r0 + 64, p, 0, :]),
                 reads=["cstt", "TCre"], writes=["TCre"])
            s.op("dve", lambda e, p=p, r0=r0, c0=c0: e.tensor_scalar_mul(out=TCim[r0:r0 + 64, p, c0:c0 + 16], in0=cstt[r0:r0 + 64, p, 1, :], scalar1=-1.0),
                 reads=["cstt", "TCim"], writes=["TCim"])
    s.barrier()
    sb.reset(mk)

    xin = A([128, 8, T], F32, "xin")
    h = A([128, 8, T], BF16, "h")
    ymix = A([128, 8, T], BF16, "ymix")
    posi = A([128, T], I32, "posi")
    posf = A([128, T], F32, "posf")
    rsn = A([128, T], F32, "rsn")
    rcs = A([128, T], F32, "rcs")
    rta = A([128, T], F32, "rta")
    rtb = A([128, T], F32, "rtb")
    Lc = A([128, 3, T], F32, "Lc")
    Lcb = A([128, 3, T], BF16, "Lcb")
    Lr = A([128, 3, T], F32, "Lr")
    Li = A([128, 3, T], F32, "Li")
    La = A([128, 3, T], F32, "La")
    La2 = A([128, 3, T], F32, "La2")
    Lg = A([128, 3, T], F32, "Lg")
    qr = A([128, 3, T], BF16, "qr")
    kr = A([128, 3, T], BF16, "kr")
    qd = A([128, 3, T], BF16, "qd")
    vtm = A([128, NCH, 384], BF16, "vtm")
    gsl = A([128, NCH, 384], F32, "gsl")
    ktm = A([128, 384], BF16, "ktm")
    sm = A([128, 6, 128], BF16, "sm")
    osb = A([128, 384], F32, "osb")
    osq = A([128, 384], F32, "osq")
    gst = A([128, 4, 6], F32, "gst")
    yret = A([128, 384], BF16, "yret")
    uf = A([128, 2, T], F32, "uf")
    ubf = A([128, 2, T], BF16, "ubf")
    t1 = A([128, T], F32, "t1")
    t2 = A([128, T], F32, "t2")
    Vr = A([128, T], F32, "Vr")
    Vi = A([128, T], F32, "Vi")
    Wr = A([128, T], F32, "Wr")
    Wi = A([128, T], F32, "Wi")
    Xr = A([128, 8, T], BF16, "Xr")
    Xi = A([128, 8, T], BF16, "Xi")
    ygf = A([128, 2, T], F32, "ygf")
    ygb = A([128, 2, T], BF16, "ygb")
    sgl = A([128, T], F32, "sgl")
    ctmp = A([128, 4], F32, "ctmp")
    tmp = dict(mean=A([128, T], F32, "mean"), m2=A([128, T], F32, "m2"), var=A([128, T], F32, "var"),
               ybf=A([128, 8, T], BF16, "ybf"), ysq=A([128, 8, T], BF16, "ysq"))
    shc = l * 72 + 3 * 8
    scc = l * 72 + 4 * 8
    pbn = [0]

    def pbank():
        b = pbn[0] % 5
        pbn[0] += 1
        return g.ps[b], "ps%d" % b

    def proj(wt, c0, ncol=128):
        ps, key = pbank()
        for k in range(8):
            s.op("pe", lambda e, k=k: e.matmul(ps[:, 0:T], lhsT=wt[:, k, c0:c0 + 128], rhs=h[:, k, :], start=(k == 0), stop=(k == 7)),
                 reads=["h%d" % k, "wts"], writes=[key])
        return ps[:, 0:T], key

    def run_tile(it, full):
        t0 = it * T
        s.dma(lambda e, t0=t0: e.dma_start(out=xin[:], in_=dram_tile(src, t0, T)), writes=["xm%d" % j for j in range(8)])
        s.dma(lambda e, t0=t0: e.dma_start(out=posi[:], in_=g.pos[0:1, t0:t0 + T].partition_broadcast(128)), writes=["posi"])
        for k in range(8):
            s.op("dve", lambda e, k=k: e.tensor_scalar(
                out=h[:, k, :], in0=xin[:, k, :], scalar1=g.sc1p[:, scc + k:scc + k + 1], scalar2=g.sc1p[:, shc + k:shc + k + 1],
                op0=ALU.mult, op1=ALU.add), reads=["xm%d" % k, "sc1p"], writes=["h%d" % k])
        s.op("dve", lambda e: e.tensor_copy(out=posf[:], in_=posi[:]), reads=["posi"], writes=["posf"])
        s.op("dve", lambda e: e.tensor_scalar_mul(out=posf[:], in0=posf[:], scalar1=cs[:, 0:1]), reads=["posf", "cs"], writes=["posf"])
        emit_sincos(g, posf[:], rsn[:], rcs[:], rta[:], rtb[:], ("posf", "rsn", "rcs", "rta", "rtb"), sin_scale=cs[:, 1:2])

        for i in range(3):
            ups, ukey = proj(winb, i * 128)
            s.op("act", lambda e, i=i, ups=ups: e.copy(out=uext[:, i, 3:3 + T], in_=ups), reads=[ukey], writes=["uext%d" % i])
            s.op("dve", lambda e, i=i: e.tensor_scalar(out=Lc[:, i, :], in0=uext[:, i, 3:3 + T], scalar1=pp[:, i * 4 + 3:i * 4 + 4],
                                                       scalar2=pp[:, 12 + i:13 + i], op0=ALU.mult, op1=ALU.add),
                 reads=["uext%d" % i, "pp"], writes=["Lc%d" % i])
            for kk in range(3):
                s.op("dve", lambda e, i=i, kk=kk: e.scalar_tensor_tensor(
                    out=Lc[:, i, :], in0=uext[:, i, kk:kk + T], scalar=pp[:, i * 4 + kk:i * 4 + kk + 1], in1=Lc[:, i, :],
                    op0=ALU.mult, op1=ALU.add), reads=["uext%d" % i, "pp", "Lc%d" % i], writes=["Lc%d" % i])
            s.op("pool", lambda e, i=i: e.tensor_copy(out=uext[:, i, 0:3], in_=uext[:, i, T:T + 3]), reads=["uext%d" % i], writes=["uext%d" % i])
            s.op("pool", lambda e, i=i: e.tensor_copy(out=Lcb[:, i, :], in_=Lc[:, i, :]), reads=["Lc%d" % i], writes=["Lcb%d" % i])
        for i in range(3):
            pa, ka = pbank()
            s.op("pe", lambda e, i=i, pa=pa: e.matmul(pa[:, 0:T], lhsT=wabd[:, i, :], rhs=Lcb[:, i, :], start=True, stop=True),
                 reads=["Lcb%d" % i, "wts"], writes=[ka])
            s.op("act", lambda e, i=i, pa=pa: e.activation(out=Lr[:, i, :], in_=pa[:, 0:T], func=AF.Sigmoid, bias=pp[:, 15 + i:16 + i]),
                 reads=[ka, "pp"], writes=["Lr%d" % i])
            px, kx = pbank()
            s.op("pe", lambda e, i=i, px=px: e.matmul(px[:, 0:T], lhsT=wxbd[:, i, :], rhs=Lcb[:, i, :], start=True, stop=True),
                 reads=["Lcb%d" % i, "wts"], writes=[kx])
            s.op("act", lambda e, i=i, px=px: e.activation(out=Li[:, i, :], in_=px[:, 0:T], func=AF.Sigmoid, bias=pp[:, 18 + i:19 + i]),
                 reads=[kx, "pp"], writes=["Li%d" % i])
        for i in range(3):
            s.op("act", lambda e, i=i: e.activation(out=La[:, i, :], in_=Lr[:, i, :], func=AF.Exp, scale=cneg[:, i:i + 1]),
                 reads=["Lr%d" % i, "cneg"], writes=["La%d" % i])
            s.op("act", lambda e, i=i: e.activation(out=La2[:, i, :], in_=Lr[:, i, :], func=AF.Exp, scale=cneg[:, 3 + i:4 + i]),
                 reads=["Lr%d" % i, "cneg2"], writes=["La2%d" % i])
        for i in range(3):
            s.op("dve", lambda e, i=i: e.tensor_scalar(out=La2[:, i, :], in0=La2[:, i, :], scalar1=-1.0, scalar2=1.0, op0=ALU.mult, op1=ALU.add),
                 reads=["La2%d" % i], writes=["La2%d" % i])
            s.op("dve", lambda e, i=i: e.tensor_scalar_max(out=La2[:, i, :], in0=La2[:, i, :], scalar1=0.0), reads=["La2%d" % i], writes=["La2%d" % i])
            s.op("act", lambda e, i=i: e.activation(out=La2[:, i, :], in_=La2[:, i, :], func=AF.Sqrt), reads=["La2%d" % i], writes=["La2%d" % i])
        for i in range(3):
            s.op("dve", lambda e, i=i: e.tensor_tensor(out=Li[:, i, :], in0=Li[:, i, :], in1=Lc[:, i, :], op=ALU.mult),
                 reads=["Li%d" % i, "Lc%d" % i], writes=["Li%d" % i])
            s.op("dve", lambda e, i=i: e.tensor_tensor(out=Li[:, i, :], in0=Li[:, i, :], in1=La2[:, i, :], op=ALU.mult),
                 reads=["Li%d" % i, "La2%d" % i], writes=["Li%d" % i])
            s.op("dve", lambda e, i=i: e.tensor_tensor_scan(out=Lr[:, i, :], data0=La[:, i, :], data1=Li[:, i, :], initial=lstate[:, i:i + 1],
                                                            op0=ALU.mult, op1=ALU.add),
                 reads=["La%d" % i, "Li%d" % i, "lstate", "Lr%d" % i], writes=["Lr%d" % i])
            s.op("dve", lambda e, i=i: e.tensor_copy(out=lstate[:, i:i + 1], in_=Lr[:, i, T - 1:T]), reads=["Lr%d" % i, "lstate"], writes=["lstate"])
        for i in range(3 if full else 0):
            gps, gkey = proj(winb, 384 + i * 128)
            s.op("act", lambda e, i=i, gps=gps: e.activation(out=Lg[:, i, :], in_=gps, func=AF.Gelu_apprx_tanh), reads=[gkey], writes=["Lg%d" % i])
            s.op("dve", lambda e, i=i: e.tensor_tensor(out=ymix[:, i, :], in0=Lr[:, i, :], in1=Lg[:, i, :], op=ALU.mult),
                 reads=["Lr%d" % i, "Lg%d" % i], writes=["ym%d" % i])

        for (dstt, c0, nm) in (((qr, 768, "qr"), (kr, 1152, "kr")) if full else ((kr, 1152, "kr"),)):
            for i in range(3):
                p1, k1 = proj(winb, c0 + i * 128)
                s.op("dve", lambda e, p1=p1: e.tensor_tensor(out=t1[:], in0=p1, in1=rcs[:], op=ALU.mult), reads=[k1, "rcs"], writes=["t1"])
                p2, k2 = proj(wswb, (0 if nm == "qr" else 384) + i * 128)
                s.op("dve", lambda e, p2=p2: e.tensor_tensor(out=t2[:], in0=p2, in1=rsn[:], op=ALU.mult), reads=[k2, "rsn"], writes=["t2"])
                s.op("pool", lambda e, i=i, dstt=dstt: e.tensor_tensor(out=dstt[:, i, :], in0=t1[:], in1=t2[:], op=ALU.add),
                     reads=["t1", "t2"], writes=["%s%d" % (nm, i)])
        for i in range(3 if full else 0):
            for c in range(NCH):
                s.op("pool", lambda e, i=i, c=c: e.tensor_tensor(out=qd[:, i, c * 128:(c + 1) * 128], in0=qr[:, i, c * 128:(c + 1) * 128],
                                                                 in1=qdec[:, i, :], op=ALU.mult),
                     reads=["qr%d" % i, "qdec"], writes=["qd%d" % i])
        for c in range(NCH):
            for (c0, nm) in (((1536, "v"), (1920, "g")) if full else ((1536, "v"),)):
                ps, key = pbank()
                for k in range(8):
                    s.op("pe", lambda e, k=k, c=c, c0=c0, ps=ps: e.matmul(ps[:, 0:384], lhsT=h[:, k, c * 128:(c + 1) * 128],
                                                                         rhs=winb[:, k, c0:c0 + 384], start=(k == 0), stop=(k == 7)),
                         reads=["h%d" % k, "wts"], writes=[key])
                if nm == "v":
                    s.op("act", lambda e, c=c, ps=ps: e.copy(out=vtm[:, c, :], in_=ps[:, 0:384]), reads=[key], writes=["vtm%d" % c])
                else:
                    s.op("act", lambda e, c=c, ps=ps: e.activation(out=gsl[:, c, :], in_=ps[:, 0:384], func=AF.Silu), reads=[key], writes=["gsl%d" % c])
        for c in range(NCH):
            cs_ = slice(c * 128, (c + 1) * 128)
            pk, kk_ = pbank()
            for i in range(3):
                s.op("pe", lambda e, i=i, pk=pk, cs_=cs_: e.matmul(pk[:, i * 128:(i + 1) * 128], lhsT=kr[:, i, cs_], rhs=identb[:], start=True, stop=True),
                     reads=["kr%d" % i, "identb"], writes=[kk_])
            s.op("dve", lambda e, pk=pk: e.tensor_tensor(out=ktm[:], in0=pk[:, 0:384], in1=kdt[:], op=ALU.mult), reads=[kk_, "kdt"], writes=["ktm"])
            po_, ko_ = g.ps[5], "ps5"
            for hd in range(6 if full else 0):
                i, po = hd // 2, 64 * (hd % 2)
                psc, ksc = pbank()
                s.op("pe", lambda e, i=i, po=po, psc=psc, cs_=cs_: e.matmul(psc[:, 0:128], lhsT=kr[po:po + 64, i, cs_], rhs=qr[po:po + 64, i, cs_],
                                                                           start=True, stop=True),
                     reads=["kr%d" % i, "qr%d" % i], writes=[ksc])
                s.op("dve", lambda e, hd=hd, psc=psc: e.tensor_tensor(out=sm[:, hd, :], in0=psc[:, 0:128], in1=maskT[:, hd, :], op=ALU.mult),
                     reads=[ksc, "maskT"], writes=["sm%d" % hd])
                s.op("pe", lambda e, hd=hd, c=c, po_=po_: e.matmul(po_[:, hd * 64:(hd + 1) * 64], lhsT=sm[:, hd, :], rhs=vtm[:, c, hd * 64:(hd + 1) * 64],
                                                                  start=True, stop=False),
                     reads=["sm%d" % hd, "vtm%d" % c], writes=[ko_])
                s.op("pe", lambda e, hd=hd, i=i, po=po, po_=po_, cs_=cs_: e.matmul(po_[:, hd * 64:(hd + 1) * 64], lhsT=qd[po:po + 64, i, cs_],
                                                                                  rhs=stbf[po:po + 64, i, :], start=False, stop=True),
                     reads=["qd%d" % i, "stbf"], writes=[ko_])
            for i in range(3):
                pkv, kkv = pbank()
                s.op("pe", lambda e, i=i, c=c, pkv=pkv: e.matmul(pkv[:, 0:128], lhsT=ktm[:, i * 128:(i + 1) * 128], rhs=vtm[:, c, i * 128:(i + 1) * 128],
                                                                start=True, stop=True),
                     reads=["ktm", "vtm%d" % c], writes=[kkv])
                for hh in range(2):
                    hd = 2 * i + hh
                    po = 64 * hh
                    cdv = float(np.exp(128.0 * np.log1p(-np.exp2(-5.0 - hd))))
                    s.op("dve", lambda e, i=i, po=po, pkv=pkv, cdv=cdv: e.scalar_tensor_tensor(
                        out=stt[po:po + 64, i, :], in0=stt[po:po + 64, i, :], scalar=cdv, in1=pkv[po:po + 64, po:po + 64],
                        op0=ALU.mult, op1=ALU.add), reads=[kkv, "stt", "stbf"], writes=["stt"])
            s.op("pool", lambda e: e.tensor_copy(out=stbf[:], in_=stt[:]), reads=["stt"], writes=["stbf"])
            if not full:
                continue
            s.op("act", lambda e, po_=po_: e.copy(out=osb[:], in_=po_[:, 0:384]), reads=[ko_], writes=["osb"])
            s.op("act", lambda e, po_=po_: e.activation(out=osq[:], in_=po_[:, 0:384], func=AF.Square), reads=[ko_], writes=["osq"])
            s.op("dve", lambda e: e.tensor_reduce(out=gst[:, 0, :], in_=osb[:].rearrange("p (a b) -> p a b", a=6), axis=AX.X, op=ALU.add),
                 reads=["osb"], writes=["gst0"])
            s.op("dve", lambda e: e.tensor_reduce(out=gst[:, 1, :], in_=osq[:].rearrange("p (a b) -> p a b", a=6), axis=AX.X, op=ALU.add),
                 reads=["osq"], writes=["gst1"])
            s.op("dve", lambda e: e.tensor_scalar_mul(out=gst[:, 0, :], in0=gst[:, 0, :], scalar1=1.0 / 64), reads=["gst0"], writes=["gst0"])
            s.op("dve", lambda e: e.tensor_tensor(out=gst[:, 2, :], in0=gst[:, 0, :], in1=gst[:, 0, :], op=ALU.mult), reads=["gst0"], writes=["gst2"])
            s.op("dve", lambda e: e.scalar_tensor_tensor(out=gst[:, 1, :], in0=gst[:, 1, :], scalar=1.0 / 64, in1=gst[:, 2, :], op0=ALU.mult, op1=ALU.subtract),
                 reads=["gst1", "gst2"], writes=["gst1"])
            s.op("dve", lambda e: e.tensor_scalar(out=gst[:, 1, :], in0=gst[:, 1, :], scalar1=0.0, scalar2=1e-5, op0=ALU.max, op1=ALU.add),
                 reads=["gst1"], writes=["gst1"])
            s.op("act", lambda e: e.activation(out=gst[:, 1, :], in_=gst[:, 1, :], func=AF.Sqrt), reads=["gst1"], writes=["gst1"])
            s.op("dve", lambda e: e.reciprocal(out=gst[:, 1, :], in_=gst[:, 1, :]), reads=["gst1"], writes=["gst1"])
            o3 = osb[:].rearrange("p (a b) -> p a b", a=6)
            s.op("dve", lambda e: e.tensor_tensor(out=o3, in0=o3, in1=bc_inner(gst[:, 0, :], 64), op=ALU.subtract), reads=["osb", "gst0"], writes=["osb"])
            s.op("dve", lambda e: e.tensor_tensor(out=o3, in0=o3, in1=bc_inner(gst[:, 1, :], 64), op=ALU.mult), reads=["osb", "gst1"], writes=["osb"])
            s.op("pool", lambda e: e.tensor_tensor(out=osb[:], in0=osb[:], in1=gng[:], op=ALU.mult), reads=["osb", "gng"], writes=["osb"])
            s.op("pool", lambda e: e.tensor_tensor(out=osb[:], in0=osb[:], in1=gnb[:], op=ALU.add), reads=["osb", "gnb"], writes=["osb"])
            s.op("dve", lambda e, c=c: e.tensor_tensor(out=yret[:], in0=osb[:], in1=gsl[:, c, :], op=ALU.mult), reads=["osb", "gsl%d" % c], writes=["yret"])
            for i in range(3):
                pt, kt = pbank()
                s.op("pe", lambda e, i=i, pt=pt: e.matmul(pt[:, 0:128], lhsT=yret[:, i * 128:(i + 1) * 128], rhs=identb[:], start=True, stop=True),
                     reads=["yret", "identb"], writes=[kt])
                s.op("act", lambda e, i=i, pt=pt, cs_=cs_: e.copy(out=ymix[:, 3 + i, cs_], in_=pt[:, 0:128]), reads=[kt], writes=["ym%d" % (3 + i)])

        for ct in range(2):
            ups, ukey = proj(winb, 2304 + ct * 128)
            s.op("act", lambda e, ct=ct, ups=ups: e.copy(out=uf[:, ct, :], in_=ups), reads=[ukey], writes=["uf%d" % ct])
            s.op("pool", lambda e, ct=ct: e.tensor_copy(out=ubf[:, ct, :], in_=uf[:, ct, :]), reads=["uf%d" % ct], writes=["ubf%d" % ct])
        for p in range(8):
            ct = p // 4
            pr, kr_ = pbank()
            s.op("pe", lambda e, p=p, ct=ct, pr=pr: e.matmul(pr[:, 0:T], lhsT=TBre[:, p, :], rhs=ubf[:, ct, :], start=True, stop=True),
                 reads=["ubf%d" % ct, "wts"], writes=[kr_])
            pi_, ki_ = pbank()
            s.op("pe", lambda e, p=p, ct=ct, pi_=pi_: e.matmul(pi_[:, 0:T], lhsT=TBim[:, p, :], rhs=ubf[:, ct, :], start=True, stop=True),
                 reads=["ubf%d" % ct, "wts"], writes=[ki_])
            Cp, Sp = CT[:, p, 0:T], ST[:, p, 0:T]
            s.op("dve", lambda e, pr=pr, Cp=Cp: e.tensor_tensor(out=t1[:], in0=pr[:, 0:T], in1=Cp, op=ALU.mult), reads=[kr_], writes=["t1"])
            s.op("dve", lambda e, pi_=pi_, Sp=Sp: e.tensor_tensor(out=t2[:], in0=pi_[:, 0:T], in1=Sp, op=ALU.mult), reads=[ki_], writes=["t2"])
            s.op("pool", lambda e: e.tensor_tensor(out=Vr[:], in0=t1[:], in1=t2[:], op=ALU.add), reads=["t1", "t2"], writes=["Vr"])
            s.op("dve", lambda e, pi_=pi_, Cp=Cp: e.tensor_tensor(out=t1[:], in0=pi_[:, 0:T], in1=Cp, op=ALU.mult), reads=[ki_, "t1"], writes=["t1"])
            s.op("dve", lambda e, pr=pr, Sp=Sp: e.tensor_tensor(out=t2[:], in0=pr[:, 0:T], in1=Sp, op=ALU.mult), reads=[kr_, "t2"], writes=["t2"])
            s.op("pool", lambda e: e.tensor_tensor(out=Vi[:], in0=t1[:], in1=t2[:], op=ALU.subtract), reads=["t1", "t2"], writes=["Vi"])
            rb = rho[:, p:p + 1].to_broadcast([128, T])
            s.op("dve", lambda e, p=p, rb=rb: e.tensor_tensor_scan(out=Wr[:], data0=rb, data1=Vr[:], initial=carr[:, p:p + 1], op0=ALU.mult, op1=ALU.add),
                 reads=["Vr", "carr", "Wr"], writes=["Wr"])
            s.op("dve", lambda e, p=p, rb=rb: e.tensor_tensor_scan(out=Wi[:], data0=rb, data1=Vi[:], initial=carr[:, 8 + p:9 + p], op0=ALU.mult, op1=ALU.add),
                 reads=["Vi", "carr", "Wi"], writes=["Wi"])
            s.op("pool", lambda e, p=p: e.tensor_scalar_mul(out=ctmp[:, 0:1], in0=Wi[:, T - 1:T], scalar1=nST[:, p:p + 1]), reads=["Wi", "nST"], writes=["ctmp0"])
            s.op("dve", lambda e, p=p: e.scalar_tensor_tensor(out=carr[:, p:p + 1], in0=Wr[:, T - 1:T], scalar=CT[:, p, T:T + 1], in1=ctmp[:, 0:1],
                                                             op0=ALU.mult, op1=ALU.add), reads=["Wr", "ctmp0", "carr"], writes=["carr"])
            s.op("pool", lambda e, p=p: e.tensor_scalar_mul(out=ctmp[:, 1:2], in0=Wr[:, T - 1:T], scalar1=ST[:, p, T:T + 1]), reads=["Wr"], writes=["ctmp1"])
            s.op("dve", lambda e, p=p: e.scalar_tensor_tensor(out=carr[:, 8 + p:9 + p], in0=Wi[:, T - 1:T], scalar=CT[:, p, T:T + 1], in1=ctmp[:, 1:2],
                                                             op0=ALU.mult, op1=ALU.add), reads=["Wi", "ctmp1", "carr"], writes=["carr"])
            if not full:
                continue
            s.op("dve", lambda e, Cp=Cp: e.tensor_tensor(out=t1[:], in0=Wr[:], in1=Cp, op=ALU.mult), reads=["Wr", "t1"], writes=["t1"])
            s.op("pool", lambda e, Sp=Sp: e.tensor_tensor(out=t2[:], in0=Wi[:], in1=Sp, op=ALU.mult), reads=["Wi", "t2"], writes=["t2"])
            s.op("pool", lambda e, p=p: e.tensor_tensor(out=Xr[:, p, :], in0=t1[:], in1=t2[:], op=ALU.subtract), reads=["t1", "t2"], writes=["Xr%d" % p])
            s.op("dve", lambda e, Cp=Cp: e.tensor_tensor(out=t1[:], in0=Wi[:], in1=Cp, op=ALU.mult), reads=["Wi", "t1"], writes=["t1"])
            s.op("pool", lambda e, Sp=Sp: e.tensor_tensor(out=t2[:], in0=Wr[:], in1=Sp, op=ALU.mult), reads=["Wr", "t2"], writes=["t2"])
            s.op("pool", lambda e, p=p: e.tensor_tensor(out=Xi[:, p, :], in0=t1[:], in1=t2[:], op=ALU.add), reads=["t1", "t2"], writes=["Xi%d" % p])
        if not full:
            return
        for ct in range(2):
            py, ky = pbank()
            for q in range(4):
                p = ct * 4 + q
                s.op("pe", lambda e, p=p, q=q, py=py: e.matmul(py[:, 0:T], lhsT=TCre[:, p, :], rhs=Xr[:, p, :], start=(q == 0), stop=False),
                     reads=["Xr%d" % p, "wts"], writes=[ky])
                s.op("pe", lambda e, p=p, q=q, py=py: e.matmul(py[:, 0:T], lhsT=TCim[:, p, :], rhs=Xi[:, p, :], start=False, stop=(q == 3)),
                     reads=["Xi%d" % p, "wts"], writes=[ky])
            s.op("dve", lambda e, ct=ct, py=py: e.scalar_tensor_tensor(out=ygf[:, ct, :], in0=uf[:, ct, :], scalar=pp[:, 24 + ct:25 + ct], in1=py[:, 0:T],
                                                                      op0=ALU.mult, op1=ALU.add), reads=[ky, "uf%d" % ct, "pp"], writes=["ygf%d" % ct])
            s.op("act", lambda e, ct=ct: e.activation(out=ygf[:, ct, :], in_=ygf[:, ct, :], func=AF.Gelu_apprx_tanh), reads=["ygf%d" % ct], writes=["ygf%d" % ct])
            s.op("pool", lambda e, ct=ct: e.tensor_copy(out=ygb[:, ct, :], in_=ygf[:, ct, :]), reads=["ygf%d" % ct], writes=["ygb%d" % ct])
        for co in range(2):
            pg, kg = pbank()
            for ck in range(2):
                s.op("pe", lambda e, co=co, ck=ck, pg=pg: e.matmul(pg[:, 0:T], lhsT=wglub[:, ck, co * 128:(co + 1) * 128], rhs=ygb[:, ck, :],
                                                                  start=(ck == 0), stop=(ck == 1)), reads=["ygb%d" % ck, "wts"], writes=[kg])
            s.op("act", lambda e, co=co, pg=pg: e.activation(out=sgl[:], in_=pg[:, 0:T], func=AF.Sigmoid, bias=pp[:, 26 + co:27 + co]),
                 reads=[kg, "pp"], writes=["sgl"])
            s.op("dve", lambda e, co=co: e.tensor_tensor(out=ymix[:, 6 + co, :], in0=ygf[:, co, :], in1=sgl[:], op=ALU.mult),
                 reads=["ygf%d" % co, "sgl"], writes=["ym%d" % (6 + co)])

        def acc_fn(j):
            pa, key = pbank()
            for k in range(8):
                s.op("pe", lambda e, k=k, j=j, pa=pa: e.matmul(pa[:, 0:T], lhsT=woutb[:, k, j * 128:(j + 1) * 128], rhs=ymix[:, k, :],
                                                              start=(k == 0), stop=(k == 7)), reads=["ym%d" % k, "wts"], writes=[key])
            return pa[:, 0:T], key

        emit_resid_ln(g, l, 1, xin, "xm", acc_fn, T, tmp)
        i_ = s.dma(lambda e, t0=t0: e.dma_start(out=dram_tile(dst, t0, T), in_=xin[:]), reads=["xm%d" % j for j in range(8)])
        s.ops[i_]["is_out"] = dst is g.outT


    import os
    MODE = os.environ.get("KMODE", "full")
    if MODE == "B":
        for it in range(ntile):
            run_tile(it, True)
        return
    for it in range(ntile):
        run_tile(it, False)
    pack_state()
    if MODE == "AB":
        unpack_state_zero()
        for it in range(ntile):
            run_tile(it, True)
        return
    contrib = xin[:].rearrange("p a b -> p (a b)")[:, 0:8 * NST].rearrange("p (a b) -> p a b", a=8)
    CK = ["xm%d" % j for j in range(8)]
    for r in range(8):
        s.op("dve", lambda e, r=r: e.tensor_scalar_mul(out=contrib[:, r, :], in0=stpack[:], scalar1=oneh[:, r:r + 1]),
             reads=["stpack", "oneh"], writes=["contrib"] + CK)
    s.dma(lambda e: e.dma_start(out=g.cc_in[l][:, :], in_=contrib.rearrange("p a b -> p (a b)")), reads=["contrib"], writes=["cc_in"], q="pool")
    s.dma(lambda e: e.collective_compute("AllReduce", ALU.add, replica_groups=[list(range(8))],
                                         ins=[g.cc_in[l].ap().opt()], outs=[g.cc_out[l].ap().opt()]),
          reads=["cc_in"], writes=["cc_out"], q="pool", inc=1)
    s.dma(lambda e: e.dma_start(out=contrib.rearrange("p a b -> p (a b)"), in_=g.cc_out[l][:, :]), reads=["cc_out"], writes=["contrib"] + CK, q="pool")
    s.op("dve", lambda e: e.tensor_scalar_mul(out=stpack[:], in0=contrib[:, 0, :], scalar1=selp[:, 0:1]), reads=["contrib", "selp"], writes=["stpack"])
    for r in range(1, 8):
        s.op("dve", lambda e, r=r: e.scalar_tensor_tensor(out=stpack[:], in0=contrib[:, r, :], scalar=selp[:, r:r + 1], in1=stpack[:],
                                                          op0=ALU.mult, op1=ALU.add), reads=["contrib", "selp", "stpack"], writes=["stpack"])
    s.op("dve", lambda e: e.tensor_copy(out=ctmp[:, 0:1], in_=contrib[:, 0, 0:1]), reads=["contrib"] + CK, writes=["ctmp0"])
    unpack_state()
    for it in range(ntile):
        run_tile(it, True)


_NC_CACHE = {}


def _prep_common(inp):
    cm = {}
    ada_b = np.asarray(inp["ada_b"], np.float32)
    cm["ada_b"] = np.ascontiguousarray(ada_b.reshape(DEPTH, 72, 128).transpose(2, 0, 1).reshape(128, DEPTH * 72))
    for nm in ("ln_g", "ln_b"):
        a = np.asarray(inp[nm], np.float32)
        cm[nm] = np.ascontiguousarray(a.reshape(DEPTH, 3, 8, 128).transpose(3, 0, 1, 2).reshape(128, DEPTH * 24))
    for nm in ("ada_w", "ffn1_w1", "ffn1_w3", "ffn1_w2", "ffn2_w1", "ffn2_w3", "ffn2_w2", "mix_w_in", "mix_w_out", "ssm_w_glu"):
        cm[nm] = np.ascontiguousarray(np.asarray(inp[nm], np.float32))
    f = lambda nm: np.asarray(inp[nm], np.float32)
    L = DEPTH
    w_in = cm["mix_w_in"]
    idx = []
    for base in (768, 1152):
        for hd in range(6):
            for j in range(64):
                idx.append(base + hd * 64 + (j + 32) % 64)
    cm["w_sw"] = np.ascontiguousarray(w_in[:, :, idx])
    pp = np.zeros((L, 128, 52), np.float32)
    pp[:, :, 0:12] = f("conv_w").reshape(L, 4, 3, 128).transpose(0, 3, 2, 1).reshape(L, 128, 12)
    for c0, nm in ((12, "conv_b"), (15, "lru_ba"), (18, "lru_bx"), (21, "lru_lam")):
        pp[:, :, c0:c0 + 3] = f(nm).reshape(L, 3, 128).transpose(0, 2, 1)
    for c0, nm in ((24, "ssm_d"), (26, "ssm_b_glu")):
        pp[:, :, c0:c0 + 2] = f(nm).reshape(L, 2, 128).transpose(0, 2, 1)
    for c0, nm in ((28, "ssm_lam_re"), (36, "ssm_lam_im")):
        pp[:, :, c0:c0 + 8] = f(nm).reshape(L, 8, 2, 64).transpose(0, 2, 3, 1).reshape(L, 128, 8)
    ls = np.repeat(f("ssm_log_step")[:, :, None], 64, axis=2)
    pp[:, :, 44:52] = ls.reshape(L, 8, 2, 64).transpose(0, 2, 3, 1).reshape(L, 128, 8)
    cm["pp"] = pp
    cm["lruw"] = np.ascontiguousarray(np.stack([f("lru_wa"), f("lru_wx")], axis=1))
    cm["gn"] = np.ascontiguousarray(np.stack([f("ret_gn_g"), f("ret_gn_b")], axis=1))
    srow = np.zeros((L, 16, 16, 5, 64), np.float32)
    srow[:, :, :, 0, :] = f("ssm_lam_re")[:, :, None, :]
    srow[:, :, :, 1, :] = f("ssm_lam_im")[:, :, None, :]
    srow[:, :, :, 2, :] = f("ssm_log_step")[:, :, None, None]
    srow[:, :, :, 3, :] = f("ssm_b_re").transpose(0, 1, 3, 2)
    srow[:, :, :, 4, :] = f("ssm_b_im").transpose(0, 1, 3, 2)
    cm["srow"] = srow.reshape(L, 256, 5, 64)
    cst = np.zeros((L, 2, 64, 8, 2, 16), np.float32)
    for ri, nm in ((0, "ssm_c_re"), (1, "ssm_c_im")):
        a = f(nm).reshape(L, 8, 2, 16, 64)
        cst[:, :, :, :, ri, :] = a.transpose(0, 2, 4, 1, 3)
    cm["cst"] = cst.reshape(L, 128, 8, 2, 16)
    p = np.arange(128)
    csm = np.zeros((128, 8), np.float32)
    csm[:, 0] = (10000.0 ** (-(p % 32).astype(np.float32) / 32.0)).astype(np.float32)
    csm[:, 1] = np.where((p % 64) < 32, -1.0, 1.0)
    cm["c_small"] = csm
    cm["c_iota"] = np.ascontiguousarray(np.broadcast_to(np.arange(257, dtype=np.float32)[None, :], (128, 257)))
    lg = np.log1p(-np.exp2(-5.0 - np.arange(6, dtype=np.float64)))
    kk = np.arange(128)[:, None]
    qq = np.arange(128)[None, :]
    mT = np.zeros((128, 6, 128), np.float32)
    for hd in range(6):
        mT[:, hd, :] = np.where(qq >= kk, np.exp(lg[hd] * np.maximum(qq - kk, 0)), 0.0) * 0.125
    cm["c_maskT"] = mT
    qd = np.zeros((128, 3, 128), np.float32)
    for i in range(3):
        for h2 in range(2):
            qd[h2 * 64:(h2 + 1) * 64, i, :] = np.exp(lg[2 * i + h2] * (np.arange(128) + 1.0))[None, :]
    cm["c_qdec"] = qd
    kd = np.zeros((128, 6, 64), np.float32)
    for hd in range(6):
        kd[:, hd, :] = (np.exp(lg[hd] * (127.0 - np.arange(128))) * 0.125)[:, None]
    cm["c_kdt"] = kd.reshape(128, 384)
    cm["c_gmask"] = (p[:, None] // 16 == np.arange(8)[None, :]).astype(np.float32)
    cm["c_ident"] = np.eye(128, dtype=np.float32)
    return cm


def kernel(**inp):
    x = np.asarray(inp["x"], np.float32)
    c = np.asarray(inp["c"], np.float32)
    pos = np.asarray(inp["positions"], np.int32)
    B, S, _ = x.shape
    if "full" not in _NC_CACHE:
        _NC_CACHE["full"] = build_program()
    nc = _NC_CACHE["full"]
    cm = _prep_common(inp)
    in_maps = []
    for core in range(8):
        b, hf = core // 2, core % 2
        m = dict(cm)
        m["xT"] = np.ascontiguousarray(x[b, hf * NTOK:(hf + 1) * NTOK, :].T)
        m["cvec"] = np.ascontiguousarray(c[b].reshape(8, 128).T)
        m["pos"] = np.ascontiguousarray(pos[b, hf * NTOK:(hf + 1) * NTOK][None, :])
        oh = np.zeros((128, 8), np.float32)
        oh[:, core] = 1.0
        sp = np.zeros((128, 8), np.float32)
        if hf == 1:
            sp[:, core - 1] = 1.0
        m["onehot"] = oh
        m["selprev"] = sp
        in_maps.append(m)
    res = run_bass_kernel_spmd(nc, in_maps, core_ids=list(range(8)))
    out = np.empty((B, S, D), np.float32)
    for core in range(8):
        b, hf = core // 2, core % 2
        out[b, hf * NTOK:(hf + 1) * NTOK, :] = res.results[core]["outT"].T
    return out
```

```python
from contextlib import ExitStack
import numpy as np
import concourse.bass as bass
import concourse.mybir as mybir
from concourse.bass_utils import run_bass_kernel_spmd

F32 = mybir.dt.float32
BF16 = mybir.dt.bfloat16
I32 = mybir.dt.int32
AF = mybir.ActivationFunctionType
ALU = mybir.AluOpType
AX = mybir.AxisListType

D = 1024
DFF = 2816
NF = DFF // 128
DEPTH = 2
NIN = 2560
ALPHA = (2.0 * DEPTH) ** 0.25
EPS_LN = 1e-5 / (ALPHA * ALPHA)
NTOK = 4096
MAGIC = 12582912.0
TWO_PI = 6.283185307179586
C1 = 6.28125
C2 = TWO_PI - C1


class Sched:
    ENGS = ("pe", "dve", "act", "pool", "sp")

    def __init__(self, nc):
        self.nc = nc
        self.ops = []
        self.last_w = {}
        self.readers = {}
        self.n_dma_sems = 24
        self.bar_start = 0

    def op(self, eng, fn, reads=(), writes=(), dma=False):
        deps = set()
        for k in reads:
            w = self.last_w.get(k)
            if w is not None:
                deps.add(w)
        for k in writes:
            w = self.last_w.get(k)
            if w is not None:
                deps.add(w)
            for r in self.readers.get(k, ()):
                deps.add(r)
        idx = len(self.ops)
        if not dma and eng == "pe":
            deps = {d for d in deps if self.ops[d]["dma"] or self.ops[d]["eng"] != "pe"}
        self.ops.append(dict(eng=eng, fn=fn, deps=deps, dma=dma, sig=False))
        for k in writes:
            self.last_w[k] = idx
            self.readers[k] = []
        for k in reads:
            if k not in writes:
                self.readers.setdefault(k, []).append(idx)
        return idx

    def dma(self, fn, reads=(), writes=(), q="sp", inc=16):
        i = self.op(q, fn, reads, writes, dma=True)
        self.ops[i]["inc"] = inc
        return i

    def barrier(self):
        last = {}
        for i, o in enumerate(self.ops):
            if i < self.bar_start:
                continue
            key = ("dma", i) if o["dma"] else o["eng"]
            last[key] = i
        deps = set(last.values())
        for e in self.ENGS:
            self.ops.append(dict(eng=e, fn=None, deps=set(deps), dma=False, sig=False))
        self.bar_start = len(self.ops)
        self.last_w = {}
        self.readers = {}

    def emit(self, es, final_wait_ops=()):
        nc = self.nc
        ops = self.ops
        for o in ops:
            for d in o["deps"]:
                ops[d]["sig"] = True
        for i in final_wait_ops:
            ops[i]["sig"] = True
        esem = {e: es.enter_context(nc.semaphore("s_" + e)) for e in self.ENGS}
        dsem = [es.enter_context(nc.semaphore("d%d" % i)) for i in range(self.n_dma_sems)]
        csem = es.enter_context(nc.semaphore("ccsem"))
        ccnt = 0
        cnt = {e: 0 for e in self.ENGS}
        dcnt = [0] * self.n_dma_sems
        dlast = [None] * self.n_dma_sems
        nd = 0
        for i, o in enumerate(ops):
            if o["dma"] and o.get("inc", 16) == 1:
                ccnt += 1
                o["ev"] = (csem, ccnt, ("c", 0))
            elif o["dma"]:
                k = nd % self.n_dma_sems
                nd += 1
                dcnt[k] += o.get("inc", 16)
                o["ev"] = (dsem[k], dcnt[k], ("d", k))
                if dlast[k] is not None:
                    o["deps"] = set(o["deps"]) | {dlast[k]}
                dlast[k] = i
            elif o["sig"]:
                cnt[o["eng"]] += 1
                o["ev"] = (esem[o["eng"]], cnt[o["eng"]], ("e", o["eng"]))
        streams = {e: [] for e in self.ENGS}
        for i, o in enumerate(ops):
            streams[o["eng"]].append(i)
        final = list(final_wait_ops)

        def run(e, eng):
            known = {}
            for i in streams[e]:
                o = ops[i]
                need = {}
                for d in o["deps"]:
                    sem, val, key = ops[d]["ev"]
                    if need.get(key, (None, 0))[1] < val:
                        need[key] = (sem, val)
                for key, (sem, val) in need.items():
                    if known.get(key, 0) < val:
                        eng.wait_ge(sem, val)
                        known[key] = val
                if o["fn"] is None:
                    continue
                ins = o["fn"](eng)
                if o["dma"]:
                    ins.then_inc(o["ev"][0], o.get("inc", 16))
                elif o["sig"]:
                    ins.then_inc(o["ev"][0], 1)
            if e == "sp":
                for i in final:
                    sem, val, key = ops[i]["ev"]
                    eng.wait_ge(sem, val)

        block = es.enter_context(nc.Block())

        @block.tensor
        def _(eng):
            run("pe", eng)

        @block.vector
        def _(eng):
            run("dve", eng)

        @block.scalar
        def _(eng):
            run("act", eng)

        @block.gpsimd
        def _(eng):
            run("pool", eng)

        @block.sync
        def _(eng):
            run("sp", eng)


class SbufAlloc:
    def __init__(self, nc, limit=229344):
        self.nc = nc
        self.off = 16512
        self.limit = limit
        self.n = 0

    def mark(self):
        return self.off

    def reset(self, m):
        self.off = m

    def alloc(self, shape, dtype, name=None):
        nbytes = int(np.prod(shape[1:])) * (2 if dtype == BF16 else 4)
        nbytes = (nbytes + 63) // 64 * 64
        assert self.off + nbytes <= self.limit, ("SBUF overflow", name, self.off, nbytes)
        self.n += 1
        t = self.nc.alloc_sbuf_tensor_at("%s_%d" % (name or "t", self.n), list(shape), dtype, offset=self.off)
        self.off += nbytes
        return t


class Ctx:
    pass


def build_program(ntok=NTOK, depth=DEPTH, phases=("ffn1", "mix", "ffn2"), debug_out=False):
    nc = bass.Bass("TRN2", target_bir_lowering=False)
    g = Ctx()
    g.nc = nc
    g.ntok = ntok
    s = Sched(nc)
    g.s = s
    sb = SbufAlloc(nc)
    g.sb = sb

    def din(name, shape, dt=F32):
        return nc.dram_tensor(name, list(shape), dt, kind="ExternalInput").ap()

    g.xT = din("xT", [D, ntok])
    g.outT = nc.dram_tensor("outT", [D, ntok], F32, kind="ExternalOutput").ap()
    g.cvec = din("cvec", [128, 8])
    g.pos = din("pos", [1, ntok], I32)
    g.ada_w = din("ada_w", [DEPTH, D, 9 * D])
    g.ada_b = din("ada_b", [128, DEPTH * 72])
    g.ln_g = din("ln_g", [128, DEPTH * 3 * 8])
    g.ln_b = din("ln_b", [128, DEPTH * 3 * 8])
    g.w1 = [din("ffn1_w1", [DEPTH, D, DFF]), din("ffn2_w1", [DEPTH, D, DFF])]
    g.w3 = [din("ffn1_w3", [DEPTH, D, DFF]), din("ffn2_w3", [DEPTH, D, DFF])]
    g.w2 = [din("ffn1_w2", [DEPTH, DFF, D]), din("ffn2_w2", [DEPTH, DFF, D])]
    g.scr = [nc.dram_tensor("scr%d" % i, [D, ntok], F32).ap() for i in range(2)]
    g.mix_w_in = din("mix_w_in", [DEPTH, D, NIN])
    g.w_sw = din("w_sw", [DEPTH, D, 768])
    g.mix_w_out = din("mix_w_out", [DEPTH, D, D])
    g.w_glu = din("ssm_w_glu", [DEPTH, 256, 256])
    g.pp = din("pp", [DEPTH, 128, 52])
    g.lruw = din("lruw", [DEPTH, 2, 6, 64, 64])
    g.gn = din("gn", [DEPTH, 2, 384])
    g.srow = din("srow", [DEPTH, 256, 5, 64])
    g.cst = din("cst", [DEPTH, 128, 8, 2, 16])
    g.c_small = din("c_small", [128, 8])
    g.c_iota = din("c_iota", [128, 257])
    g.c_maskT = din("c_maskT", [128, 6, 128])
    g.c_qdec = din("c_qdec", [128, 3, 128])
    g.c_kdt = din("c_kdt", [128, 384])
    g.c_gmask = din("c_gmask", [128, 8])
    g.c_ident = din("c_ident", [128, 128])
    g.onehot = din("onehot", [128, 8])
    g.selprev = din("selprev", [128, 8])
    g.cc_in = [nc.dram_tensor("cc_in%d" % i, [128, 8 * NST], F32) for i in range(DEPTH)]
    g.cc_out = [nc.dram_tensor("cc_out%d" % i, [128, 8 * NST], F32) for i in range(DEPTH)]

    g.ones = sb.alloc([128, 128], BF16, "ones")
    g.mod = sb.alloc([128, DEPTH * 72], F32, "mod")
    g.sc1p = sb.alloc([128, DEPTH * 72], F32, "sc1p")
    g.lng = sb.alloc([128, DEPTH * 24], F32, "lng")
    g.lnb = sb.alloc([128, DEPTH * 24], F32, "lnb")
    g.cond = sb.alloc([128, 8], F32, "cond")
    g.adab = sb.alloc([128, DEPTH * 72], F32, "adab")
    g.psum = None
    base_mark = sb.mark()

    s.op("pool", lambda e: e.memset(g.ones[:], 1.0 / 1024.0), writes=["ones"])
    s.dma(lambda e: e.dma_start(out=g.cond[:], in_=g.cvec), writes=["cond"])
    s.dma(lambda e: e.dma_start(out=g.adab[:], in_=g.ada_b), writes=["adab"])
    s.dma(lambda e: e.dma_start(out=g.lng[:], in_=g.ln_g), writes=["lng"])
    s.dma(lambda e: e.dma_start(out=g.lnb[:], in_=g.ln_b), writes=["lnb"])
    s.op("act", lambda e: e.activation(out=g.cond[:], in_=g.cond[:], func=AF.Silu), reads=["cond"], writes=["cond"])

    es = ExitStack()
    g.ps = [es.enter_context(nc.psum_tensor("ps%d" % i, [128, 512], F32)) for i in range(8)]

    emit_mod(g, depth)
    s.barrier()

    src = g.xT
    nsub = 0
    for l in range(depth):
        for ph in phases:
            last = (l == depth - 1) and (ph == phases[-1])
            dst = g.outT if last else g.scr[nsub % 2]
            sb.reset(base_mark)
            if ph == "ffn1":
                emit_ffn(g, l, 0, src, dst)
            elif ph == "ffn2":
                emit_ffn(g, l, 1, src, dst)
            else:
                emit_mixer(g, l, src, dst)
            s.barrier()
            src = dst
            nsub += 1
    final = [i for i, o in enumerate(s.ops) if o["dma"] and o.get("is_out")]
    s.emit(es, final_wait_ops=final)
    es.close()
    return nc


def emit_mod(g, depth):
    s, sb, nc = g.s, g.sb, g.nc
    m = sb.mark()
    stg = [sb.alloc([128, 8, 512], F32, "adastg") for _ in range(3)]
    ps = g.ps[0]
    n = 0
    for l in range(depth):
        for piece in range(18):
            b = n % 3
            n += 1
            st = stg[b]
            s.dma(lambda e, st=st, l=l, piece=piece: e.dma_start(
                out=st[:], in_=g.ada_w[l, :, piece * 512:(piece + 1) * 512].rearrange("(k p) n -> p k n", p=128)),
                writes=["adastg%d" % b])
            for c in range(4):
                col = l * 72 + piece * 4 + c
                for k in range(8):
                    s.op("pe", lambda e, st=st, c=c, col=col, k=k: e.matmul(
                        ps[:, col:col + 1], lhsT=st[:, k, c * 128:(c + 1) * 128], rhs=g.cond[:, k:k + 1],
                        start=(k == 0), stop=(k == 7)),
                        reads=["adastg%d" % b, "cond"], writes=["modps"])
    W = depth * 72
    s.op("dve", lambda e: e.tensor_tensor(out=g.mod[:, 0:W], in0=ps[:, 0:W], in1=g.adab[:, 0:W], op=ALU.add),
         reads=["modps", "adab"], writes=["mod"])
    for l in range(depth):
        for n_ in range(9):
            c0 = l * 72 + n_ * 8
            if n_ in (1, 4, 7):
                s.op("dve", lambda e, c0=c0: e.tensor_scalar_add(out=g.sc1p[:, c0:c0 + 8], in0=g.mod[:, c0:c0 + 8], scalar1=1.0),
                     reads=["mod"], writes=["sc1p"])
            elif n_ in (2, 5, 8):
                coef = (0.5 if n_ in (2, 8) else 1.0) / ALPHA
                s.op("dve", lambda e, c0=c0, coef=coef: e.tensor_scalar_mul(out=g.sc1p[:, c0:c0 + 8], in0=g.mod[:, c0:c0 + 8], scalar1=coef),
                     reads=["mod"], writes=["sc1p"])
            else:
                s.op("dve", lambda e, c0=c0: e.tensor_copy(out=g.sc1p[:, c0:c0 + 8], in_=g.mod[:, c0:c0 + 8]),
                     reads=["mod"], writes=["sc1p"])
    sb.reset(m)


def load_cast(g, dram_ap, dst_ap, stage_tiles, stage_keys, n, dst_key, shape3=None):
    s = g.s
    b = n % len(stage_tiles)
    st = stage_tiles[b]
    view = st[:, 0:int(np.prod(dst_ap.shape[1:]))]
    if len(dst_ap.shape) == 3:
        view = view.rearrange("p (a b) -> p a b", a=dst_ap.shape[1])
    s.dma(lambda e: e.dma_start(out=view, in_=dram_ap), writes=[stage_keys[b]])
    eng = ("act", "pool", "dve")[n % 3]
    if eng == "act":
        s.op("act", lambda e: e.copy(out=dst_ap, in_=view), reads=[stage_keys[b]], writes=[dst_key])
    else:
        s.op(eng, lambda e: e.tensor_copy(out=dst_ap, in_=view), reads=[stage_keys[b]], writes=[dst_key])


def emit_resid_ln(g, l, isub, xin, xkey, acc_fn, T, tmp):
    s = g.s
    gcol = l * 72 + (isub * 3 + 2) * 8
    lcol = l * 24 + isub * 8
    ybf, ysq, mean_sb, m2, var_sb = tmp["ybf"], tmp["ysq"], tmp["mean"], tmp["m2"], tmp["var"]
    for j in range(8):
        acc, akey = acc_fn(j)
        s.op("dve", lambda e, j=j, acc=acc: e.scalar_tensor_tensor(
            out=xin[:, j, :], in0=acc, scalar=g.sc1p[:, gcol + j:gcol + j + 1], in1=xin[:, j, :],
            op0=ALU.mult, op1=ALU.add), reads=[akey, xkey + "%d" % j, "sc1p"], writes=[xkey + "%d" % j])
        s.op("pool", lambda e, j=j: e.tensor_copy(out=ybf[:, j, :], in_=xin[:, j, :]),
             reads=[xkey + "%d" % j], writes=["ybf%d" % j])
        s.op("act", lambda e, j=j: e.activation(out=ysq[:, j, :], in_=xin[:, j, :], func=AF.Square),
             reads=[xkey + "%d" % j], writes=["ysq%d" % j])
    import os
    DBG = int(os.environ.get("KDBG", "9"))
    if DBG <= 4:
        return
    pm, pe2 = g.ps[6], g.ps[7]
    for j in range(8):
        s.op("pe", lambda e, j=j: e.matmul(pm[:, 0:T], lhsT=g.ones[:], rhs=ybf[:, j, :], start=(j == 0), stop=(j == 7)),
             reads=["ybf%d" % j, "ones"], writes=["ps6"])
    for j in range(8):
        s.op("pe", lambda e, j=j: e.matmul(pe2[:, 0:T], lhsT=g.ones[:], rhs=ysq[:, j, :], start=(j == 0), stop=(j == 7)),
             reads=["ysq%d" % j, "ones"], writes=["ps7"])
    if DBG <= 5:
        return
    s.op("act", lambda e: e.copy(out=mean_sb[:], in_=pm[:, 0:T]), reads=["ps6"], writes=["mean"])
    s.op("dve", lambda e: e.tensor_tensor(out=m2[:], in0=mean_sb[:], in1=mean_sb[:], op=ALU.mult), reads=["mean"], writes=["m2"])
    s.op("dve", lambda e: e.tensor_tensor(out=var_sb[:], in0=pe2[:, 0:T], in1=m2[:], op=ALU.subtract), reads=["ps7", "m2"], writes=["var"])
    s.op("dve", lambda e: e.tensor_scalar(out=var_sb[:], in0=var_sb[:], scalar1=0.0, scalar2=EPS_LN, op0=ALU.max, op1=ALU.add),
         reads=["var"], writes=["var"])
    s.op("act", lambda e: e.activation(out=var_sb[:], in_=var_sb[:], func=AF.Sqrt), reads=["var"], writes=["var"])
    s.op("dve", lambda e: e.reciprocal(out=var_sb[:], in_=var_sb[:]), reads=["var"], writes=["var"])
    if DBG <= 6:
        return
    for j in range(8):
        k = xkey + "%d" % j
        s.op("dve", lambda e, j=j: e.tensor_tensor(out=xin[:, j, :], in0=xin[:, j, :], in1=mean_sb[:], op=ALU.subtract),
             reads=[k, "mean"], writes=[k])
        s.op("dve", lambda e, j=j: e.tensor_tensor(out=xin[:, j, :], in0=xin[:, j, :], in1=var_sb[:], op=ALU.mult),
             reads=[k, "var"], writes=[k])
        s.op("pool", lambda e, j=j: e.tensor_scalar(out=xin[:, j, :], in0=xin[:, j, :],
                                                     scalar1=g.lng[:, lcol + j:lcol + j + 1], scalar2=g.lnb[:, lcol + j:lcol + j + 1],
                                                     op0=ALU.mult, op1=ALU.add),
             reads=[k, "lng", "lnb"], writes=[k])


def dram_tile(ap, t0, T):
    return ap[:, t0:t0 + T].rearrange("(k p) t -> p k t", p=128)


def emit_ffn(g, l, which, src, dst):
    s, sb, nc = g.s, g.sb, g.nc
    T = 512
    isub = 0 if which == 0 else 2
    w1b = sb.alloc([128, 8, DFF], BF16, "w1b")
    w3b = sb.alloc([128, 8, DFF], BF16, "w3b")
    w2b = sb.alloc([128, NF, D], BF16, "w2b")
    xin = [sb.alloc([128, 8, T], F32, "xin") for _ in range(1)]
    h = sb.alloc([128, 8, T], BF16, "h")
    sil = [sb.alloc([128, T], BF16, "sil") for _ in range(2)]
    tmp = dict(mean=sb.alloc([128, T], F32, "mean"), m2=sb.alloc([128, T], F32, "m2"), var=sb.alloc([128, T], F32, "var"))
    mk = sb.mark()
    stg = [sb.alloc([128, DFF], F32, "stg") for _ in range(2)]
    skeys = ["stg0", "stg1"]
    n = 0
    for k in range(8):
        load_cast(g, g.w1[which][l, k * 128:(k + 1) * 128, :], w1b[:, k, :], stg, skeys, n, "w1b"); n += 1
        load_cast(g, g.w3[which][l, k * 128:(k + 1) * 128, :], w3b[:, k, :], stg, skeys, n, "w3b"); n += 1
    for c in range(0, NF, 2):
        load_cast(g, g.w2[which][l, c * 128:(c + 2) * 128, :].rearrange("(c p) n -> p c n", p=128),
                  w2b[:, c:c + 2, :], stg, skeys, n, "w2b"); n += 1
    s.barrier()
    import os
    DBG = int(os.environ.get("KDBG", "9"))
    if DBG <= 1:
        return
    sb.reset(mk)
    gT = sb.alloc([128, NF, T], BF16, "gT")
    tmp["ybf"] = sb.alloc([128, 8, T], BF16, "ybf")
    tmp["ysq"] = sb.alloc([128, 8, T], BF16, "ysq")
    shc = l * 72 + (isub * 3 + 0) * 8
    scc = l * 72 + (isub * 3 + 1) * 8
    ntile = g.ntok // T
    for it in range(ntile):
        t0 = it * T
        xb = 0
        xt = xin[xb]
        xkey = "xin%d_" % xb
        for j in range(8):
            s.dma(lambda e, xt=xt, t0=t0, j=j: e.dma_start(out=xt[:, j, :], in_=src[j * 128:(j + 1) * 128, t0:t0 + T]),
                  writes=[xkey + "%d" % j])
        for k in range(8):
            s.op("dve", lambda e, k=k, xt=xt: e.tensor_scalar(
                out=h[:, k, :], in0=xt[:, k, :], scalar1=g.sc1p[:, scc + k:scc + k + 1], scalar2=g.sc1p[:, shc + k:shc + k + 1],
                op0=ALU.mult, op1=ALU.add), reads=[xkey + "%d" % k, "sc1p"], writes=["h%d" % k])
        if DBG <= 2:
            continue
        for f in range(NF):
            pb = f % 2
            p1, p3 = g.ps[2 * pb], g.ps[2 * pb + 1]
            for k in range(8):
                s.op("pe", lambda e, k=k, f=f, p1=p1: e.matmul(p1[:, 0:T], lhsT=w1b[:, k, f * 128:(f + 1) * 128], rhs=h[:, k, :],
                                                              start=(k == 0), stop=(k == 7)),
                     reads=["h%d" % k, "w1b"], writes=["ps%d" % (2 * pb)])
            for k in range(8):
                s.op("pe", lambda e, k=k, f=f, p3=p3: e.matmul(p3[:, 0:T], lhsT=w3b[:, k, f * 128:(f + 1) * 128], rhs=h[:, k, :],
                                                              start=(k == 0), stop=(k == 7)),
                     reads=["h%d" % k, "w3b"], writes=["ps%d" % (2 * pb + 1)])
            sl = sil[pb]
            s.op("act", lambda e, p1=p1, sl=sl: e.activation(out=sl[:], in_=p1[:, 0:T], func=AF.Silu),
                 reads=["ps%d" % (2 * pb)], writes=["sil%d" % pb])
            s.op("dve", lambda e, f=f, p3=p3, sl=sl: e.tensor_tensor(out=gT[:, f, :], in0=p3[:, 0:T], in1=sl[:], op=ALU.mult),
                 reads=["ps%d" % (2 * pb + 1), "sil%d" % pb], writes=["gT%d" % f])

        def acc_fn(j):
            pa = g.ps[4 + j % 2]
            key = "ps%d" % (4 + j % 2)
            for f in range(NF):
                s.op("pe", lambda e, f=f, j=j, pa=pa: e.matmul(pa[:, 0:T], lhsT=w2b[:, f, j * 128:(j + 1) * 128], rhs=gT[:, f, :],
                                                              start=(f == 0), stop=(f == NF - 1)),
                     reads=["gT%d" % f, "w2b"], writes=[key])
            return pa[:, 0:T], key

        if DBG <= 3:
            continue
        emit_resid_ln(g, l, isub, xt, xkey, acc_fn, T, tmp)
        for j in range(8):
            i = s.dma(lambda e, xt=xt, t0=t0, j=j: e.dma_start(out=dst[j * 128:(j + 1) * 128, t0:t0 + T], in_=xt[:, j, :]),
                      reads=[xkey + "%d" % j])
            s.ops[i]["is_out"] = dst is g.outT


NST = 3 + 9 + 192 + 16


def bc_inner(ap2, n):
    return ap2.unsqueeze(2).to_broadcast([ap2.shape[0], ap2.shape[1], n])


def emit_sincos(g, ang, sin_out, cos_out, tmpa, tmpb, keys, sin_scale=None, eng="dve"):
    s = g.s
    ka, ks, kc, kt1, kt2 = keys
    for (shift, outp, okey, scale) in ((0.0, sin_out, ks, sin_scale), (0.25, cos_out, kc, None)):
        s.op(eng, lambda e, shift=shift: e.tensor_scalar(out=tmpa, in0=ang, scalar1=1.0 / TWO_PI, scalar2=shift,
                                                         op0=ALU.mult, op1=ALU.add), reads=[ka], writes=[kt1])
        s.op(eng, lambda e: e.tensor_scalar(out=tmpa, in0=tmpa, scalar1=MAGIC, scalar2=MAGIC, op0=ALU.add, op1=ALU.subtract),
             reads=[kt1], writes=[kt1])
        s.op(eng, lambda e: e.scalar_tensor_tensor(out=tmpb, in0=tmpa, scalar=-C1, in1=ang, op0=ALU.mult, op1=ALU.add),
             reads=[kt1, ka], writes=[kt2])
        s.op(eng, lambda e: e.scalar_tensor_tensor(out=tmpb, in0=tmpa, scalar=-C2, in1=tmpb, op0=ALU.mult, op1=ALU.add),
             reads=[kt1, kt2], writes=[kt2])
        s.op(eng, lambda e, shift=shift: e.tensor_scalar(out=tmpb, in0=tmpb, scalar1=shift * TWO_PI, scalar2=-3.1415925,
                                                         op0=ALU.add, op1=ALU.max), reads=[kt2], writes=[kt2])
        s.op(eng, lambda e: e.tensor_scalar_min(out=tmpb, in0=tmpb, scalar1=3.1415925), reads=[kt2], writes=[kt2])
        if scale is None:
            s.op("act", lambda e, outp=outp: e.activation(out=outp, in_=tmpb, func=AF.Sin), reads=[kt2], writes=[okey])
        else:
            s.op("act", lambda e, outp=outp, scale=scale: e.activation(out=outp, in_=tmpb, func=AF.Sin, scale=scale),
                 reads=[kt2], writes=[okey])


def emit_mixer(g, l, src, dst):
    s, sb, nc = g.s, g.sb, g.nc
    T = 256
    NCH = T // 128
    ntile = g.ntok // T
    A = sb.alloc
    winb = A([128, 8, NIN], BF16, "winb")
    wswb = A([128, 8, 768], BF16, "wswb")
    woutb = A([128, 8, D], BF16, "woutb")
    pp = A([128, 52], F32, "pp")
    cs = A([128, 8], F32, "cs")
    iota = A([128, 257], F32, "iota")
    maskT = A([128, 6, 128], F32, "maskT")
    qdec = A([128, 3, 128], F32, "qdec")
    kdt = A([128, 384], F32, "kdt")
    gng = A([128, 384], F32, "gng")
    gnb = A([128, 384], F32, "gnb")
    identb = A([128, 128], BF16, "identb")
    gmask = A([128, 8], F32, "gmask")
    CT = A([128, 8, 257], F32, "CT")
    ST = A([128, 8, 257], F32, "ST")
    TBre = A([128, 8, 128], BF16, "TBre")
    TBim = A([128, 8, 128], BF16, "TBim")
    TCre = A([128, 8, 128], BF16, "TCre")
    TCim = A([128, 8, 128], BF16, "TCim")
    wglub = A([128, 2, 256], BF16, "wglub")
    wabd = A([128, 3, 128], BF16, "wabd")
    wxbd = A([128, 3, 128], BF16, "wxbd")
    sp_ = A([128, 64], F32, "sp")
    cneg = A([128, 6], F32, "cneg")
    stt = A([128, 3, 64], F32, "stt")
    stbf = A([128, 3, 64], BF16, "stbf")
    lstate = A([128, 3], F32, "lstate")
    uext = A([128, 3, T + 3], F32, "uext")
    carr = A([128, 16], F32, "carr")
    stpack = A([128, NST], F32, "stpack")
    oneh = A([128, 8], F32, "oneh")
    selp = A([128, 8], F32, "selp")
    mk = sb.mark()
    stg = [A([128, NIN], F32, "stg") for _ in range(2)]
    skeys = ["stg0", "stg1"]
    n = 0
    for k in range(8):
        load_cast(g, g.mix_w_in[l, k * 128:(k + 1) * 128, :], winb[:, k, :], stg, skeys, n, "winb"); n += 1
        load_cast(g, g.w_sw[l, k * 128:(k + 1) * 128, :], wswb[:, k, :], stg, skeys, n, "wswb"); n += 1
        load_cast(g, g.mix_w_out[l, k * 128:(k + 1) * 128, :], woutb[:, k, :], stg, skeys, n, "woutb"); n += 1
    load_cast(g, g.w_glu[l].rearrange("(c p) n -> p c n", p=128), wglub[:, :, :], stg, skeys, n, "wglub"); n += 1
    s.dma(lambda e: e.dma_start(out=pp[:], in_=g.pp[l]), writes=["pp"])
    s.dma(lambda e: e.dma_start(out=cs[:], in_=g.c_small), writes=["cs"])
    s.dma(lambda e: e.dma_start(out=iota[:], in_=g.c_iota), writes=["iota"])
    s.dma(lambda e: e.dma_start(out=maskT[:], in_=g.c_maskT), writes=["maskT"])
    s.dma(lambda e: e.dma_start(out=qdec[:], in_=g.c_qdec), writes=["qdec"])
    s.dma(lambda e: e.dma_start(out=kdt[:], in_=g.c_kdt), writes=["kdt"])
    s.dma(lambda e: e.dma_start(out=gmask[:], in_=g.c_gmask), writes=["gmask"])
    s.dma(lambda e: e.dma_start(out=gng[:], in_=g.gn[l, 0:1, :].partition_broadcast(128)), writes=["gng"])
    s.dma(lambda e: e.dma_start(out=gnb[:], in_=g.gn[l, 1:2, :].partition_broadcast(128)), writes=["gnb"])
    s.dma(lambda e: e.dma_start(out=oneh[:], in_=g.onehot), writes=["oneh"])
    s.dma(lambda e: e.dma_start(out=selp[:], in_=g.selprev), writes=["selp"])
    s.op("pool", lambda e: e.memset(stpack[:], 0.0), writes=["stpack"])
    idf = A([128, 128], F32, "idf")
    s.dma(lambda e: e.dma_start(out=idf[:], in_=g.c_ident), writes=["idf"])
    s.op("dve", lambda e: e.tensor_copy(out=identb[:], in_=idf[:]), reads=["idf"], writes=["identb"])
    bdf = A([128, 2, 3, 128], F32, "bdf")
    s.op("pool", lambda e: e.memset(bdf[:], 0.0), writes=["bdf"])
    for ax in range(2):
        for hd in range(6):
            po = 64 * (hd % 2)
            s.dma(lambda e, ax=ax, hd=hd, po=po: e.dma_start(out=bdf[po:po + 64, ax, hd // 2, po:po + 64], in_=g.lruw[l, ax, hd]),
                  reads=["bdf"], writes=["bdf"])
    s.op("dve", lambda e: e.tensor_copy(out=wabd[:], in_=bdf[:, 0, :, :]), reads=["bdf"], writes=["wabd"])
    s.op("dve", lambda e: e.tensor_copy(out=wxbd[:], in_=bdf[:, 1, :, :]), reads=["bdf"], writes=["wxbd"])
    s.op("act", lambda e: e.activation(out=cneg[:, 0:3], in_=pp[:, 21:24], func=AF.Exp, scale=-1.0), reads=["pp"], writes=["cneg"])
    s.op("dve", lambda e: e.tensor_scalar_add(out=cneg[:, 0:3], in0=cneg[:, 0:3], scalar1=1.0), reads=["cneg"], writes=["cneg"])
    s.op("act", lambda e: e.activation(out=cneg[:, 0:3], in_=cneg[:, 0:3], func=AF.Ln), reads=["cneg"], writes=["cneg"])
    s.op("dve", lambda e: e.tensor_scalar_mul(out=cneg[:, 3:6], in0=cneg[:, 0:3], scalar1=-16.0), reads=["cneg"], writes=["cneg2"])
    s.op("dve", lambda e: e.tensor_scalar_mul(out=cneg[:, 0:3], in0=cneg[:, 0:3], scalar1=-8.0), reads=["cneg", "cneg2"], writes=["cneg"])
    def unpack_state():
        s.op("dve", lambda e: e.tensor_copy(out=lstate[:], in_=stpack[:, 0:3]), reads=["stpack"], writes=["lstate"])
        s.op("dve", lambda e: e.tensor_copy(out=uext[:, :, 0:3], in_=stpack[:, 3:12].rearrange("p (a b) -> p a b", a=3)),
             reads=["stpack"], writes=["uext0", "uext1", "uext2"])
        s.op("dve", lambda e: e.tensor_copy(out=stt[:], in_=stpack[:, 12:204].rearrange("p (a b) -> p a b", a=3)),
             reads=["stpack"], writes=["stt"])
        s.op("dve", lambda e: e.tensor_copy(out=stbf[:], in_=stt[:]), reads=["stt"], writes=["stbf"])
        s.op("dve", lambda e: e.tensor_copy(out=carr[:], in_=stpack[:, 204:220]), reads=["stpack"], writes=["carr"])

    def pack_state():
        s.op("dve", lambda e: e.tensor_copy(out=stpack[:, 0:3], in_=lstate[:]), reads=["lstate"], writes=["stpack"])
        s.op("dve", lambda e: e.tensor_copy(out=stpack[:, 3:12].rearrange("p (a b) -> p a b", a=3), in_=uext[:, :, 0:3]),
             reads=["uext0", "uext1", "uext2", "stpack"], writes=["stpack"])
        s.op("dve", lambda e: e.tensor_copy(out=stpack[:, 12:204].rearrange("p (a b) -> p a b", a=3), in_=stt[:]), reads=["stt", "stpack"], writes=["stpack"])
        s.op("dve", lambda e: e.tensor_copy(out=stpack[:, 204:220], in_=carr[:]), reads=["carr", "stpack"], writes=["stpack"])

    unpack_state()

    def unpack_state_zero():
        s.op("pool", lambda e: e.memset(stpack[:], 0.0), reads=["stpack"], writes=["stpack"])
        unpack_state()

    def ssm_params(lr, li, ls, W, pfx, tmp):
        t = lambda i: tmp[:, i, :]
        dt_, mag, ang, sn, cn, ta, tb, zr1, er, ei, den, tt = [t(i) for i in range(12)]
        K = lambda nm: pfx + nm
        s.op("act", lambda e: e.activation(out=dt_, in_=ls, func=AF.Exp), reads=[K("in")], writes=[K("dt")])
        s.op("dve", lambda e: e.tensor_tensor(out=mag, in0=lr, in1=dt_, op=ALU.mult), reads=[K("in"), K("dt")], writes=[K("mag")])
        s.op("act", lambda e: e.activation(out=mag, in_=mag, func=AF.Exp), reads=[K("mag")], writes=[K("mag")])
        s.op("dve", lambda e: e.tensor_tensor(out=ang, in0=li, in1=dt_, op=ALU.mult), reads=[K("in"), K("dt")], writes=[K("ang")])
        emit_sincos(g, ang, sn, cn, ta, tb, (K("ang"), K("sn"), K("cn"), K("ta"), K("tb")))
        s.op("dve", lambda e: e.tensor_tensor(out=zr1, in0=mag, in1=cn, op=ALU.mult), reads=[K("mag"), K("cn")], writes=[K("zr1")])
        s.op("dve", lambda e: e.tensor_scalar_add(out=zr1, in0=zr1, scalar1=-1.0), reads=[K("zr1")], writes=[K("zr1")])
        s.op("dve", lambda e: e.tensor_tensor(out=tt, in0=mag, in1=sn, op=ALU.mult), reads=[K("mag"), K("sn")], writes=[K("zi")])
        s.op("dve", lambda e: e.tensor_tensor(out=den, in0=lr, in1=lr, op=ALU.mult), reads=[K("in")], writes=[K("den")])
        s.op("dve", lambda e: e.tensor_tensor(out=ta, in0=li, in1=li, op=ALU.mult), reads=[K("in"), K("ta")], writes=[K("ta")])
        s.op("dve", lambda e: e.tensor_tensor(out=den, in0=den, in1=ta, op=ALU.add), reads=[K("den"), K("ta")], writes=[K("den")])
        s.op("dve", lambda e: e.reciprocal(out=den, in_=den), reads=[K("den")], writes=[K("den")])
        s.op("dve", lambda e: e.tensor_tensor(out=er, in0=zr1, in1=lr, op=ALU.mult), reads=[K("zr1"), K("in")], writes=[K("er")])
        s.op("dve", lambda e: e.tensor_tensor(out=ta, in0=tt, in1=li, op=ALU.mult), reads=[K("zi"), K("in"), K("ta")], writes=[K("ta")])
        s.op("dve", lambda e: e.tensor_tensor(out=er, in0=er, in1=ta, op=ALU.add), reads=[K("er"), K("ta")], writes=[K("er")])
        s.op("dve", lambda e: e.tensor_tensor(out=er, in0=er, in1=den, op=ALU.mult), reads=[K("er"), K("den")], writes=[K("er")])
        s.op("dve", lambda e: e.tensor_tensor(out=ei, in0=tt, in1=lr, op=ALU.mult), reads=[K("zi"), K("in")], writes=[K("ei")])
        s.op("dve", lambda e: e.tensor_tensor(out=tb, in0=zr1, in1=li, op=ALU.mult), reads=[K("zr1"), K("in"), K("tb")], writes=[K("tb")])
        s.op("dve", lambda e: e.tensor_tensor(out=ei, in0=ei, in1=tb, op=ALU.subtract), reads=[K("ei"), K("tb")], writes=[K("ei")])
        s.op("dve", lambda e: e.tensor_tensor(out=ei, in0=ei, in1=den, op=ALU.mult), reads=[K("ei"), K("den")], writes=[K("ei")])
        return dict(mag=mag, ang=ang, er=er, ei=ei)

    sptmp = A([128, 12, 8], F32, "sptmp")
    s.op("dve", lambda e: e.tensor_copy(out=sp_[:, 0:24], in_=pp[:, 28:52]), reads=["pp"], writes=["S_in"])
    P = ssm_params(sp_[:, 0:8], sp_[:, 8:16], sp_[:, 16:24], 8, "S_", sptmp)
    rho = sp_[:, 48:56]
    s.op("dve", lambda e: e.tensor_copy(out=rho, in_=P["mag"]), reads=["S_mag"], writes=["rho"])
    th = sp_[:, 24:32]
    s.op("dve", lambda e: e.tensor_scalar(out=sp_[:, 32:40], in0=P["ang"], scalar1=1.0 / TWO_PI, scalar2=MAGIC, op0=ALU.mult, op1=ALU.add),
         reads=["S_ang"], writes=["S_k"])
    s.op("dve", lambda e: e.tensor_scalar_add(out=sp_[:, 32:40], in0=sp_[:, 32:40], scalar1=-MAGIC), reads=["S_k"], writes=["S_k"])
    s.op("dve", lambda e: e.scalar_tensor_tensor(out=th, in0=sp_[:, 32:40], scalar=-C1, in1=P["ang"], op0=ALU.mult, op1=ALU.add),
         reads=["S_k", "S_ang"], writes=["S_th"])
    s.op("dve", lambda e: e.scalar_tensor_tensor(out=th, in0=sp_[:, 32:40], scalar=-C2, in1=th, op0=ALU.mult, op1=ALU.add),
         reads=["S_k", "S_th"], writes=["S_th"])
    angt = A([128, 257], F32, "angt")
    tta = A([128, 257], F32, "tta")
    ttb = A([128, 257], F32, "ttb")
    for p in range(8):
        s.op("dve", lambda e, p=p: e.tensor_scalar_mul(out=angt[:], in0=iota[:], scalar1=th[:, p:p + 1]),
             reads=["iota", "S_th"], writes=["angt"])
        emit_sincos(g, angt[:], ST[:, p, :], CT[:, p, :], tta[:], ttb[:], ("angt", "ST%d" % p, "CT%d" % p, "tta", "ttb"))
    nST = sp_[:, 40:48]
    s.op("dve", lambda e: e.tensor_scalar_mul(out=nST, in0=ST[:, :, 256], scalar1=-1.0), reads=["ST%d" % p for p in range(8)], writes=["nST"])
    srow = A([128, 2, 5, 64], F32, "srow")
    s.dma(lambda e: e.dma_start(out=srow[:], in_=g.srow[l].rearrange("(c p) a b -> p c a b", p=128)), writes=["R_in"])
    rtmp = A([128, 12, 128], F32, "rtmp")
    rin = A([128, 3, 128], F32, "rin")
    for a_ in range(3):
        s.op("dve", lambda e, a_=a_: e.tensor_copy(out=rin[:, a_, :].rearrange("p (c b) -> p c b", c=2), in_=srow[:, :, a_, :]),
             reads=["R_in"], writes=["R_in2"])
    s.op("dve", lambda e: e.tensor_copy(out=rin[:, 0, 0:1], in_=rin[:, 0, 0:1]), reads=["R_in2"], writes=["R_in"])
    R = ssm_params(rin[:, 0, :], rin[:, 1, :], rin[:, 2, :], 128, "R_", rtmp)
    bbr = A([128, 2, 64], F32, "bbr")
    bbi = A([128, 2, 64], F32, "bbi")
    bt1 = A([128, 2, 64], F32, "bt1")
    er2 = R["er"].rearrange("p (c b) -> p c b", c=2)
    ei2 = R["ei"].rearrange("p (c b) -> p c b", c=2)
    s.op("dve", lambda e: e.tensor_tensor(out=bbr[:], in0=er2, in1=srow[:, :, 3, :], op=ALU.mult), reads=["R_er", "R_in"], writes=["bbr"])
    s.op("dve", lambda e: e.tensor_tensor(out=bt1[:], in0=ei2, in1=srow[:, :, 4, :], op=ALU.mult), reads=["R_ei", "R_in"], writes=["bt1"])
    s.op("dve", lambda e: e.tensor_tensor(out=bbr[:], in0=bbr[:], in1=bt1[:], op=ALU.subtract), reads=["bbr", "bt1"], writes=["bbr"])
    s.op("dve", lambda e: e.tensor_tensor(out=bbi[:], in0=er2, in1=srow[:, :, 4, :], op=ALU.mult), reads=["R_er", "R_in"], writes=["bbi"])
    s.op("dve", lambda e: e.tensor_tensor(out=bt1[:], in0=ei2, in1=srow[:, :, 3, :], op=ALU.mult), reads=["R_ei", "R_in", "bbr"], writes=["bt1"])
    s.op("dve", lambda e: e.tensor_tensor(out=bbi[:], in0=bbi[:], in1=bt1[:], op=ALU.add), reads=["bbi", "bt1"], writes=["bbi"])
    for p in range(8):
        ct, q = p // 4, p % 4
        for gl in range(2):
            mcol = 2 * q + gl
            s.op("dve", lambda e, p=p, ct=ct, gl=gl, mcol=mcol: e.tensor_scalar_mul(
                out=TBre[:, p, 64 * gl:64 * gl + 64], in0=bbr[:, ct, :], scalar1=gmask[:, mcol:mcol + 1]),
                reads=["bbr", "gmask"], writes=["TBre"])
            s.op("dve", lambda e, p=p, ct=ct, gl=gl, mcol=mcol: e.tensor_scalar_mul(
                out=TBim[:, p, 64 * gl:64 * gl + 64], in0=bbi[:, ct, :], scalar1=gmask[:, mcol:mcol + 1]),
                reads=["bbi", "gmask"], writes=["TBim"])
    cstt = A([128, 8, 2, 16], F32, "cstt")
    s.dma(lambda e: e.dma_start(out=cstt[:], in_=g.cst[l]), writes=["cstt"])
    s.op("pool", lambda e: e.memset(TCre[:], 0.0), writes=["TCre"])
    s.op("pool", lambda e: e.memset(TCim[:], 0.0), writes=["TCim"])
    for p in range(8):
        q = p % 4
        for gl in range(2):
            r0 = 64 * gl
            c0 = 32 * q + 16 * gl
            s.op("dve", lambda e, p=p, r0=r0, c0=c0: e.tensor_copy(out=TCre[r0:r0 + 64, p, c0:c0 + 16], in_=cstt[r0:r0 + 64, p, 0, :]),
                 reads=["cstt", "TCre"], writes=["TCre"])
            s.op("dve", lambda e, p=p, r0=r0, c0=c0: e.tensor_scalar_mul(out=TCim[r0:r0 + 64, p, c0:c0 + 16], in0=cstt[r0:r0 + 64, p, 1, :], scalar1=-1.0),
                 reads=["cstt", "TCim"], writes=["TCim"])
    s.barrier()
    sb.reset(mk)

    xin = A([128, 8, T], F32, "xin")
    h = A([128, 8, T], BF16, "h")
    ymix = A([128, 8, T], BF16, "ymix")
    posi = A([128, T], I32, "posi")
    posf = A([128, T], F32, "posf")
    rsn = A([128, T], F32, "rsn")
    rcs = A([128, T], F32, "rcs")
    rta = A([128, T], F32, "rta")
    rtb = A([128, T], F32, "rtb")
    Lc = A([128, 3, T], F32, "Lc")
    Lcb = A([128, 3, T], BF16, "Lcb")
    Lr = A([128, 3, T], F32, "Lr")
    Li = A([128, 3, T], F32, "Li")
    La = A([128, 3, T], F32, "La")
    La2 = A([128, 3, T], F32, "La2")
    Lg = A([128, 3, T], F32, "Lg")
    qr = A([128, 3, T], BF16, "qr")
    kr = A([128, 3, T], BF16, "kr")
    qd = A([128, 3, T], BF16, "qd")
    vtm = A([128, NCH, 384], BF16, "vtm")
    gsl = A([128, NCH, 384], F32, "gsl")
    ktm = A([128, 384], BF16, "ktm")
    sm = A([128, 6, 128], BF16, "sm")
    osb = A([128, 384], F32, "osb")
    osq = A([128, 384], F32, "osq")
    gst = A([128, 4, 6], F32, "gst")
    yret = A([128, 384], BF16, "yret")
    uf = A([128, 2, T], F32, "uf")
    ubf = A([128, 2, T], BF16, "ubf")
    t1 = A([128, T], F32, "t1")
    t2 = A([128, T], F32, "t2")
    Vr = A([128, T], F32, "Vr")
    Vi = A([128, T], F32, "Vi")
    Wr = A([128, T], F32, "Wr")
    Wi = A([128, T], F32, "Wi")
    Xr = A([128, 8, T], BF16, "Xr")
    Xi = A([128, 8, T], BF16, "Xi")
    ygf = A([128, 2, T], F32, "ygf")
    ygb = A([128, 2, T], BF16, "ygb")
    sgl = A([128, T], F32, "sgl")
    ctmp = A([128, 4], F32, "ctmp")
    tmp = dict(mean=A([128, T], F32, "mean"), m2=A([128, T], F32, "m2"), var=A([128, T], F32, "var"),
               ybf=A([128, 8, T], BF16, "ybf"), ysq=A([128, 8, T], BF16, "ysq"))
    shc = l * 72 + 3 * 8
    scc = l * 72 + 4 * 8
    pbn = [0]

    def pbank():
        b = pbn[0] % 5
        pbn[0] += 1
        return g.ps[b], "ps%d" % b

    def proj(wt, c0, ncol=128):
        ps, key = pbank()
        for k in range(8):
            s.op("pe", lambda e, k=k: e.matmul(ps[:, 0:T], lhsT=wt[:, k, c0:c0 + 128], rhs=h[:, k, :], start=(k == 0), stop=(k == 7)),
                 reads=["h%d" % k, "wts"], writes=[key])
        return ps[:, 0:T], key

    def run_tile(it, full):
        t0 = it * T
        for j in range(8):
            s.dma(lambda e, t0=t0, j=j: e.dma_start(out=xin[:, j, :], in_=src[j * 128:(j + 1) * 128, t0:t0 + T]), writes=["xm%d" % j])
        s.dma(lambda e, t0=t0: e.dma_start(out=posi[:], in_=g.pos[0:1, t0:t0 + T].partition_broadcast(128)), writes=["posi"])
        for k in range(8):
            s.op("dve", lambda e, k=k: e.tensor_scalar(
                out=h[:, k, :], in0=xin[:, k, :], scalar1=g.sc1p[:, scc + k:scc + k + 1], scalar2=g.sc1p[:, shc + k:shc + k + 1],
                op0=ALU.mult, op1=ALU.add), reads=["xm%d" % k, "sc1p"], writes=["h%d" % k])
        s.op("dve", lambda e: e.tensor_copy(out=posf[:], in_=posi[:]), reads=["posi"], writes=["posf"])
        s.op("dve", lambda e: e.tensor_scalar_mul(out=posf[:], in0=posf[:], scalar1=cs[:, 0:1]), reads=["posf", "cs"], writes=["posf"])
        emit_sincos(g, posf[:], rsn[:], rcs[:], rta[:], rtb[:], ("posf", "rsn", "rcs", "rta", "rtb"), sin_scale=cs[:, 1:2])

        for i in range(3):
            ups, ukey = proj(winb, i * 128)
            s.op("act", lambda e, i=i, ups=ups: e.copy(out=uext[:, i, 3:3 + T], in_=ups), reads=[ukey], writes=["uext%d" % i])
            s.op("dve", lambda e, i=i: e.tensor_scalar(out=Lc[:, i, :], in0=uext[:, i, 3:3 + T], scalar1=pp[:, i * 4 + 3:i * 4 + 4],
                                                       scalar2=pp[:, 12 + i:13 + i], op0=ALU.mult, op1=ALU.add),
                 reads=["uext%d" % i, "pp"], writes=["Lc%d" % i])
            for kk in range(3):
                s.op("dve", lambda e, i=i, kk=kk: e.scalar_tensor_tensor(
                    out=Lc[:, i, :], in0=uext[:, i, kk:kk + T], scalar=pp[:, i * 4 + kk:i * 4 + kk + 1], in1=Lc[:, i, :],
                    op0=ALU.mult, op1=ALU.add), reads=["uext%d" % i, "pp", "Lc%d" % i], writes=["Lc%d" % i])
            s.op("pool", lambda e, i=i: e.tensor_copy(out=uext[:, i, 0:3], in_=uext[:, i, T:T + 3]), reads=["uext%d" % i], writes=["uext%d" % i])
            s.op("pool", lambda e, i=i: e.tensor_copy(out=Lcb[:, i, :], in_=Lc[:, i, :]), reads=["Lc%d" % i], writes=["Lcb%d" % i])
        for i in range(3):
            pa, ka = pbank()
            s.op("pe", lambda e, i=i, pa=pa: e.matmul(pa[:, 0:T], lhsT=wabd[:, i, :], rhs=Lcb[:, i, :], start=True, stop=True),
                 reads=["Lcb%d" % i, "wts"], writes=[ka])
            s.op("act", lambda e, i=i, pa=pa: e.activation(out=Lr[:, i, :], in_=pa[:, 0:T], func=AF.Sigmoid, bias=pp[:, 15 + i:16 + i]),
                 reads=[ka, "pp"], writes=["Lr%d" % i])
            px, kx = pbank()
            s.op("pe", lambda e, i=i, px=px: e.matmul(px[:, 0:T], lhsT=wxbd[:, i, :], rhs=Lcb[:, i, :], start=True, stop=True),
                 reads=["Lcb%d" % i, "wts"], writes=[kx])
            s.op("act", lambda e, i=i, px=px: e.activation(out=Li[:, i, :], in_=px[:, 0:T], func=AF.Sigmoid, bias=pp[:, 18 + i:19 + i]),
                 reads=[kx, "pp"], writes=["Li%d" % i])
        for i in range(3):
            s.op("act", lambda e, i=i: e.activation(out=La[:, i, :], in_=Lr[:, i, :], func=AF.Exp, scale=cneg[:, i:i + 1]),
                 reads=["Lr%d" % i, "cneg"], writes=["La%d" % i])
            s.op("act", lambda e, i=i: e.activation(out=La2[:, i, :], in_=Lr[:, i, :], func=AF.Exp, scale=cneg[:, 3 + i:4 + i]),
                 reads=["Lr%d" % i, "cneg2"], writes=["La2%d" % i])
        for i in range(3):
            s.op("dve", lambda e, i=i: e.tensor_scalar(out=La2[:, i, :], in0=La2[:, i, :], scalar1=-1.0, scalar2=1.0, op0=ALU.mult, op1=ALU.add),
                 reads=["La2%d" % i], writes=["La2%d" % i])
            s.op("dve", lambda e, i=i: e.tensor_scalar_max(out=La2[:, i, :], in0=La2[:, i, :], scalar1=0.0), reads=["La2%d" % i], writes=["La2%d" % i])
            s.op("act", lambda e, i=i: e.activation(out=La2[:, i, :], in_=La2[:, i, :], func=AF.Sqrt), reads=["La2%d" % i], writes=["La2%d" % i])
        for i in range(3):
            s.op("dve", lambda e, i=i: e.tensor_tensor(out=Li[:, i, :], in0=Li[:, i, :], in1=Lc[:, i, :], op=ALU.mult),
                 reads=["Li%d" % i, "Lc%d" % i], writes=["Li%d" % i])
            s.op("dve", lambda e, i=i: e.tensor_tensor(out=Li[:, i, :], in0=Li[:, i, :], in1=La2[:, i, :], op=ALU.mult),
                 reads=["Li%d" % i, "La2%d" % i], writes=["Li%d" % i])
            s.op("dve", lambda e, i=i: e.tensor_tensor_scan(out=Lr[:, i, :], data0=La[:, i, :], data1=Li[:, i, :], initial=lstate[:, i:i + 1],
                                                            op0=ALU.mult, op1=ALU.add),
                 reads=["La%d" % i, "Li%d" % i, "lstate", "Lr%d" % i], writes=["Lr%d" % i])
            s.op("dve", lambda e, i=i: e.tensor_copy(out=lstate[:, i:i + 1], in_=Lr[:, i, T - 1:T]), reads=["Lr%d" % i, "lstate"], writes=["lstate"])
        for i in range(3 if full else 0):
            gps, gkey = proj(winb, 384 + i * 128)
            s.op("act", lambda e, i=i, gps=gps: e.activation(out=Lg[:, i, :], in_=gps, func=AF.Gelu_apprx_tanh), reads=[gkey], writes=["Lg%d" % i])
            s.op("dve", lambda e, i=i: e.tensor_tensor(out=ymix[:, i, :], in0=Lr[:, i, :], in1=Lg[:, i, :], op=ALU.mult),
                 reads=["Lr%d" % i, "Lg%d" % i], writes=["ym%d" % i])

        for (dstt, c0, nm) in (((qr, 768, "qr"), (kr, 1152, "kr")) if full else ((kr, 1152, "kr"),)):
            for i in range(3):
                p1, k1 = proj(winb, c0 + i * 128)
                s.op("dve", lambda e, p1=p1: e.tensor_tensor(out=t1[:], in0=p1, in1=rcs[:], op=ALU.mult), reads=[k1, "rcs"], writes=["t1"])
                p2, k2 = proj(wswb, (0 if nm == "qr" else 384) + i * 128)
                s.op("dve", lambda e, p2=p2: e.tensor_tensor(out=t2[:], in0=p2, in1=rsn[:], op=ALU.mult), reads=[k2, "rsn"], writes=["t2"])
                s.op("pool", lambda e, i=i, dstt=dstt: e.tensor_tensor(out=dstt[:, i, :], in0=t1[:], in1=t2[:], op=ALU.add),
                     reads=["t1", "t2"], writes=["%s%d" % (nm, i)])
        for i in range(3 if full else 0):
            for c in range(NCH):
                s.op("pool", lambda e, i=i, c=c: e.tensor_tensor(out=qd[:, i, c * 128:(c + 1) * 128], in0=qr[:, i, c * 128:(c + 1) * 128],
                                                                 in1=qdec[:, i, :], op=ALU.mult),
                     reads=["qr%d" % i, "qdec"], writes=["qd%d" % i])
        for c in range(NCH):
            for (c0, nm) in (((1536, "v"), (1920, "g")) if full else ((1536, "v"),)):
                ps, key = pbank()
                for k in range(8):
                    s.op("pe", lambda e, k=k, c=c, c0=c0, ps=ps: e.matmul(ps[:, 0:384], lhsT=h[:, k, c * 128:(c + 1) * 128],
                                                                         rhs=winb[:, k, c0:c0 + 384], start=(k == 0), stop=(k == 7)),
                         reads=["h%d" % k, "wts"], writes=[key])
                if nm == "v":
                    s.op("act", lambda e, c=c, ps=ps: e.copy(out=vtm[:, c, :], in_=ps[:, 0:384]), reads=[key], writes=["vtm%d" % c])
                else:
                    s.op("act", lambda e, c=c, ps=ps: e.activation(out=gsl[:, c, :], in_=ps[:, 0:384], func=AF.Silu), reads=[key], writes=["gsl%d" % c])
        for c in range(NCH):
            cs_ = slice(c * 128, (c + 1) * 128)
            pk, kk_ = pbank()
            for i in range(3):
                s.op("pe", lambda e, i=i, pk=pk, cs_=cs_: e.matmul(pk[:, i * 128:(i + 1) * 128], lhsT=kr[:, i, cs_], rhs=identb[:], start=True, stop=True),
                     reads=["kr%d" % i, "identb"], writes=[kk_])
            s.op("dve", lambda e, pk=pk: e.tensor_tensor(out=ktm[:], in0=pk[:, 0:384], in1=kdt[:], op=ALU.mult), reads=[kk_, "kdt"], writes=["ktm"])
            po_, ko_ = g.ps[5], "ps5"
            for hd in range(6 if full else 0):
                i, po = hd // 2, 64 * (hd % 2)
                psc, ksc = pbank()
                s.op("pe", lambda e, i=i, po=po, psc=psc, cs_=cs_: e.matmul(psc[:, 0:128], lhsT=kr[po:po + 64, i, cs_], rhs=qr[po:po + 64, i, cs_],
                                                                           start=True, stop=True),
                     reads=["kr%d" % i, "qr%d" % i], writes=[ksc])
                s.op("dve", lambda e, hd=hd, psc=psc: e.tensor_tensor(out=sm[:, hd, :], in0=psc[:, 0:128], in1=maskT[:, hd, :], op=ALU.mult),
                     reads=[ksc, "maskT"], writes=["sm%d" % hd])
                s.op("pe", lambda e, hd=hd, c=c, po_=po_: e.matmul(po_[:, hd * 64:(hd + 1) * 64], lhsT=sm[:, hd, :], rhs=vtm[:, c, hd * 64:(hd + 1) * 64],
                                                                  start=True, stop=False),
                     reads=["sm%d" % hd, "vtm%d" % c], writes=[ko_])
                s.op("pe", lambda e, hd=hd, i=i, po=po, po_=po_, cs_=cs_: e.matmul(po_[:, hd * 64:(hd + 1) * 64], lhsT=qd[po:po + 64, i, cs_],
                                                                                  rhs=stbf[po:po + 64, i, :], start=False, stop=True),
                     reads=["qd%d" % i, "stbf"], writes=[ko_])
            for i in range(3):
                pkv, kkv = pbank()
                s.op("pe", lambda e, i=i, c=c, pkv=pkv: e.matmul(pkv[:, 0:128], lhsT=ktm[:, i * 128:(i + 1) * 128], rhs=vtm[:, c, i * 128:(i + 1) * 128],
                                                                start=True, stop=True),
                     reads=["ktm", "vtm%d" % c], writes=[kkv])
                for hh in range(2):
                    hd = 2 * i + hh
                    po = 64 * hh
                    cdv = float(np.exp(128.0 * np.log1p(-np.exp2(-5.0 - hd))))
                    s.op("dve", lambda e, i=i, po=po, pkv=pkv, cdv=cdv: e.scalar_tensor_tensor(
                        out=stt[po:po + 64, i, :], in0=stt[po:po + 64, i, :], scalar=cdv, in1=pkv[po:po + 64, po:po + 64],
                        op0=ALU.mult, op1=ALU.add), reads=[kkv, "stt", "stbf"], writes=["stt"])
            s.op("pool", lambda e: e.tensor_copy(out=stbf[:], in_=stt[:]), reads=["stt"], writes=["stbf"])
            if not full:
                continue
            s.op("act", lambda e, po_=po_: e.copy(out=osb[:], in_=po_[:, 0:384]), reads=[ko_], writes=["osb"])
            s.op("act", lambda e, po_=po_: e.activation(out=osq[:], in_=po_[:, 0:384], func=AF.Square), reads=[ko_], writes=["osq"])
            s.op("dve", lambda e: e.tensor_reduce(out=gst[:, 0, :], in_=osb[:].rearrange("p (a b) -> p a b", a=6), axis=AX.X, op=ALU.add),
                 reads=["osb"], writes=["gst0"])
            s.op("dve", lambda e: e.tensor_reduce(out=gst[:, 1, :], in_=osq[:].rearrange("p (a b) -> p a b", a=6), axis=AX.X, op=ALU.add),
                 reads=["osq"], writes=["gst1"])
            s.op("dve", lambda e: e.tensor_scalar_mul(out=gst[:, 0, :], in0=gst[:, 0, :], scalar1=1.0 / 64), reads=["gst0"], writes=["gst0"])
            s.op("dve", lambda e: e.tensor_tensor(out=gst[:, 2, :], in0=gst[:, 0, :], in1=gst[:, 0, :], op=ALU.mult), reads=["gst0"], writes=["gst2"])
            s.op("dve", lambda e: e.scalar_tensor_tensor(out=gst[:, 1, :], in0=gst[:, 1, :], scalar=1.0 / 64, in1=gst[:, 2, :], op0=ALU.mult, op1=ALU.subtract),
                 reads=["gst1", "gst2"], writes=["gst1"])
            s.op("dve", lambda e: e.tensor_scalar(out=gst[:, 1, :], in0=gst[:, 1, :], scalar1=0.0, scalar2=1e-5, op0=ALU.max, op1=ALU.add),
                 reads=["gst1"], writes=["gst1"])
            s.op("act", lambda e: e.activation(out=gst[:, 1, :], in_=gst[:, 1, :], func=AF.Sqrt), reads=["gst1"], writes=["gst1"])
            s.op("dve", lambda e: e.reciprocal(out=gst[:, 1, :], in_=gst[:, 1, :]), reads=["gst1"], writes=["gst1"])
            o3 = osb[:].rearrange("p (a b) -> p a b", a=6)
            s.op("dve", lambda e: e.tensor_tensor(out=o3, in0=o3, in1=bc_inner(gst[:, 0, :], 64), op=ALU.subtract), reads=["osb", "gst0"], writes=["osb"])
            s.op("dve", lambda e: e.tensor_tensor(out=o3, in0=o3, in1=bc_inner(gst[:, 1, :], 64), op=ALU.mult), reads=["osb", "gst1"], writes=["osb"])
            s.op("pool", lambda e: e.tensor_tensor(out=osb[:], in0=osb[:], in1=gng[:], op=ALU.mult), reads=["osb", "gng"], writes=["osb"])
            s.op("pool", lambda e: e.tensor_tensor(out=osb[:], in0=osb[:], in1=gnb[:], op=ALU.add), reads=["osb", "gnb"], writes=["osb"])
            s.op("dve", lambda e, c=c: e.tensor_tensor(out=yret[:], in0=osb[:], in1=gsl[:, c, :], op=ALU.mult), reads=["osb", "gsl%d" % c], writes=["yret"])
            for i in range(3):
                pt, kt = pbank()
                s.op("pe", lambda e, i=i, pt=pt: e.matmul(pt[:, 0:128], lhsT=yret[:, i * 128:(i + 1) * 128], rhs=identb[:], start=True, stop=True),
                     reads=["yret", "identb"], writes=[kt])
                s.op("act", lambda e, i=i, pt=pt, cs_=cs_: e.copy(out=ymix[:, 3 + i, cs_], in_=pt[:, 0:128]), reads=[kt], writes=["ym%d" % (3 + i)])

        for ct in range(2):
            ups, ukey = proj(winb, 2304 + ct * 128)
            s.op("act", lambda e, ct=ct, ups=ups: e.copy(out=uf[:, ct, :], in_=ups), reads=[ukey], writes=["uf%d" % ct])
            s.op("pool", lambda e, ct=ct: e.tensor_copy(out=ubf[:, ct, :], in_=uf[:, ct, :]), reads=["uf%d" % ct], writes=["ubf%d" % ct])
        for p in range(8):
            ct = p // 4
            pr, kr_ = pbank()
            s.op("pe", lambda e, p=p, ct=ct, pr=pr: e.matmul(pr[:, 0:T], lhsT=TBre[:, p, :], rhs=ubf[:, ct, :], start=True, stop=True),
                 reads=["ubf%d" % ct, "wts"], writes=[kr_])
            pi_, ki_ = pbank()
            s.op("pe", lambda e, p=p, ct=ct, pi_=pi_: e.matmul(pi_[:, 0:T], lhsT=TBim[:, p, :], rhs=ubf[:, ct, :], start=True, stop=True),
                 reads=["ubf%d" % ct, "wts"], writes=[ki_])
            Cp, Sp = CT[:, p, 0:T], ST[:, p, 0:T]
            s.op("dve", lambda e, pr=pr, Cp=Cp: e.tensor_tensor(out=t1[:], in0=pr[:, 0:T], in1=Cp, op=ALU.mult), reads=[kr_], writes=["t1"])
            s.op("dve", lambda e, pi_=pi_, Sp=Sp: e.tensor_tensor(out=t2[:], in0=pi_[:, 0:T], in1=Sp, op=ALU.mult), reads=[ki_], writes=["t2"])
            s.op("pool", lambda e: e.tensor_tensor(out=Vr[:], in0=t1[:], in1=t2[:], op=ALU.add), reads=["t1", "t2"], writes=["Vr"])
            s.op("dve", lambda e, pi_=pi_, Cp=Cp: e.tensor_tensor(out=t1[:], in0=pi_[:, 0:T], in1=Cp, op=ALU.mult), reads=[ki_, "t1"], writes=["t1"])
            s.op("dve", lambda e, pr=pr, Sp=Sp: e.tensor_tensor(out=t2[:], in0=pr[:, 0:T], in1=Sp, op=ALU.mult), reads=[kr_, "t2"], writes=["t2"])
            s.op("pool", lambda e: e.tensor_tensor(out=Vi[:], in0=t1[:], in1=t2[:], op=ALU.subtract), reads=["t1", "t2"], writes=["Vi"])
            rb = rho[:, p:p + 1].to_broadcast([128, T])
            s.op("dve", lambda e, p=p, rb=rb: e.tensor_tensor_scan(out=Wr[:], data0=rb, data1=Vr[:], initial=carr[:, p:p + 1], op0=ALU.mult, op1=ALU.add),
                 reads=["Vr", "carr", "Wr"], writes=["Wr"])
            s.op("dve", lambda e, p=p, rb=rb: e.tensor_tensor_scan(out=Wi[:], data0=rb, data1=Vi[:], initial=carr[:, 8 + p:9 + p], op0=ALU.mult, op1=ALU.add),
                 reads=["Vi", "carr", "Wi"], writes=["Wi"])
            s.op("pool", lambda e, p=p: e.tensor_scalar_mul(out=ctmp[:, 0:1], in0=Wi[:, T - 1:T], scalar1=nST[:, p:p + 1]), reads=["Wi", "nST"], writes=["ctmp0"])
            s.op("dve", lambda e, p=p: e.scalar_tensor_tensor(out=carr[:, p:p + 1], in0=Wr[:, T - 1:T], scalar=CT[:, p, T:T + 1], in1=ctmp[:, 0:1],
                                                             op0=ALU.mult, op1=ALU.add), reads=["Wr", "ctmp0", "carr"], writes=["carr"])
            s.op("pool", lambda e, p=p: e.tensor_scalar_mul(out=ctmp[:, 1:2], in0=Wr[:, T - 1:T], scalar1=ST[:, p, T:T + 1]), reads=["Wr"], writes=["ctmp1"])
            s.op("dve", lambda e, p=p: e.scalar_tensor_tensor(out=carr[:, 8 + p:9 + p], in0=Wi[:, T - 1:T], scalar=CT[:, p, T:T + 1], in1=ctmp[:, 1:2],
                                                             op0=ALU.mult, op1=ALU.add), reads=["Wi", "ctmp1", "carr"], writes=["carr"])
            if not full:
                continue
            s.op("dve", lambda e, Cp=Cp: e.tensor_tensor(out=t1[:], in0=Wr[:], in1=Cp, op=ALU.mult), reads=["Wr", "t1"], writes=["t1"])
            s.op("pool", lambda e, Sp=Sp: e.tensor_tensor(out=t2[:], in0=Wi[:], in1=Sp, op=ALU.mult), reads=["Wi", "t2"], writes=["t2"])
            s.op("pool", lambda e, p=p: e.tensor_tensor(out=Xr[:, p, :], in0=t1[:], in1=t2[:], op=ALU.subtract), reads=["t1", "t2"], writes=["Xr%d" % p])
            s.op("dve", lambda e, Cp=Cp: e.tensor_tensor(out=t1[:], in0=Wi[:], in1=Cp, op=ALU.mult), reads=["Wi", "t1"], writes=["t1"])
            s.op("pool", lambda e, Sp=Sp: e.tensor_tensor(out=t2[:], in0=Wr[:], in1=Sp, op=ALU.mult), reads=["Wr", "t2"], writes=["t2"])
            s.op("pool", lambda e, p=p: e.tensor_tensor(out=Xi[:, p, :], in0=t1[:], in1=t2[:], op=ALU.add), reads=["t1", "t2"], writes=["Xi%d" % p])
        if not full:
            return
        for ct in range(2):
            py, ky = pbank()
            for q in range(4):
                p = ct * 4 + q
                s.op("pe", lambda e, p=p, q=q, py=py: e.matmul(py[:, 0:T], lhsT=TCre[:, p, :], rhs=Xr[:, p, :], start=(q == 0), stop=False),
                     reads=["Xr%d" % p, "wts"], writes=[ky])
                s.op("pe", lambda e, p=p, q=q, py=py: e.matmul(py[:, 0:T], lhsT=TCim[:, p, :], rhs=Xi[:, p, :], start=False, stop=(q == 3)),
                     reads=["Xi%d" % p, "wts"], writes=[ky])
            s.op("dve", lambda e, ct=ct, py=py: e.scalar_tensor_tensor(out=ygf[:, ct, :], in0=uf[:, ct, :], scalar=pp[:, 24 + ct:25 + ct], in1=py[:, 0:T],
                                                                      op0=ALU.mult, op1=ALU.add), reads=[ky, "uf%d" % ct, "pp"], writes=["ygf%d" % ct])
            s.op("act", lambda e, ct=ct: e.activation(out=ygf[:, ct, :], in_=ygf[:, ct, :], func=AF.Gelu_apprx_tanh), reads=["ygf%d" % ct], writes=["ygf%d" % ct])
            s.op("pool", lambda e, ct=ct: e.tensor_copy(out=ygb[:, ct, :], in_=ygf[:, ct, :]), reads=["ygf%d" % ct], writes=["ygb%d" % ct])
        for co in range(2):
            pg, kg = pbank()
            for ck in range(2):
                s.op("pe", lambda e, co=co, ck=ck, pg=pg: e.matmul(pg[:, 0:T], lhsT=wglub[:, ck, co * 128:(co + 1) * 128], rhs=ygb[:, ck, :],
                                                                  start=(ck == 0), stop=(ck == 1)), reads=["ygb%d" % ck, "wts"], writes=[kg])
            s.op("act", lambda e, co=co, pg=pg: e.activation(out=sgl[:], in_=pg[:, 0:T], func=AF.Sigmoid, bias=pp[:, 26 + co:27 + co]),
                 reads=[kg, "pp"], writes=["sgl"])
            s.op("dve", lambda e, co=co: e.tensor_tensor(out=ymix[:, 6 + co, :], in0=ygf[:, co, :], in1=sgl[:], op=ALU.mult),
                 reads=["ygf%d" % co, "sgl"], writes=["ym%d" % (6 + co)])

        def acc_fn(j):
            pa, key = pbank()
            for k in range(8):
                s.op("pe", lambda e, k=k, j=j, pa=pa: e.matmul(pa[:, 0:T], lhsT=woutb[:, k, j * 128:(j + 1) * 128], rhs=ymix[:, k, :],
                                                              start=(k == 0), stop=(k == 7)), reads=["ym%d" % k, "wts"], writes=[key])
            return pa[:, 0:T], key

        emit_resid_ln(g, l, 1, xin, "xm", acc_fn, T, tmp)
        for j in range(8):
            i_ = s.dma(lambda e, t0=t0, j=j: e.dma_start(out=dst[j * 128:(j + 1) * 128, t0:t0 + T], in_=xin[:, j, :]), reads=["xm%d" % j])
            s.ops[i_]["is_out"] = dst is g.outT


    import os
    MODE = os.environ.get("KMODE", "full")
    if MODE == "B":
        for it in range(ntile):
            run_tile(it, True)
        return
    for it in range(ntile):
        run_tile(it, False)
    pack_state()
    if MODE == "AB":
        unpack_state_zero()
        for it in range(ntile):
            run_tile(it, True)
        return
    contrib = xin[:].rearrange("p a b -> p (a b)")[:, 0:8 * NST].rearrange("p (a b) -> p a b", a=8)
    CK = ["xm%d" % j for j in range(8)]
    for r in range(8):
        s.op("dve", lambda e, r=r: e.tensor_scalar_mul(out=contrib[:, r, :], in0=stpack[:], scalar1=oneh[:, r:r + 1]),
             reads=["stpack", "oneh"], writes=["contrib"] + CK)
    s.dma(lambda e: e.dma_start(out=g.cc_in[l][:, :], in_=contrib.rearrange("p a b -> p (a b)")), reads=["contrib"], writes=["cc_in"], q="pool")
    s.dma(lambda e: e.collective_compute("AllReduce", ALU.add, replica_groups=[list(range(8))],
                                         ins=[g.cc_in[l].ap().opt()], outs=[g.cc_out[l].ap().opt()]),
          reads=["cc_in"], writes=["cc_out"], q="pool", inc=1)
    s.dma(lambda e: e.dma_start(out=contrib.rearrange("p a b -> p (a b)"), in_=g.cc_out[l][:, :]), reads=["cc_out"], writes=["contrib"] + CK, q="pool")
    s.op("dve", lambda e: e.tensor_scalar_mul(out=stpack[:], in0=contrib[:, 0, :], scalar1=selp[:, 0:1]), reads=["contrib", "selp"], writes=["stpack"])
    for r in range(1, 8):
        s.op("dve", lambda e, r=r: e.scalar_tensor_tensor(out=stpack[:], in0=contrib[:, r, :], scalar=selp[:, r:r + 1], in1=stpack[:],
                                                          op0=ALU.mult, op1=ALU.add), reads=["contrib", "selp", "stpack"], writes=["stpack"])
    s.op("dve", lambda e: e.tensor_copy(out=ctmp[:, 0:1], in_=contrib[:, 0, 0:1]), reads=["contrib"] + CK, writes=["ctmp0"])
    unpack_state()
    for it in range(ntile):
        run_tile(it, True)


_NC_CACHE = {}


def _prep_common(inp):
    cm = {}
    ada_b = np.asarray(inp["ada_b"], np.float32)
    cm["ada_b"] = np.ascontiguousarray(ada_b.reshape(DEPTH, 72, 128).transpose(2, 0, 1).reshape(128, DEPTH * 72))
    for nm in ("ln_g", "ln_b"):
        a = np.asarray(inp[nm], np.float32)
        cm[nm] = np.ascontiguousarray(a.reshape(DEPTH, 3, 8, 128).transpose(3, 0, 1, 2).reshape(128, DEPTH * 24))
    for nm in ("ada_w", "ffn1_w1", "ffn1_w3", "ffn1_w2", "ffn2_w1", "ffn2_w3", "ffn2_w2", "mix_w_in", "mix_w_out", "ssm_w_glu"):
        cm[nm] = np.ascontiguousarray(np.asarray(inp[nm], np.float32))
    f = lambda nm: np.asarray(inp[nm], np.float32)
    L = DEPTH
    w_in = cm["mix_w_in"]
    idx = []
    for base in (768, 1152):
        for hd in range(6):
            for j in range(64):
                idx.append(base + hd * 64 + (j + 32) % 64)
    cm["w_sw"] = np.ascontiguousarray(w_in[:, :, idx])
    pp = np.zeros((L, 128, 52), np.float32)
    pp[:, :, 0:12] = f("conv_w").reshape(L, 4, 3, 128).transpose(0, 3, 2, 1).reshape(L, 128, 12)
    for c0, nm in ((12, "conv_b"), (15, "lru_ba"), (18, "lru_bx"), (21, "lru_lam")):
        pp[:, :, c0:c0 + 3] = f(nm).reshape(L, 3, 128).transpose(0, 2, 1)
    for c0, nm in ((24, "ssm_d"), (26, "ssm_b_glu")):
        pp[:, :, c0:c0 + 2] = f(nm).reshape(L, 2, 128).transpose(0, 2, 1)
    for c0, nm in ((28, "ssm_lam_re"), (36, "ssm_lam_im")):
        pp[:, :, c0:c0 + 8] = f(nm).reshape(L, 8, 2, 64).transpose(0, 2, 3, 1).reshape(L, 128, 8)
    ls = np.repeat(f("ssm_log_step")[:, :, None], 64, axis=2)
    pp[:, :, 44:52] = ls.reshape(L, 8, 2, 64).transpose(0, 2, 3, 1).reshape(L, 128, 8)
    cm["pp"] = pp
    cm["lruw"] = np.ascontiguousarray(np.stack([f("lru_wa"), f("lru_wx")], axis=1))
    cm["gn"] = np.ascontiguousarray(np.stack([f("ret_gn_g"), f("ret_gn_b")], axis=1))
    srow = np.zeros((L, 16, 16, 5, 64), np.float32)
    srow[:, :, :, 0, :] = f("ssm_lam_re")[:, :, None, :]
    srow[:, :, :, 1, :] = f("ssm_lam_im")[:, :, None, :]
    srow[:, :, :, 2, :] = f("ssm_log_step")[:, :, None, None]
    srow[:, :, :, 3, :] = f("ssm_b_re").transpose(0, 1, 3, 2)
    srow[:, :, :, 4, :] = f("ssm_b_im").transpose(0, 1, 3, 2)
    cm["srow"] = srow.reshape(L, 256, 5, 64)
    cst = np.zeros((L, 2, 64, 8, 2, 16), np.float32)
    for ri, nm in ((0, "ssm_c_re"), (1, "ssm_c_im")):
        a = f(nm).reshape(L, 8, 2, 16, 64)
        cst[:, :, :, :, ri, :] = a.transpose(0, 2, 4, 1, 3)
    cm["cst"] = cst.reshape(L, 128, 8, 2, 16)
    p = np.arange(128)
    csm = np.zeros((128, 8), np.float32)
    csm[:, 0] = (10000.0 ** (-(p % 32).astype(np.float32) / 32.0)).astype(np.float32)
    csm[:, 1] = np.where((p % 64) < 32, -1.0, 1.0)
    cm["c_small"] = csm
    cm["c_iota"] = np.ascontiguousarray(np.broadcast_to(np.arange(257, dtype=np.float32)[None, :], (128, 257)))
    lg = np.log1p(-np.exp2(-5.0 - np.arange(6, dtype=np.float64)))
    kk = np.arange(128)[:, None]
    qq = np.arange(128)[None, :]
    mT = np.zeros((128, 6, 128), np.float32)
    for hd in range(6):
        mT[:, hd, :] = np.where(qq >= kk, np.exp(lg[hd] * np.maximum(qq - kk, 0)), 0.0) * 0.125
    cm["c_maskT"] = mT
    qd = np.zeros((128, 3, 128), np.float32)
    for i in range(3):
        for h2 in range(2):
            qd[h2 * 64:(h2 + 1) * 64, i, :] = np.exp(lg[2 * i + h2] * (np.arange(128) + 1.0))[None, :]
    cm["c_qdec"] = qd
    kd = np.zeros((128, 6, 64), np.float32)
    for hd in range(6):
        kd[:, hd, :] = (np.exp(lg[hd] * (127.0 - np.arange(128))) * 0.125)[:, None]
    cm["c_kdt"] = kd.reshape(128, 384)
    cm["c_gmask"] = (p[:, None] // 16 == np.arange(8)[None, :]).astype(np.float32)
    cm["c_ident"] = np.eye(128, dtype=np.float32)
    return cm


def kernel(**inp):
    x = np.asarray(inp["x"], np.float32)
    c = np.asarray(inp["c"], np.float32)
    pos = np.asarray(inp["positions"], np.int32)
    B, S, _ = x.shape
    if "full" not in _NC_CACHE:
        _NC_CACHE["full"] = build_program()
    nc = _NC_CACHE["full"]
    cm = _prep_common(inp)
    in_maps = []
    for core in range(8):
        b, hf = core // 2, core % 2
        m = dict(cm)
        m["xT"] = np.ascontiguousarray(x[b, hf * NTOK:(hf + 1) * NTOK, :].T)
        m["cvec"] = np.ascontiguousarray(c[b].reshape(8, 128).T)
        m["pos"] = np.ascontiguousarray(pos[b, hf * NTOK:(hf + 1) * NTOK][None, :])
        oh = np.zeros((128, 8), np.float32)
        oh[:, core] = 1.0
        sp = np.zeros((128, 8), np.float32)
        if hf == 1:
            sp[:, core - 1] = 1.0
        m["onehot"] = oh
        m["selprev"] = sp
        in_maps.append(m)
    res = run_bass_kernel_spmd(nc, in_maps, core_ids=list(range(8)))
    out = np.empty((B, S, D), np.float32)
    for core in range(8):
        b, hf = core // 2, core % 2
        out[b, hf * NTOK:(hf + 1) * NTOK, :] = res.results[core]["outT"].T
    return out
```
